# Optimizing a Trainium2 kernel written in Bass

```python
import jax, jax.numpy as jnp
from jax import lax
import numpy as np

D_MODEL = 2048
BATCH = 2
SEQ = 16384
DEPTH = 2

N_META = 16
BLOCK_Q = 128
D_MIX = D_MODEL
D_ATTN = D_MIX // 2
D_CONV = D_MIX - D_ATTN
ATTN_HEAD_DIM = 128
N_ATTN_HEADS = D_ATTN // ATTN_HEAD_DIM
CONV_GROUP_DIM = 128
N_CONV_GROUPS = D_CONV // CONV_GROUP_DIM
GROUP_DIM = 128
N_GROUPS = D_MIX // GROUP_DIM
CONV_WIDTH = 3
D_FF = -(-8 * D_MODEL // (3 * 256)) * 256
D_IN = 3 * D_ATTN + 3 * D_CONV + N_ATTN_HEADS
EPS = 1e-6
NEG = -1e30

kernel_name = "hymba_fox_shortconv_hybrid"


def rmsnorm(x, g):
    xf = x.astype(jnp.float32)
    y = xf * lax.rsqrt(jnp.mean(xf * xf, axis=-1, keepdims=True) + EPS)
    return (y * g.astype(jnp.float32)).astype(x.dtype)


def short_conv(u, w):
    L = u.shape[1]
    up = jnp.pad(u, ((0, 0), (CONV_WIDTH - 1, 0), (0, 0)))
    y = w[0] * up[:, 0:L]
    for k in range(1, CONV_WIDTH):
        y = y + w[k] * up[:, k:k + L]
    return y


def forgetting_attention(q, k, v, log_f):
    Bsz, L, H, Dh = q.shape
    n_pad = (-L) % BLOCK_Q
    Lp = L + n_pad
    nb = Lp // BLOCK_Q
    pad4 = ((0, 0), (n_pad, 0), (0, 0), (0, 0))
    qp = jnp.pad(q, pad4).transpose(0, 2, 1, 3)
    kp = jnp.pad(k, pad4).transpose(0, 2, 1, 3)
    vp = jnp.pad(v, pad4).transpose(0, 2, 1, 3)
    lf = jnp.pad(log_f.astype(jnp.float32), ((0, 0), (n_pad, 0), (0, 0)))
    c = jnp.cumsum(lf, axis=1).transpose(0, 2, 1)
    key_pos = jnp.arange(Lp)
    key_valid = key_pos >= n_pad
    scale = ATTN_HEAD_DIM ** -0.5
    qb = qp.reshape(Bsz, H, nb, BLOCK_Q, Dh).transpose(2, 0, 1, 3, 4)
    cqb = c.reshape(Bsz, H, nb, BLOCK_Q).transpose(2, 0, 1, 3)

    def block(args):
        qi, cqi, i = args
        s = jnp.einsum('bhqd,bhkd->bhqk', qi, kp,
                       preferred_element_type=jnp.float32) * scale
        s = s + cqi[..., :, None] - c[:, :, None, :]
        qpos = i * BLOCK_Q + jnp.arange(BLOCK_Q)
        mask = (key_pos[None, :] <= qpos[:, None]) & key_valid[None, :]
        s = jnp.where(mask, s, NEG)
        p = jax.nn.softmax(s, axis=-1)
        return jnp.einsum('bhqk,bhkd->bhqd', p.astype(vp.dtype), vp)

    out = lax.map(block, (qb, cqb, jnp.arange(nb)))
    out = out.transpose(1, 0, 3, 2, 4).reshape(Bsz, Lp, H, Dh)
    return out[:, n_pad:]


def hybrid_mixer(h, w_in, b_f, conv_w, out_gain, w_out):
    Bsz, L, _ = h.shape
    z = h @ w_in
    o = 0
    q = z[..., o:o + D_ATTN]; o += D_ATTN
    k = z[..., o:o + D_ATTN]; o += D_ATTN
    v = z[..., o:o + D_ATTN]; o += D_ATTN
    gate_b = z[..., o:o + D_CONV]; o += D_CONV
    gate_c = z[..., o:o + D_CONV]; o += D_CONV
    hc = z[..., o:o + D_CONV]; o += D_CONV
    f_logit = z[..., o:o + N_ATTN_HEADS]
    shp = (Bsz, L, N_ATTN_HEADS, ATTN_HEAD_DIM)
    log_f = jax.nn.log_sigmoid((f_logit + b_f).astype(jnp.float32))
    attn = forgetting_attention(q.reshape(shp), k.reshape(shp), v.reshape(shp), log_f)
    attn = attn.reshape(Bsz, L, D_ATTN)
    conv = gate_b * short_conv(gate_c * hc, conv_w)
    y = jnp.concatenate([attn, conv], axis=-1).reshape(Bsz, L, N_GROUPS, GROUP_DIM)
    y = rmsnorm(y, out_gain.reshape(N_GROUPS, GROUP_DIM)).reshape(Bsz, L, D_MIX)
    return y @ w_out


def swiglu(h, w_gate, w_up, w_down):
    return (jax.nn.silu(h @ w_gate) * (h @ w_up)) @ w_down


def setup_inputs(seed: int = 0) -> dict:
    key = jax.random.key(seed)
    ks = jax.random.split(key, 13)
    f32 = jnp.float32
    nrm = lambda k, s, sc: jax.random.normal(k, s, f32) * sc
    x = nrm(ks[0], (BATCH, SEQ, D_MODEL), 1.0)
    meta = nrm(ks[1], (N_META, D_MODEL), 1.0)
    norm_mix = 1.0 + nrm(ks[2], (DEPTH, D_MODEL), 0.02)
    w_in = nrm(ks[3], (DEPTH, D_MODEL, D_IN), D_MODEL ** -0.5)
    b_f = (jnp.linspace(1.0, 6.0, N_ATTN_HEADS, dtype=f32)[None, :]
           + nrm(ks[4], (DEPTH, N_ATTN_HEADS), 0.1))
    conv_w = nrm(ks[5], (DEPTH, CONV_WIDTH, D_CONV), CONV_WIDTH ** -0.5)
    out_gain = 1.0 + nrm(ks[6], (DEPTH, D_MIX), 0.02)
    w_out = nrm(ks[7], (DEPTH, D_MIX, D_MODEL), D_MIX ** -0.5)
    norm_ffn = 1.0 + nrm(ks[8], (DEPTH, D_MODEL), 0.02)
    w_gate = nrm(ks[9], (DEPTH, D_MODEL, D_FF), D_MODEL ** -0.5)
    w_up = nrm(ks[10], (DEPTH, D_MODEL, D_FF), D_MODEL ** -0.5)
    w_down = nrm(ks[11], (DEPTH, D_FF, D_MODEL), D_FF ** -0.5)
    final_norm = 1.0 + nrm(ks[12], (D_MODEL,), 0.02)
    return {"x": x, "meta": meta, "norm_mix": norm_mix, "w_in": w_in, "b_f": b_f,
            "conv_w": conv_w, "out_gain": out_gain, "w_out": w_out,
            "norm_ffn": norm_ffn, "w_gate": w_gate, "w_up": w_up,
            "w_down": w_down, "final_norm": final_norm}


def reference(x, meta, norm_mix, w_in, b_f, conv_w, out_gain, w_out,
              norm_ffn, w_gate, w_up, w_down, final_norm):
    Bsz = x.shape[0]
    m = jnp.broadcast_to(meta.astype(x.dtype)[None], (Bsz, N_META, D_MODEL))
    h = jnp.concatenate([m, x], axis=1)
    for l in range(DEPTH):
        h = h + hybrid_mixer(rmsnorm(h, norm_mix[l]), w_in[l], b_f[l], conv_w[l],
                             out_gain[l], w_out[l])
        h = h + swiglu(rmsnorm(h, norm_ffn[l]), w_gate[l], w_up[l], w_down[l])
    h = rmsnorm(h, final_norm)
    return h[:, N_META:]
```

```python
from contextlib import ExitStack
import numpy as np
import ml_dtypes
import concourse.bass as bass
import concourse.mybir as mybir
from concourse.bass_utils import run_bass_kernel_spmd

F32, BF16 = mybir.dt.float32, mybir.dt.bfloat16
AF = mybir.ActivationFunctionType
ALU = mybir.AluOpType

D = 2048
KC = 16
NH = 8
DFF = 5632
FC = 44
DIN = 6152
NMETA = 16
CH = 512
EPS = 1e-6
SCALE = 128 ** -0.5
NRANK = 4


class Sem:
    def __init__(self, h):
        self.h = h
        self.val = 0


class Buf:
    __slots__ = ("w", "r")

    def __init__(self):
        self.w = {}
        self.r = {}


class Eng:
    def __init__(self, name, eng, sem):
        self.name, self.eng, self.sem, self.cnt, self.seen = name, eng, sem, 0, {}
        self.is_pe = name == "pe"

    def wait_t(self, s, v):
        if v <= 0:
            return
        if s is self.sem:
            if self.is_pe or v > self.cnt:
                return
        if self.seen.get(s, 0) >= v:
            return
        self.eng.wait_ge(s.h, v)
        self.seen[s] = v

    def wait_bufs(self, reads, writes):
        for b in reads:
            for s, v in b.w.items():
                self.wait_t(s, v)
        for b in writes:
            for s, v in b.w.items():
                self.wait_t(s, v)
            for s, v in b.r.items():
                self.wait_t(s, v)


def _commit(s, t, reads, writes):
    for b in writes:
        if b.w.get(s, 0) < t:
            b.w[s] = t
    for b in reads:
        if b.r.get(s, 0) < t:
            b.r[s] = t


def op(E, fn, reads=(), writes=(), sig=True):
    E.wait_bufs(reads, writes)
    ins = fn()
    if sig:
        ins.then_inc(E.sem.h, 1)
        E.cnt += 1
        t = E.cnt
    else:
        t = E.cnt + 1
    _commit(E.sem, t, reads, writes)


class DmaQ:
    def __init__(self, E, sems):
        self.E, self.sems, self.i = E, sems, 0

    def dma(self, out, in_, reads=(), writes=()):
        s = self.sems[self.i]
        self.i = (self.i + 1) % len(self.sems)
        self.E.wait_t(s, s.val)
        self.E.wait_bufs(reads, writes)
        self.E.eng.dma_start(out=out, in_=in_).then_inc(s.h, 16)
        s.val += 16
        _commit(s, s.val, reads, writes)


def chunk_owner(j):
    i, pos = divmod(j, 8)
    if pos < 4:
        return pos, 2 * i
    return 7 - pos, 2 * i + 1


def chunk_global(r, lam):
    return 8 * (lam // 2) + (r if lam % 2 == 0 else 7 - r)


W_IN_COLS = ([128 * i for i in range(8)] + [1024 + 128 * i for i in range(8)]
             + [2048 + 128 * i for i in range(8)] + [3072 + 128 * i for i in range(8)])
for _g in range(8):
    W_IN_COLS += [4096 + 128 * _g, 5120 + 128 * _g]
NV = 160
BIG = 1.0e5 / SCALE


DEBUG = False


def build(NLOC=8, NLAYERS=2):
    nc = bass.Bass("TRN2", target_bir_lowering=False)
    TL = NMETA + NLOC * CH
    NG = 4 * NLOC
    NTOK = NMETA + NG * CH
    NKT = 1 + 4 * NG
    NSUB = 1 + 4 * NLOC
    NXL = NLOC * CH

    def din(name, shape, dt=F32):
        return nc.dram_tensor(name, shape, dt, kind="ExternalInput")

    x_in = din("x", [NXL, D])
    meta_in = din("meta", [NMETA, D])
    w_in = din("w_in", [2, D, DIN])
    w_out = din("w_out", [2, D, D])
    w_gate = din("w_gate", [2, D, DFF])
    w_up = din("w_up", [2, D, DFF])
    w_down = din("w_down", [2, DFF, D])
    vecs_in = din("vecs", [128, NV])
    bf_in = din("bf", [8, 2])
    identf_in = din("identf", [128, 128])
    identb_in = din("identb", [128, 128], BF16)
    tri_in = din("tri", [128, 128], BF16)
    oh_in = din("oh", [128, 4])
    pen_in = din("pen", [128, NLOC * NKT])
    out_d = nc.dram_tensor("out", [NXL, D], F32, kind="ExternalOutput")

    def dscr(name, shape, dt):
        if DEBUG and not name.startswith("w") and not name.endswith("_in") and not name.endswith("_out"):
            return nc.dram_tensor(name, shape, dt, kind="ExternalOutput")
        return nc.dram_tensor(name, shape, dt)

    wib = [dscr(f"wib{l}", [48, 128, KC, 128], BF16) for l in range(2)]
    wfb = [dscr(f"wfb{l}", [128, KC, 8], BF16) for l in range(2)]
    wob = [dscr(f"wob{l}", [16, 128, KC, 128], BF16) for l in range(2)]
    wgb = [dscr(f"wgb{l}", [FC, 128, KC, 128], BF16) for l in range(2)]
    wub = [dscr(f"wub{l}", [FC, 128, KC, 128], BF16) for l in range(2)]
    wdb = [dscr(f"wdb{l}", [16, 128, FC, 128], BF16) for l in range(2)]
    hT_d = dscr("hT", [D, TL], F32)
    qT_d = dscr("qT", [1024, TL], BF16)
    kmeta_d = dscr("kmeta", [1024, NMETA], BF16)
    vmeta_d = dscr("vmeta", [1024, NMETA], BF16)
    kg_in = [dscr(f"kg{i}_in", [1024, CH], BF16) for i in range(NLOC)]
    vg_in = [dscr(f"vg{i}_in", [1024, CH], BF16) for i in range(NLOC)]
    kg_out = [dscr(f"kg{i}_out", [NRANK * 1024, CH], BF16) for i in range(NLOC)]
    vg_out = [dscr(f"vg{i}_out", [NRANK * 1024, CH], BF16) for i in range(NLOC)]
    gbT_d = dscr("gbT", [1024, TL], BF16)
    uT_d = dscr("uT", [1024, TL], BF16)
    ut_in = dscr("ut_in", [1024, 2 * NLOC], BF16)
    ut_out = dscr("ut_out", [NRANK * 1024, 2 * NLOC], BF16)
    lf_in = dscr("lf_in", [8, NXL], F32)
    lf_out = dscr("lf_out", [NRANK * 8, NXL], F32)
    lfm_d = dscr("lfm", [8, NMETA], F32)
    yT_d = dscr("yT", [1024, TL], BF16)

    B = {k: Buf() for k in ["hT", "qT", "kmeta", "vmeta", "kg_in", "vg_in", "kg_out", "vg_out", "gbT", "uT",
                            "ut_in", "ut_out", "lf_in", "lf_out", "lfm", "yT", "out", "const"]}
    for i_ in range(NLOC):
        for n_ in ["kg_in", "vg_in", "kg_out", "vg_out"]:
            B[(n_, i_)] = Buf()
    Bw = {(n, l): Buf() for n in ["wib", "wfb", "wob", "wgb", "wub", "wdb"] for l in range(2)}

    es = ExitStack()
    with es:
        def sem(name):
            return Sem(es.enter_context(nc.semaphore(name)))

        PE = Eng("pe", nc.tensor, sem("s_pe"))
        ACT = Eng("act", nc.scalar, sem("s_act"))
        DVE = Eng("dve", nc.vector, sem("s_dve"))
        POOL = Eng("pool", nc.gpsimd, sem("s_pool"))
        SP = Eng("sp", nc.sync, sem("s_sp"))
        SPQ = DmaQ(SP, [sem(f"dq{i}") for i in range(24)])
        PQ = DmaQ(POOL, [sem(f"pq{i}") for i in range(16)])
        conv_jobs = []
        ccsems = [sem(f"cc{i}") for i in range(8)]
        cc_i = [0]

        sb_n = [0]

        def sb(name, shape, dt, stack=None):
            sb_n[0] += 1
            return (stack or es).enter_context(nc.sbuf_tensor(f"sb{sb_n[0]}_{name}", shape, dt))

        pbank = [es.enter_context(nc.psum_tensor(f"pb{i}", [128, 512], F32)) for i in range(8)]
        Bp = [Buf() for _ in range(8)]

        identf = sb("identf", [128, 128], F32)
        identb = sb("identb", [128, 128], BF16)
        tri = sb("tri", [128, 128], BF16)
        vecs = sb("vecs", [128, NV], F32)
        bfv = sb("bfv", [8, 2], F32)
        nbf = sb("nbf", [8, 2], F32)
        oh = sb("oh", [128, 4], F32)
        pen = sb("pen", [128, NLOC, NKT], F32)
        onesD = sb("onesD", [128, 128], BF16)
        onesG = sb("onesG", [128, 128], BF16)
        ones1 = sb("ones1", [128, 128], BF16)
        ones8f = sb("ones8f", [8, 128], F32)
        zerosf = sb("zerosf", [128, 512], F32)
        Bc = B["const"]
        SPQ.dma(identf[:], identf_in.ap(), writes=[Bc])
        SPQ.dma(identb[:], identb_in.ap(), writes=[Bc])
        SPQ.dma(tri[:], tri_in.ap(), writes=[Bc])
        SPQ.dma(vecs[:], vecs_in.ap(), writes=[Bc])
        SPQ.dma(bfv[:], bf_in.ap(), writes=[Bc])
        SPQ.dma(oh[:], oh_in.ap(), writes=[Bc])
        SPQ.dma(pen[:], pen_in.ap().rearrange("p (a b) -> p a b", a=NLOC), writes=[Bc])
        op(DVE, lambda: nc.vector.memset(onesD[:], 1.0 / D), writes=[Bc])
        op(DVE, lambda: nc.vector.memset(onesG[:], 1.0 / 128), writes=[Bc])
        op(DVE, lambda: nc.vector.memset(ones1[:], 1.0), writes=[Bc])
        op(DVE, lambda: nc.vector.memset(ones8f[:], 1.0), writes=[Bc])
        op(DVE, lambda: nc.vector.memset(zerosf[:], 0.0), writes=[Bc])
        op(DVE, lambda: nc.vector.tensor_scalar(nbf[:], bfv[:], -1.0, None, ALU.mult), reads=[Bc], writes=[Bc])

        def vcol(c):
            return vecs[:, c:c + 1]

        def convert(l, defer=False):
            jobs = []

            class _Q:
                @staticmethod
                def dma(out, in_, writes):
                    jobs.append((out, in_, writes))
            PQ_ = _Q
            convert_body(l, PQ_)
            if defer:
                conv_jobs.extend(jobs)
            else:
                for (o_, i_, w_) in jobs:
                    PQ.dma(o_, i_, writes=w_)

        def conv_slice(n):
            for _ in range(min(n, len(conv_jobs))):
                o_, i_, w_ = conv_jobs.pop(0)
                PQ.dma(o_, i_, writes=w_)

        def convert_body(l, PQ):
            for bi, c0 in enumerate(W_IN_COLS):
                PQ.dma(wib[l].ap()[bi], w_in.ap()[l, :, c0:c0 + 128].rearrange("(k p) j -> p k j", p=128),
                       writes=[Bw[("wib", l)]])
            PQ.dma(wfb[l].ap(), w_in.ap()[l, :, 6144:6152].rearrange("(k p) j -> p k j", p=128),
                   writes=[Bw[("wfb", l)]])
            for m in range(16):
                PQ.dma(wob[l].ap()[m], w_out.ap()[l, :, m * 128:(m + 1) * 128].rearrange("(k p) j -> p k j", p=128),
                       writes=[Bw[("wob", l)]])
            for f in range(FC):
                PQ.dma(wgb[l].ap()[f], w_gate.ap()[l, :, f * 128:(f + 1) * 128].rearrange("(k p) j -> p k j", p=128),
                       writes=[Bw[("wgb", l)]])
                PQ.dma(wub[l].ap()[f], w_up.ap()[l, :, f * 128:(f + 1) * 128].rearrange("(k p) j -> p k j", p=128),
                       writes=[Bw[("wub", l)]])
            for m in range(16):
                for half in range(2):
                    PQ.dma(wdb[l].ap()[m, :, half * 22:(half + 1) * 22, :],
                           w_down.ap()[l, half * 2816:(half + 1) * 2816, m * 128:(m + 1) * 128].rearrange(
                               "(k p) j -> p k j", p=128),
                           writes=[Bw[("wdb", l)]])

        convert(0)

        ev_i = [0]

        def evac(out, in_, reads, writes):
            ev_i[0] ^= 1
            if ev_i[0]:
                op(ACT, lambda: nc.scalar.copy(out, in_), reads, writes)
            else:
                op(DVE, lambda: nc.vector.tensor_copy(out, in_), reads, writes)

        def rsqrt_act(out, in_, eps, reads, writes):
            op(ACT, lambda: nc.scalar.activation(out, in_, AF.Ln, bias=eps, scale=1.0), reads, writes)
            op(ACT, lambda: nc.scalar.activation(out, out, AF.Exp, scale=-0.5), writes, writes)

        acc_i = [0]

        def next_acc(banks=(0, 1, 2, 3, 4, 5)):
            acc_i[0] = (acc_i[0] + 1) % len(banks)
            return banks[acc_i[0]]

        def unit_info(u):
            if u == 0:
                return 0, [(0, NMETA), (NMETA, CH)], NMETA + CH
            return NMETA + u * CH, [(0, CH)], CH

        def allgather(src, dst, bsrc, bdst):
            s = ccsems[cc_i[0] % len(ccsems)]
            cc_i[0] += 1
            POOL.wait_t(s, s.val)
            POOL.wait_bufs([bsrc], [bdst])
            nc.gpsimd.collective_compute("AllGather", ALU.bypass, replica_groups=[[0, 1, 2, 3], [4, 5, 6, 7]],
                                         ins=[src.ap().opt()], outs=[dst.ap().opt()]).then_inc(s.h, 1)
            s.val += 1
            _commit(s, s.val, [bsrc], [bdst])

        TUM = NMETA + CH

        def phase1(l):
            with ExitStack() as st:
                hT = sb("p1_hT", [128, KC, TUM], F32, st)
                Bh = [Buf() for _ in range(KC)]
                sq = sb("p1_sq", [128, KC, TUM], BF16, st)
                Bsq = [Buf() for _ in range(KC)]
                hns = [sb(f"p1_hn{i}", [128, KC, TUM], BF16, st) for i in range(2)]
                Bhns = [Buf(), Buf()]
                rstd = sb("p1_rstd", [128, TUM], F32, st)
                Brs = Buf()
                wt = [sb(f"p1_wt{i}", [128, 4, KC, 128], BF16, st) for i in range(2)]
                Bwt = [Buf(), Buf()]
                wf = sb("p1_wf", [128, KC, 8], BF16, st)
                Bwf = Buf()
                ost = [sb(f"p1_ost{i}", [128, TUM], BF16, st) for i in range(3)]
                Bost = [Buf() for _ in range(3)]
                gcs = sb("p1_gcs", [128, TUM], BF16, st)
                Bgcs = Buf()
                lfe = sb("p1_lfe", [8, TUM], F32, st)
                lfs = sb("p1_lfs", [8, TUM], F32, st)
                Blf = Buf()
                if l == 0:
                    xt = sb("p1_xt", [128, 4, D], F32, st)
                    Bxt = [Buf() for _ in range(4)]
                    xm = sb("p1_xm", [NMETA, D], F32, st)
                    Bxm = Buf()
                SPQ.dma(wf[:], wfb[l].ap(), reads=[Bw[("wfb", l)]], writes=[Bwf])
                ost_i = [0]
                def prologue(u):
                    t0, segs, TU = unit_info(u)
                    hn, Bhn = hns[u % 2], Bhns[u % 2]
                    if l == 0:
                        for (so, n) in segs:
                            if n == NMETA:
                                SPQ.dma(xm[:], meta_in.ap(), writes=[Bxm])
                                for kc in range(KC):
                                    op(PE, lambda: nc.tensor.transpose(pbank[6][:, kc * 16:(kc + 1) * 16],
                                                                       xm[:, kc * 128:(kc + 1) * 128],
                                                                       identf[:NMETA, :NMETA]),
                                       reads=[Bxm, Bc], writes=[Bp[6]], sig=(kc == KC - 1))
                                for kc in range(KC):
                                    evac(hT[:, kc, so:so + n], pbank[6][:, kc * 16:(kc + 1) * 16], [Bp[6]], [Bh[kc]])
                            else:
                                for s4 in range(4):
                                    r0 = u * CH + s4 * 128
                                    SPQ.dma(xt[:, s4, :], x_in.ap()[r0:r0 + 128, :], writes=[Bxt[s4]])
                                for kc in range(KC):
                                    bk = next_acc()
                                    for s4 in range(4):
                                        op(PE, lambda: nc.tensor.transpose(pbank[bk][:, s4 * 128:(s4 + 1) * 128],
                                                                           xt[:, s4, kc * 128:(kc + 1) * 128],
                                                                           identf[:]),
                                           reads=[Bxt[s4], Bc], writes=[Bp[bk]], sig=(s4 == 3))
                                    evac(hT[:, kc, so:so + n], pbank[bk][:, :n], [Bp[bk]], [Bh[kc]])
                        SPQ.dma(hT_d.ap()[:, t0:t0 + TU].rearrange("(k p) t -> p k t", p=128), hT[:, :, :TU],
                                reads=Bh, writes=[B["hT"]])
                    else:
                        SPQ.dma(hT[:, :, :TU], hT_d.ap()[:, t0:t0 + TU].rearrange("(k p) t -> p k t", p=128),
                                reads=[B["hT"]], writes=Bh)
                    for kc in range(KC):
                        op(ACT, lambda: nc.scalar.activation(sq[:, kc, :TU], hT[:, kc, :TU], AF.Square),
                           reads=[Bh[kc]], writes=[Bsq[kc]])
                    for (so, n) in segs:
                        for kc in range(KC):
                            op(PE, lambda: nc.tensor.matmul(pbank[6][:, :n], lhsT=onesD[:], rhs=sq[:, kc, so:so + n],
                                                            start=(kc == 0), stop=(kc == KC - 1)),
                               reads=[Bsq[kc], Bc], writes=[Bp[6]], sig=(kc == KC - 1))
                        rsqrt_act(rstd[:, so:so + n], pbank[6][:, :n], EPS, [Bp[6]], [Brs])
                    for kc in range(KC):
                        op(DVE, lambda: nc.vector.scalar_tensor_tensor(hn[:, kc, :TU], hT[:, kc, :TU],
                                                                       vcol(l * 16 + kc), rstd[:, :TU],
                                                                       ALU.mult, ALU.mult),
                           reads=[Bh[kc], Brs, Bc], writes=[Bhn])
                def proj(u):
                    t0, segs, TU = unit_info(u)
                    hn, Bhn = hns[u % 2], Bhns[u % 2]
                    def load_w(bg):
                        SPQ.dma(wt[bg % 2][:], wib[l].ap()[4 * bg:4 * bg + 4].rearrange("b p k j -> p b k j"),
                                reads=[Bw[("wib", l)]], writes=[Bwt[bg % 2]])

                    if u == 0:
                        load_w(0)
                    for bg in range(12):
                        if bg + 1 < 12:
                            load_w(bg + 1)
                        elif u + 1 < NLOC:
                            load_w(0)
                        for b4 in range(4):
                            blk = 4 * bg + b4
                            kind = blk // 8 if blk < 32 else (4 if blk % 2 == 0 else 5)
                            if kind == 4:
                                dst, Bdst = gcs, Bgcs
                            else:
                                oi = ost_i[0] = (ost_i[0] + 1) % 3
                                dst, Bdst = ost[oi], Bost[oi]
                            for (so, n) in segs:
                                bk = next_acc()
                                for kc in range(KC):
                                    op(PE, lambda: nc.tensor.matmul(pbank[bk][:, :n], lhsT=wt[bg % 2][:, b4, kc, :],
                                                                    rhs=hn[:, kc, so:so + n],
                                                                    start=(kc == 0), stop=(kc == KC - 1)),
                                       reads=[Bwt[bg % 2], Bhn], writes=[Bp[bk]], sig=(kc == KC - 1))
                                if kind == 5:
                                    op(DVE, lambda: nc.vector.tensor_tensor(dst[:, so:so + n], pbank[bk][:, :n],
                                                                            gcs[:, so:so + n], ALU.mult),
                                       reads=[Bp[bk], Bgcs], writes=[Bdst])
                                else:
                                    evac(dst[:, so:so + n], pbank[bk][:, :n], [Bp[bk]], [Bdst])
                            if kind == 4:
                                continue
                            xo = u * CH
                            if kind == 0:
                                SPQ.dma(qT_d.ap()[blk * 128:(blk + 1) * 128, t0:t0 + TU], dst[:, :TU],
                                        reads=[Bdst], writes=[B["qT"]])
                            elif kind in (1, 2):
                                hb = blk - 8 * kind
                                md, gd, bm, bgn = ((kmeta_d, kg_in, "kmeta", "kg_in") if kind == 1
                                                   else (vmeta_d, vg_in, "vmeta", "vg_in"))
                                if u == 0:
                                    SPQ.dma(md.ap()[hb * 128:(hb + 1) * 128, :], dst[:, :NMETA],
                                            reads=[Bdst], writes=[B[bm]])
                                SPQ.dma(gd[u].ap()[hb * 128:(hb + 1) * 128, :], dst[:, TU - CH:TU],
                                        reads=[Bdst], writes=[B[(bgn, u)]])
                            elif kind == 3:
                                g = blk - 24
                                SPQ.dma(gbT_d.ap()[g * 128:(g + 1) * 128, t0:t0 + TU], dst[:, :TU],
                                        reads=[Bdst], writes=[B["gbT"]])
                            else:
                                g = (blk - 32) // 2
                                SPQ.dma(uT_d.ap()[g * 128:(g + 1) * 128, t0:t0 + TU], dst[:, :TU],
                                        reads=[Bdst], writes=[B["uT"]])
                                SPQ.dma(ut_in.ap()[g * 128:(g + 1) * 128, 2 * u:2 * u + 2], dst[:, TU - 2:TU],
                                        reads=[Bdst], writes=[B["ut_in"]])
                    for (so, n) in segs:
                        for kc in range(KC):
                            op(PE, lambda: nc.tensor.matmul(pbank[7][:8, :n], lhsT=wf[:, kc, :], rhs=hn[:, kc, so:so + n],
                                                            start=(kc == 0), stop=(kc == KC - 1)),
                               reads=[Bwf, Bhn], writes=[Bp[7]], sig=(kc == KC - 1))
                        op(ACT, lambda: nc.scalar.activation(lfe[:, so:so + n], pbank[7][:8, :n], AF.Exp,
                                                             bias=nbf[:, l:l + 1], scale=-1.0),
                           reads=[Bp[7], Bc], writes=[Blf])
                        op(ACT, lambda: nc.scalar.activation(lfe[:, so:so + n], lfe[:, so:so + n], AF.Ln,
                                                             bias=1.0, scale=1.0),
                           reads=[Blf], writes=[Blf])
                        op(DVE, lambda: nc.vector.tensor_scalar(lfs[:, so:so + n], lfe[:, so:so + n], -1.0, None,
                                                                ALU.mult),
                           reads=[Blf], writes=[Blf])
                    if u == 0:
                        SPQ.dma(lfm_d.ap(), lfs[:, :NMETA], reads=[Blf], writes=[B["lfm"]])
                    SPQ.dma(lf_in.ap()[:, u * CH:(u + 1) * CH], lfs[:, TU - CH:TU], reads=[Blf], writes=[B["lf_in"]])
                    allgather(kg_in[u], kg_out[u], B[("kg_in", u)], B[("kg_out", u)])
                    allgather(vg_in[u], vg_out[u], B[("vg_in", u)], B[("vg_out", u)])
                prologue(0)
                for u in range(NLOC):
                    if u + 1 < NLOC:
                        prologue(u + 1)
                    proj(u)
            allgather(lf_in, lf_out, B["lf_in"], B["lf_out"])
            allgather(ut_in, ut_out, B["ut_in"], B["ut_out"])


        def barrier():
            engs = [PE, ACT, DVE, POOL]
            dsems = SPQ.sems + PQ.sems + ccsems
            for E in engs + [SP]:
                for F_ in engs:
                    if F_ is not E:
                        E.wait_t(F_.sem, F_.cnt)
                for s in dsems:
                    E.wait_t(s, s.val)

        def gcols(jj):
            rho, lam = chunk_owner(jj)
            return rho, lam * CH

        def jmax(lam):
            return 8 * (lam // 2) + (3 if lam % 2 == 0 else 7)

        def cands(lam):
            return [chunk_global(r, lam) for r in range(4)]

        def phase2(l):
            with ExitStack() as st:
                CTn = sb("p2_CTn", [128, NKT, 8], F32, st)
                CTo = sb("p2_CTo", [128, 4 * NLOC, 8], F32, st)
                Rbc = sb("p2_Rbc", [128, 8 * NSUB], F32, st)
                Btab = Buf()
                with ExitStack() as st2:
                    lfF = sb("p2_lfF", [8, NTOK], F32, st2)
                    cF = sb("p2_cF", [8, NTOK], F32, st2)
                    cown = sb("p2_cown", [8, NXL], F32, st2)
                    Rm = sb("p2_Rm", [8, NSUB], F32, st2)
                    Dh = sb("p2_Dh", [8, NSUB], F32, st2)
                    Bl, Bcf, Bco, Brm, Bdh = Buf(), Buf(), Buf(), Buf(), Buf()
                    SPQ.dma(lfF[:, :NMETA], lfm_d.ap(), reads=[B["lfm"]], writes=[Bl])
                    for jj in range(NG):
                        rho, co = gcols(jj)
                        SPQ.dma(lfF[:, NMETA + jj * CH:NMETA + (jj + 1) * CH],
                                lf_out.ap()[rho * 8:(rho + 1) * 8, co:co + CH], reads=[B["lf_out"]], writes=[Bl])
                    pos = 0
                    while pos < NTOK:
                        n = min(2048, NTOK - pos)
                        init = 0.0 if pos == 0 else cF[:, pos - 1:pos]
                        op(DVE, lambda: nc.vector.tensor_tensor_scan(cF[:, pos:pos + n], lfF[:, pos:pos + n],
                                                                     lfF[:, pos:pos + n], init, ALU.add, ALU.min),
                           reads=[Bl, Bcf], writes=[Bcf])
                        pos += n
                    for lam in range(NLOC):
                        cs = cands(lam)
                        dstc = cown[:, lam * CH:(lam + 1) * CH]
                        for r4 in range(4):
                            src = cF[:, NMETA + cs[r4] * CH:NMETA + (cs[r4] + 1) * CH]
                            if r4 == 0:
                                op(DVE, lambda: nc.vector.tensor_scalar(dstc, src, oh[:8, 0:1], None, ALU.mult),
                                   reads=[Bcf, Bc], writes=[Bco])
                            else:
                                op(DVE, lambda: nc.vector.scalar_tensor_tensor(dstc, src, oh[:8, r4:r4 + 1], dstc,
                                                                               ALU.mult, ALU.add),
                                   reads=[Bcf, Bc, Bco], writes=[Bco])
                    op(DVE, lambda: nc.vector.tensor_tensor(Rm[:, 0:1], cF[:, 0:1], cF[:, NMETA - 1:NMETA], ALU.add),
                       reads=[Bcf], writes=[Brm])
                    for s in range(4 * NLOC):
                        a = s * 128
                        op(DVE, lambda: nc.vector.tensor_tensor(Rm[:, 1 + s:2 + s], cown[:, a:a + 1],
                                                                cown[:, a + 127:a + 128], ALU.add),
                           reads=[Bco], writes=[Brm])
                    for h in range(NH):
                        op(DVE, lambda: nc.vector.tensor_scalar(Dh[:], Rm[:], identf[:8, h:h + 1], None, ALU.mult),
                           reads=[Brm, Bc], writes=[Bdh])
                        op(PE, lambda: nc.tensor.matmul(pbank[6][:, h * NSUB:(h + 1) * NSUB], lhsT=ones8f[:], rhs=Dh[:],
                                                        start=True, stop=True),
                           reads=[Bdh, Bc], writes=[Bp[6]])
                    op(DVE, lambda: nc.vector.tensor_scalar(Rbc[:], pbank[6][:, :8 * NSUB], 0.5 / SCALE, None, ALU.mult),
                       reads=[Bp[6]], writes=[Btab])
                    for b0 in range(0, NKT, 64):
                        cnt = min(64, NKT - b0)
                        bk = next_acc()
                        for i in range(cnt):
                            kt = b0 + i
                            k0, nk = (0, NMETA) if kt == 0 else (NMETA + (kt - 1) * 128, 128)
                            op(PE, lambda: nc.tensor.transpose(pbank[bk][:nk, i * 8:(i + 1) * 8], cF[:, k0:k0 + nk],
                                                               identf[:8, :8]),
                               reads=[Bcf, Bc], writes=[Bp[bk]], sig=(i == cnt - 1))
                        op(DVE, lambda: nc.vector.tensor_scalar(
                            CTn[:, b0:b0 + cnt, :], pbank[bk][:, :cnt * 8].rearrange("p (a b) -> p a b", b=8),
                            -1.0 / SCALE, None, ALU.mult), reads=[Bp[bk]], writes=[Btab])
                    bk = next_acc()
                    for i in range(4 * NLOC):
                        op(PE, lambda: nc.tensor.transpose(pbank[bk][:, i * 8:(i + 1) * 8], cown[:, i * 128:(i + 1) * 128],
                                                           identf[:8, :8]),
                           reads=[Bco, Bc], writes=[Bp[bk]], sig=(i == 4 * NLOC - 1))
                    op(DVE, lambda: nc.vector.tensor_scalar(
                        CTo[:], pbank[bk][:, :4 * NLOC * 8].rearrange("p (a b) -> p a b", b=8),
                        -1.0 / SCALE, None, ALU.mult), reads=[Bp[bk]], writes=[Btab])
                    if DEBUG:
                        dC = nc.dram_tensor(f"dbg_CTn{l}", [128, NKT * 8], F32, kind="ExternalOutput")
                        dR = nc.dram_tensor(f"dbg_Rbc{l}", [128, 8 * NSUB], F32, kind="ExternalOutput")
                        dO = nc.dram_tensor(f"dbg_CTo{l}", [128, 4 * NLOC * 8], F32, kind="ExternalOutput")
                        dcF = nc.dram_tensor(f"dbg_cF{l}", [8, NTOK], F32, kind="ExternalOutput")
                        SPQ.dma(dC.ap(), CTn[:].rearrange("p a b -> p (a b)"), reads=[Btab], writes=[B["out"]])
                        SPQ.dma(dR.ap(), Rbc[:], reads=[Btab], writes=[B["out"]])
                        SPQ.dma(dO.ap(), CTo[:].rearrange("p a b -> p (a b)"), reads=[Btab], writes=[B["out"]])
                        SPQ.dma(dcF.ap(), cF[:], reads=[Bcf], writes=[B["out"]])
                    barrier()

                kT = [sb(f"p2_kT{i}", [128, NTOK], BF16, st) for i in range(2)]
                BkT = [Buf(), Buf()]
                kown = [sb(f"p2_ko{i}", [128, NXL], BF16, st) for i in range(2)]
                Bko = [Buf(), Buf()]
                vT = sb("p2_vT", [128, NTOK], BF16, st)
                BvT = Buf()
                vownT = sb("p2_voT", [128, NXL], BF16, st)
                BvoT = Buf()
                V = sb("p2_V", [128, NKT, 128], BF16, st)
                BV = Buf()
                Vo = sb("p2_Vo", [128, 4 * NLOC, 128], BF16, st)
                BVo = Buf()
                qs = [sb(f"p2_q{i}", [128, CH], BF16, st) for i in range(2)]
                Bq = [Buf(), Buf()]
                Rt = [sb(f"p2_Rt{i}", [128, CH], F32, st) for i in range(2)]
                BRt = [Buf(), Buf()]
                CTl = [sb(f"p2_CTl{i}", [128, NKT], F32, st) for i in range(2)]
                BCl = [Buf(), Buf()]
                Tb = [sb(f"p2_T{i}", [128, CH], F32, st) for i in range(4)]
                BT = [Buf() for _ in range(4)]
                Pb = [sb(f"p2_P{i}", [128, CH], BF16, st) for i in range(5)]
                BP = [Buf() for _ in range(5)]
                LA = 3
                pending = []
                Osb = sb("p2_Osb", [128, CH], F32, st)
                d2 = sb("p2_d2", [128, CH], F32, st)
                sqo = sb("p2_sqo", [128, CH], BF16, st)
                uu = sb("p2_uu", [128, CH], F32, st)
                yst = [sb(f"p2_y{i}", [128, CH], BF16, st) for i in range(2)]
                BOs, Bd2, Bsqo, Buu = Buf(), Buf(), Buf(), Buf()
                Byst = [Buf(), Buf()]
                TRB = [7, 6]
                pb67b = [pbank[7][:].bitcast(BF16), pbank[6][:].bitcast(BF16)]

                def load_head(h):
                    kb, Bk = kT[h % 2], BkT[h % 2]
                    hs = slice(h * 128, (h + 1) * 128)
                    SPQ.dma(kb[:, :NMETA], kmeta_d.ap()[hs, :], reads=[B["kmeta"]], writes=[Bk])
                    SPQ.dma(vT[:, :NMETA], vmeta_d.ap()[hs, :], reads=[B["vmeta"]], writes=[BvT])
                    for jj in range(NG):
                        rho, lp = chunk_owner(jj)
                        rs_ = slice(rho * 1024 + h * 128, rho * 1024 + (h + 1) * 128)
                        SPQ.dma(kb[:, NMETA + jj * CH:NMETA + (jj + 1) * CH], kg_out[lp].ap()[rs_, :],
                                reads=[B[("kg_out", lp)]], writes=[Bk])
                        SPQ.dma(vT[:, NMETA + jj * CH:NMETA + (jj + 1) * CH], vg_out[lp].ap()[rs_, :],
                                reads=[B[("vg_out", lp)]], writes=[BvT])
                    for lam in range(NLOC):
                        SPQ.dma(kown[h % 2][:, lam * CH:(lam + 1) * CH], kg_in[lam].ap()[hs, :],
                                reads=[B[("kg_in", lam)]], writes=[Bko[h % 2]])
                        SPQ.dma(vownT[:, lam * CH:(lam + 1) * CH], vg_in[lam].ap()[hs, :],
                                reads=[B[("vg_in", lam)]], writes=[BvoT])

                tcount = [0]
                segcount = [0]
                load_head(0)
                for h in range(NH):
                    while pending:
                        pending.pop(0)()
                    ti = 0
                    for b0 in range(0, NKT, 8):
                        cnt = min(8, NKT - b0)
                        pi = ti % 2
                        ti += 1
                        for i in range(cnt):
                            kt = b0 + i
                            k0, nk = (0, NMETA) if kt == 0 else (NMETA + (kt - 1) * 128, 128)
                            op(PE, lambda: nc.tensor.transpose(pb67b[pi][:nk, i * 128:(i + 1) * 128],
                                                               vT[:, k0:k0 + nk], identb[:]),
                               reads=[BvT, Bc], writes=[Bp[TRB[pi]]], sig=(i == cnt - 1))
                        evac(V[:, b0:b0 + cnt, :], pb67b[pi][:, :cnt * 128].rearrange("p (a d) -> p a d", d=128),
                             [Bp[TRB[pi]]], [BV])
                    for b0 in range(0, 4 * NLOC, 8):
                        cnt = min(8, 4 * NLOC - b0)
                        pi = ti % 2
                        ti += 1
                        for i in range(cnt):
                            op(PE, lambda: nc.tensor.transpose(pb67b[pi][:, i * 128:(i + 1) * 128],
                                                               vownT[:, (b0 + i) * 128:(b0 + i + 1) * 128], identb[:]),
                               reads=[BvoT, Bc], writes=[Bp[TRB[pi]]], sig=(i == cnt - 1))
                        evac(Vo[:, b0:b0 + cnt, :], pb67b[pi][:, :cnt * 128].rearrange("p (a d) -> p a d", d=128),
                             [Bp[TRB[pi]]], [BVo])
                    if h + 1 < NH:
                        load_head(h + 1)
                    kb, Bk = kT[h % 2], BkT[h % 2]
                    ko, Bkow = kown[h % 2], Bko[h % 2]
                    for u in range(NLOC):
                        seglist = ([("meta", 0, NMETA)] if u == 0 else []) + [("chunk", NMETA + u * CH, CH)]
                        for (skind, t0, nq) in seglist:
                            si = segcount[0] = segcount[0] + 1
                            q, Bqq = qs[si % 2], Bq[si % 2]
                            rt, Brt = Rt[si % 2], BRt[si % 2]
                            ctl, Bcl = CTl[si % 2], BCl[si % 2]
                            ob, db = 4, 5
                            SPQ.dma(q[:, :nq], qT_d.ap()[h * 128:(h + 1) * 128, t0:t0 + nq], reads=[B["qT"]], writes=[Bqq])
                            tiles = []
                            if skind == "meta":
                                op(POOL, lambda: nc.gpsimd.tensor_scalar(rt[:, :nq], zerosf[:, :nq],
                                                                         Rbc[:, h * NSUB:h * NSUB + 1], None, ALU.add),
                                   reads=[Btab, Bc], writes=[Brt])
                                tiles.append((kb[:, 0:NMETA], V[:NMETA, 0, :], CTn[:NMETA, 0, h:h + 1], NMETA, 0,
                                              [Bk], [BV], [Btab]))
                            else:
                                lam = u
                                for sj in range(4):
                                    s = 1 + 4 * lam + sj
                                    op(POOL, lambda: nc.gpsimd.tensor_scalar(
                                        rt[:, sj * 128:(sj + 1) * 128], zerosf[:, :128],
                                        Rbc[:, h * NSUB + s:h * NSUB + s + 1], None, ALU.add),
                                       reads=[Btab, Bc], writes=[Brt])
                                npast = 1 + 4 * jmax(lam)
                                op(DVE, lambda: nc.vector.tensor_tensor(ctl[:, :npast], CTn[:, :npast, h],
                                                                        pen[:, lam, :npast], ALU.add),
                                   reads=[Btab, Bc], writes=[Bcl])
                                tiles.append((kb[:, 0:NMETA], V[:NMETA, 0, :], ctl[:NMETA, 0:1], NMETA, None,
                                              [Bk], [BV], [Bcl]))
                                for kt in range(1, npast):
                                    k0 = NMETA + (kt - 1) * 128
                                    tiles.append((kb[:, k0:k0 + 128], V[:, kt, :], ctl[:, kt:kt + 1], 128, None,
                                                  [Bk], [BV], [Bcl]))
                                for i in range(4):
                                    k0 = lam * CH + i * 128
                                    tiles.append((ko[:, k0:k0 + 128], Vo[:, 4 * lam + i, :],
                                                  CTo[:, 4 * lam + i, h:h + 1], 128, i, [Bkow], [BVo], [Btab]))
                            nt = len(tiles)

                            def emit_S(i):
                                kap, vap, bcol, nk, dg, rk, rv, rb = tiles[i]
                                sbk = (tcount[0] + i) % 4
                                op(PE, lambda: nc.tensor.matmul(pbank[sbk][:nk, :nq], lhsT=kap, rhs=q[:, :nq],
                                                                start=True, stop=True),
                                   reads=rk + [Bqq], writes=[Bp[sbk]])

                            for i in range(min(LA, nt)):
                                emit_S(i)
                            for i in range(nt):
                                if i + LA < nt:
                                    emit_S(i + LA)
                                kap, vap, bcol, nk, dg, rk, rv, rb = tiles[i]
                                g = tcount[0] + i
                                sbk = g % 4
                                T_, BT_ = Tb[g % 4], BT[g % 4]
                                P_, BP_ = Pb[g % 5], BP[g % 5]
                                if i == min(4, nt - 1) and pending:
                                    pending.pop(0)()
                                c0 = 0 if (dg is None or skind == "meta") else dg * 128
                                op(DVE, lambda: nc.vector.scalar_tensor_tensor(T_[:nk, c0:nq], pbank[sbk][:nk, c0:nq],
                                                                               bcol, rt[:nk, c0:nq], ALU.add, ALU.add),
                                   reads=[Bp[sbk], Brt] + rb, writes=[BT_])
                                op(ACT, lambda: nc.scalar.activation(P_[:nk, c0:nq], T_[:nk, c0:nq], AF.Exp, scale=SCALE),
                                   reads=[BT_], writes=[BP_])
                                if dg is not None:
                                    if c0 > 0:
                                        op(POOL, lambda: nc.gpsimd.memset(P_[:nk, :c0], 0.0), reads=[], writes=[BP_])
                                    w = min(128, nq)
                                    op(POOL, lambda: nc.gpsimd.tensor_tensor(P_[:nk, c0:c0 + w], P_[:nk, c0:c0 + w],
                                                                             tri[:nk, :w], ALU.mult),
                                       reads=[Bc], writes=[BP_])
                                op(PE, lambda: nc.tensor.matmul(pbank[ob][:, :nq], lhsT=vap, rhs=P_[:nk, :nq],
                                                                start=(i == 0), stop=(i == nt - 1)),
                                   reads=rv + [BP_], writes=[Bp[ob]], sig=False)
                                op(PE, lambda: nc.tensor.matmul(pbank[db][:, :nq], lhsT=ones1[:nk, :], rhs=P_[:nk, :nq],
                                                                start=(i == 0), stop=(i == nt - 1)),
                                   reads=[Bc, BP_], writes=[Bp[db]], sig=True)
                            tcount[0] += nt
                            ys, Bys = yst[si % 2], Byst[si % 2]
                            op(DVE, lambda: nc.vector.reciprocal(d2[:, :nq], pbank[db][:, :nq]), reads=[Bp[db]], writes=[Bd2])
                            op(DVE, lambda: nc.vector.tensor_tensor(Osb[:, :nq], pbank[ob][:, :nq], d2[:, :nq], ALU.mult),
                               reads=[Bp[ob], Bd2], writes=[BOs])
                            op(POOL, lambda: nc.gpsimd.tensor_tensor(sqo[:, :nq], Osb[:, :nq], Osb[:, :nq], ALU.mult),
                               reads=[BOs], writes=[Bsqo])

                            def ep_tail(ys=ys, Bys=Bys, nq=nq, t0=t0, h=h):
                                op(PE, lambda: nc.tensor.matmul(pbank[6][:, :nq], lhsT=onesG[:], rhs=sqo[:, :nq],
                                                                start=True, stop=True),
                                   reads=[Bsqo, Bc], writes=[Bp[6]])
                                rsqrt_act(uu[:, :nq], pbank[6][:, :nq], EPS, [Bp[6]], [Buu])
                                op(DVE, lambda: nc.vector.scalar_tensor_tensor(ys[:, :nq], Osb[:, :nq],
                                                                               vcol(80 + l * 16 + h), uu[:, :nq],
                                                                               ALU.mult, ALU.mult),
                                   reads=[BOs, Buu, Bc], writes=[Bys])
                                SPQ.dma(yT_d.ap()[h * 128:(h + 1) * 128, t0:t0 + nq], ys[:, :nq], reads=[Bys],
                                        writes=[B["yT"]])

                            pending.append(ep_tail)
                            if nt <= 4:
                                while pending:
                                    pending.pop(0)()
                while pending:
                    pending.pop(0)()

        def phase34(l, last):
            with ExitStack() as st:
                hT1 = sb("p3_hT1", [128, KC, TUM], F32, st)
                Bh = [Buf() for _ in range(KC)]
                yTs = sb("p3_yTs", [128, KC, TUM], BF16, st)
                By = Buf()
                aT = sb("p3_aT", [128, FC, TUM], BF16, st)
                Ba = [Buf() for _ in range(FC)]
                wA = [sb(f"p3_wA{i}", [128, KC, 128], BF16, st) for i in range(2)]
                wB_ = [sb(f"p3_wB{i}", [128, KC, 128], BF16, st) for i in range(2)]
                BwA = [Buf(), Buf()]
                BwB = [Buf(), Buf()]
                wD = [sb(f"p3_wD{i}", [128, FC, 128], BF16, st) for i in range(2)]
                BwD = [Buf(), Buf()]
                wO = [sb(f"p3_wO{i}", [128, 2, KC, 128], BF16, st) for i in range(2)]
                BwO = [Buf(), Buf()]
                uh = sb("p3_uh", [128, 8, 2 + CH], BF16, st)
                Buh = Buf()
                gb = sb("p3_gb", [128, 8, TUM], BF16, st)
                Bgb = Buf()
                t1 = sb("p3_t1", [128, CH], F32, st)
                cv = sb("p3_cv", [128, CH], F32, st)
                sqc = sb("p3_sqc", [128, CH], BF16, st)
                rsx = sb("p3_rsx", [128, TUM], F32, st)
                sg = [sb(f"p3_sg{i}", [128, CH], F32, st) for i in range(2)]
                Bt1, Bcv, Bsqc, Brsx = Buf(), Buf(), Buf(), Buf()
                Bsg = [Buf(), Buf()]
                tl = sb("p3_tl", [128, NRANK, 8, 2 * NLOC], BF16, st)
                mt = sb("p3_mt", [128, 8, 2], BF16, st)
                hal = sb("p3_hal", [128, 8, 2], F32, st)
                Btl, Bhal = Buf(), Buf()
                if last:
                    otile = [sb(f"p3_ot{i}", [128, D], F32, st) for i in range(2)]
                    Bot = [Buf(), Buf()]
                SPQ.dma(tl[:], ut_out.ap().rearrange("(r g p) c -> p r g c", r=NRANK, g=8), reads=[B["ut_out"]],
                        writes=[Btl])
                SPQ.dma(mt[:], uT_d.ap()[:, NMETA - 2:NMETA].rearrange("(g p) c -> p g c", p=128), reads=[B["uT"]],
                        writes=[Btl])
                cwb = 112 + l * 24
                oti = [0]
                sgi = [0]
                for u in range(NLOC):
                    t0, segs, TU = unit_info(u)
                    SPQ.dma(hT1[:, :, :TU], hT_d.ap()[:, t0:t0 + TU].rearrange("(k p) t -> p k t", p=128),
                            reads=[B["hT"]], writes=Bh)
                    SPQ.dma(yTs[:, 0:8, :TU], yT_d.ap()[:, t0:t0 + TU].rearrange("(k p) t -> p k t", p=128),
                            reads=[B["yT"]], writes=[By])
                    SPQ.dma(gb[:, :, :TU], gbT_d.ap()[:, t0:t0 + TU].rearrange("(k p) t -> p k t", p=128),
                            reads=[B["gbT"]], writes=[Bgb])
                    def load_wo(i):
                        SPQ.dma(wO[i % 2][:], wob[l].ap()[2 * i:2 * i + 2].rearrange("b p k j -> p b k j"),
                                reads=[Bw[("wob", l)]], writes=[BwO[i % 2]])

                    load_wo(0)
                    def load_gu(f):
                        SPQ.dma(wA[f % 2][:], wgb[l].ap()[f], reads=[Bw[("wgb", l)]], writes=[BwA[f % 2]])
                        SPQ.dma(wB_[f % 2][:], wub[l].ap()[f], reads=[Bw[("wub", l)]], writes=[BwB[f % 2]])

                    load_gu(0)
                    def load_d(m):
                        SPQ.dma(wD[m % 2][:], wdb[l].ap()[m], reads=[Bw[("wdb", l)]], writes=[BwD[m % 2]])

                    load_d(0)
                    for (so, n) in segs:
                        SPQ.dma(uh[:, :, 2:2 + n], uT_d.ap()[:, t0 + so:t0 + so + n].rearrange("(k p) t -> p k t", p=128),
                                reads=[B["uT"]], writes=[Buh])
                        if n == NMETA:
                            op(DVE, lambda: nc.vector.memset(uh[:, :, 0:2], 0.0), reads=[], writes=[Buh])
                        else:
                            lam = u
                            for r4 in range(4):
                                j = chunk_global(r4, lam)
                                if j == 0:
                                    cand = mt[:]
                                else:
                                    rho, lp = chunk_owner(j - 1)
                                    cand = tl[:, rho, :, 2 * lp:2 * lp + 2]
                                if r4 == 0:
                                    op(DVE, lambda: nc.vector.tensor_scalar(hal[:], cand, oh[:, 0:1], None, ALU.mult),
                                       reads=[Btl, Bc], writes=[Bhal])
                                else:
                                    op(DVE, lambda: nc.vector.scalar_tensor_tensor(hal[:], cand, oh[:, r4:r4 + 1], hal[:],
                                                                                   ALU.mult, ALU.add),
                                       reads=[Btl, Bc, Bhal], writes=[Bhal])
                            op(DVE, lambda: nc.vector.tensor_copy(uh[:, :, 0:2], hal[:]), reads=[Bhal], writes=[Buh])
                        for g in range(8):
                            op(DVE, lambda: nc.vector.tensor_scalar(t1[:, :n], uh[:, g, 0:n], vcol(cwb + g), None, ALU.mult),
                               reads=[Buh, Bc], writes=[Bt1])
                            op(DVE, lambda: nc.vector.scalar_tensor_tensor(t1[:, :n], uh[:, g, 1:n + 1], vcol(cwb + 8 + g),
                                                                           t1[:, :n], ALU.mult, ALU.add),
                               reads=[Buh, Bc, Bt1], writes=[Bt1])
                            op(DVE, lambda: nc.vector.scalar_tensor_tensor(t1[:, :n], uh[:, g, 2:n + 2], vcol(cwb + 16 + g),
                                                                           t1[:, :n], ALU.mult, ALU.add),
                               reads=[Buh, Bc, Bt1], writes=[Bt1])
                            op(DVE, lambda: nc.vector.tensor_tensor(cv[:, :n], t1[:, :n], gb[:, g, so:so + n], ALU.mult),
                               reads=[Bt1, Bgb], writes=[Bcv])
                            op(ACT, lambda: nc.scalar.activation(sqc[:, :n], cv[:, :n], AF.Square),
                               reads=[Bcv], writes=[Bsqc])
                            op(PE, lambda: nc.tensor.matmul(pbank[6][:, :n], lhsT=onesG[:], rhs=sqc[:, :n],
                                                            start=True, stop=True),
                               reads=[Bsqc, Bc], writes=[Bp[6]])
                            rsqrt_act(rsx[:, :n], pbank[6][:, :n], EPS, [Bp[6]], [Brsx])
                            op(DVE, lambda: nc.vector.scalar_tensor_tensor(yTs[:, 8 + g, so:so + n], cv[:, :n],
                                                                           vcol(80 + l * 16 + 8 + g), rsx[:, :n],
                                                                           ALU.mult, ALU.mult),
                               reads=[Bcv, Brsx, Bc], writes=[By])
                    for i in range(8):
                        if i + 1 < 8:
                            load_wo(i + 1)
                        for b2 in range(2):
                            m = 2 * i + b2
                            for (so, n) in segs:
                                bk = next_acc()
                                for kc in range(KC):
                                    op(PE, lambda: nc.tensor.matmul(pbank[bk][:, :n], lhsT=wO[i % 2][:, b2, kc, :],
                                                                    rhs=yTs[:, kc, so:so + n],
                                                                    start=(kc == 0), stop=(kc == KC - 1)),
                                       reads=[BwO[i % 2], By], writes=[Bp[bk]], sig=(kc == KC - 1))
                                op(DVE, lambda: nc.vector.tensor_tensor(hT1[:, m, so:so + n], hT1[:, m, so:so + n],
                                                                        pbank[bk][:, :n], ALU.add),
                                   reads=[Bp[bk]], writes=[Bh[m]])

                    def rms_stats():
                        for kc in range(KC):
                            op(ACT, lambda: nc.scalar.activation(aT[:, kc, :TU], hT1[:, kc, :TU], AF.Square),
                               reads=[Bh[kc]], writes=[Ba[kc]])
                        for (so, n) in segs:
                            for kc in range(KC):
                                op(PE, lambda: nc.tensor.matmul(pbank[6][:, :n], lhsT=onesD[:], rhs=aT[:, kc, so:so + n],
                                                                start=(kc == 0), stop=(kc == KC - 1)),
                                   reads=[Ba[kc], Bc], writes=[Bp[6]], sig=(kc == KC - 1))
                            rsqrt_act(rsx[:, so:so + n], pbank[6][:, :n], EPS, [Bp[6]], [Brsx])

                    rms_stats()
                    for kc in range(KC):
                        op(DVE, lambda: nc.vector.scalar_tensor_tensor(yTs[:, kc, :TU], hT1[:, kc, :TU],
                                                                       vcol(32 + l * 16 + kc), rsx[:, :TU],
                                                                       ALU.mult, ALU.mult),
                           reads=[Bh[kc], Brsx, Bc], writes=[By])
                    for f in range(FC):
                        conv_slice(1)
                        if f + 1 < FC:
                            load_gu(f + 1)
                        for (so, n) in segs:
                            bg_, bu_ = next_acc(), next_acc()
                            for kc in range(KC):
                                op(PE, lambda: nc.tensor.matmul(pbank[bg_][:, :n], lhsT=wA[f % 2][:, kc, :],
                                                                rhs=yTs[:, kc, so:so + n], start=(kc == 0),
                                                                stop=(kc == KC - 1)),
                                   reads=[BwA[f % 2], By], writes=[Bp[bg_]], sig=(kc == KC - 1))
                            for kc in range(KC):
                                op(PE, lambda: nc.tensor.matmul(pbank[bu_][:, :n], lhsT=wB_[f % 2][:, kc, :],
                                                                rhs=yTs[:, kc, so:so + n], start=(kc == 0),
                                                                stop=(kc == KC - 1)),
                                   reads=[BwB[f % 2], By], writes=[Bp[bu_]], sig=(kc == KC - 1))
                            k_ = sgi[0] = (sgi[0] + 1) % 2
                            op(ACT, lambda: nc.scalar.activation(sg[k_][:, :n], pbank[bg_][:, :n], AF.Silu),
                               reads=[Bp[bg_]], writes=[Bsg[k_]])
                            op(DVE, lambda: nc.vector.tensor_tensor(aT[:, f, so:so + n], sg[k_][:, :n], pbank[bu_][:, :n],
                                                                    ALU.mult),
                               reads=[Bsg[k_], Bp[bu_]], writes=[Ba[f]])
                    for m in range(16):
                        if m + 1 < 16:
                            load_d(m + 1)
                        for (so, n) in segs:
                            bk = next_acc()
                            for f in range(FC):
                                op(PE, lambda: nc.tensor.matmul(pbank[bk][:, :n], lhsT=wD[m % 2][:, f, :],
                                                                rhs=aT[:, f, so:so + n], start=(f == 0), stop=(f == FC - 1)),
                                   reads=[BwD[m % 2], Ba[f]], writes=[Bp[bk]], sig=(f == FC - 1))
                            op(DVE, lambda: nc.vector.tensor_tensor(hT1[:, m, so:so + n], hT1[:, m, so:so + n],
                                                                    pbank[bk][:, :n], ALU.add),
                               reads=[Bp[bk]], writes=[Bh[m]])
                    if not last:
                        SPQ.dma(hT_d.ap()[:, t0:t0 + TU].rearrange("(k p) t -> p k t", p=128), hT1[:, :, :TU],
                                reads=Bh, writes=[B["hT"]])
                    else:
                        rms_stats()
                        for kc in range(KC):
                            op(DVE, lambda: nc.vector.scalar_tensor_tensor(hT1[:, kc, :TU], hT1[:, kc, :TU],
                                                                           vcol(64 + kc), rsx[:, :TU], ALU.mult, ALU.mult),
                               reads=[Brsx, Bc], writes=[Bh[kc]])
                        so = TU - CH
                        for s4 in range(4):
                            oi = oti[0] = (oti[0] + 1) % 2
                            for k4 in range(4):
                                bk = next_acc()
                                for kk in range(4):
                                    kc = 4 * k4 + kk
                                    op(PE, lambda: nc.tensor.transpose(pbank[bk][:, kk * 128:(kk + 1) * 128],
                                                                       hT1[:, kc, so + s4 * 128:so + (s4 + 1) * 128],
                                                                       identf[:]),
                                       reads=[Bh[kc], Bc], writes=[Bp[bk]], sig=(kk == 3))
                                evac(otile[oi][:, k4 * 512:(k4 + 1) * 512], pbank[bk][:, :], [Bp[bk]], [Bot[oi]])
                            r0 = u * CH + s4 * 128
                            SPQ.dma(out_d.ap()[r0:r0 + 128, :], otile[oi][:], reads=[Bot[oi]], writes=[B["out"]])

        for l in range(NLAYERS):
            phase1(l)
            barrier()
            phase2(l)
            if l == 0 and NLAYERS > 1:
                convert(1, defer=True)
            barrier()
            phase34(l, l == NLAYERS - 1)
            conv_slice(10 ** 6)
            barrier()
    return nc


_PROG_CACHE = {}


def _run(inputs, NLOC, NLAYERS):
    f32 = np.float32
    x = np.asarray(inputs["x"], f32)
    meta = np.ascontiguousarray(np.asarray(inputs["meta"], f32))
    norm_mix = np.asarray(inputs["norm_mix"], f32)
    b_f = np.asarray(inputs["b_f"], f32)
    conv_w = np.asarray(inputs["conv_w"], f32)
    out_gain = np.asarray(inputs["out_gain"], f32)
    norm_ffn = np.asarray(inputs["norm_ffn"], f32)
    final_norm = np.asarray(inputs["final_norm"], f32)
    wts = {k: np.ascontiguousarray(np.asarray(inputs[k], f32)) for k in ["w_in", "w_out", "w_gate", "w_up", "w_down"]}
    NG = 4 * NLOC
    NKT = 1 + 4 * NG
    vecs = np.zeros((128, NV), f32)
    for l in range(2):
        vecs[:, l * 16:(l + 1) * 16] = norm_mix[l].reshape(16, 128).T
        vecs[:, 32 + l * 16:32 + (l + 1) * 16] = norm_ffn[l].reshape(16, 128).T
        vecs[:, 80 + l * 16:80 + (l + 1) * 16] = out_gain[l].reshape(16, 128).T
        for k in range(3):
            vecs[:, 112 + l * 24 + k * 8:112 + l * 24 + (k + 1) * 8] = conv_w[l, k].reshape(8, 128).T
    vecs[:, 64:80] = final_norm.reshape(16, 128).T
    bf = np.ascontiguousarray(b_f.T)
    identf = np.eye(128, dtype=f32)
    identb = np.eye(128).astype(ml_dtypes.bfloat16)
    tri = np.triu(np.ones((128, 128))).astype(ml_dtypes.bfloat16)
    in_maps = []
    for c in range(8):
        b, r = divmod(c, 4)
        xl = np.concatenate([x[b, chunk_global(r, lam) * CH:(chunk_global(r, lam) + 1) * CH] for lam in range(NLOC)], 0)
        ohm = np.zeros((128, 4), f32)
        ohm[:, r] = 1.0
        penm = np.zeros((NLOC, NKT), f32)
        for lam in range(NLOC):
            j = chunk_global(r, lam)
            for kt in range(1, NKT):
                if (kt - 1) // 4 >= j:
                    penm[lam, kt] = -BIG
        penb = np.ascontiguousarray(np.broadcast_to(penm.reshape(1, -1), (128, NLOC * NKT)))
        m = {"x": np.ascontiguousarray(xl), "meta": meta, "vecs": vecs, "bf": bf, "identf": identf, "identb": identb,
             "tri": tri, "oh": ohm, "pen": penb}
        m.update(wts)
        in_maps.append(m)
    key = (NLOC, NLAYERS)
    if key not in _PROG_CACHE:
        _PROG_CACHE[key] = build(NLOC, NLAYERS)
    nc = _PROG_CACHE[key]
    res = run_bass_kernel_spmd(nc, in_maps, core_ids=list(range(8)))
    if DEBUG:
        _run.last = res.results
    out = np.zeros((2, NG * CH, D), f32)
    for c in range(8):
        b, r = divmod(c, 4)
        o = np.asarray(res.results[c]["out"])
        for lam in range(NLOC):
            j = chunk_global(r, lam)
            out[b, j * CH:(j + 1) * CH] = o[lam * CH:(lam + 1) * CH]
    return out


def kernel(**inputs):
    return _run(inputs, 8, 2)
```

```python
from contextlib import ExitStack
import numpy as np
import ml_dtypes
import concourse.bass as bass
import concourse.mybir as mybir
from concourse.bass_utils import run_bass_kernel_spmd

F32, BF16 = mybir.dt.float32, mybir.dt.bfloat16
AF = mybir.ActivationFunctionType
ALU = mybir.AluOpType

D = 2048
KC = 16
NH = 8
DFF = 5632
FC = 44
DIN = 6152
NMETA = 16
CH = 512
EPS = 1e-6
SCALE = 128 ** -0.5
NRANK = 4


class Sem:
    def __init__(self, h):
        self.h = h
        self.val = 0


class Buf:
    __slots__ = ("w", "r")

    def __init__(self):
        self.w = {}
        self.r = {}


class Eng:
    def __init__(self, name, eng, sem):
        self.name, self.eng, self.sem, self.cnt, self.seen = name, eng, sem, 0, {}
        self.is_pe = name == "pe"

    def wait_t(self, s, v):
        if v <= 0:
            return
        if s is self.sem:
            if self.is_pe or v > self.cnt:
                return
        if self.seen.get(s, 0) >= v:
            return
        self.eng.wait_ge(s.h, v)
        self.seen[s] = v

    def wait_bufs(self, reads, writes):
        for b in reads:
            for s, v in b.w.items():
                self.wait_t(s, v)
        for b in writes:
            for s, v in b.w.items():
                self.wait_t(s, v)
            for s, v in b.r.items():
                self.wait_t(s, v)


def _commit(s, t, reads, writes):
    for b in writes:
        if b.w.get(s, 0) < t:
            b.w[s] = t
    for b in reads:
        if b.r.get(s, 0) < t:
            b.r[s] = t


def op(E, fn, reads=(), writes=(), sig=True):
    E.wait_bufs(reads, writes)
    ins = fn()
    if sig:
        ins.then_inc(E.sem.h, 1)
        E.cnt += 1
        t = E.cnt
    else:
        t = E.cnt + 1
    _commit(E.sem, t, reads, writes)


class DmaQ:
    def __init__(self, E, sems):
        self.E, self.sems, self.i = E, sems, 0

    def dma(self, out, in_, reads=(), writes=()):
        s = self.sems[self.i]
        self.i = (self.i + 1) % len(self.sems)
        self.E.wait_t(s, s.val)
        self.E.wait_bufs(reads, writes)
        self.E.eng.dma_start(out=out, in_=in_).then_inc(s.h, 16)
        s.val += 16
        _commit(s, s.val, reads, writes)


def chunk_owner(j):
    i, pos = divmod(j, 8)
    if pos < 4:
        return pos, 2 * i
    return 7 - pos, 2 * i + 1


def chunk_global(r, lam):
    return 8 * (lam // 2) + (r if lam % 2 == 0 else 7 - r)


W_IN_COLS = ([128 * i for i in range(8)] + [1024 + 128 * i for i in range(8)]
             + [2048 + 128 * i for i in range(8)] + [3072 + 128 * i for i in range(8)])
for _g in range(8):
    W_IN_COLS += [4096 + 128 * _g, 5120 + 128 * _g]
NV = 160
BIG = 1.0e5 / SCALE


DEBUG = False


def build(NLOC=8, NLAYERS=2):
    nc = bass.Bass("TRN2", target_bir_lowering=False)
    TL = NMETA + NLOC * CH
    NG = 4 * NLOC
    NTOK = NMETA + NG * CH
    NKT = 1 + 4 * NG
    NSUB = 1 + 4 * NLOC
    NXL = NLOC * CH

    def din(name, shape, dt=F32):
        return nc.dram_tensor(name, shape, dt, kind="ExternalInput")

    x_in = din("x", [NXL, D])
    meta_in = din("meta", [NMETA, D])
    w_in = din("w_in", [2, D, DIN])
    w_out = din("w_out", [2, D, D])
    w_gate = din("w_gate", [2, D, DFF])
    w_up = din("w_up", [2, D, DFF])
    w_down = din("w_down", [2, DFF, D])
    vecs_in = din("vecs", [128, NV])
    bf_in = din("bf", [8, 2])
    identf_in = din("identf", [128, 128])
    identb_in = din("identb", [128, 128], BF16)
    tri_in = din("tri", [128, 128], BF16)
    oh_in = din("oh", [128, 4])
    pen_in = din("pen", [128, NLOC * NKT])
    out_d = nc.dram_tensor("out", [NXL, D], F32, kind="ExternalOutput")

    def dscr(name, shape, dt):
        if DEBUG and not name.startswith("w") and not name.endswith("_in") and not name.endswith("_out"):
            return nc.dram_tensor(name, shape, dt, kind="ExternalOutput")
        return nc.dram_tensor(name, shape, dt)

    wib = [dscr(f"wib{l}", [48, 128, KC, 128], BF16) for l in range(2)]
    wfb = [dscr(f"wfb{l}", [128, KC, 8], BF16) for l in range(2)]
    wob = [dscr(f"wob{l}", [16, 128, KC, 128], BF16) for l in range(2)]
    wgb = [dscr(f"wgb{l}", [FC, 128, KC, 128], BF16) for l in range(2)]
    wub = [dscr(f"wub{l}", [FC, 128, KC, 128], BF16) for l in range(2)]
    wdb = [dscr(f"wdb{l}", [16, 128, FC, 128], BF16) for l in range(2)]
    hT_d = dscr("hT", [D, TL], F32)
    qT_d = dscr("qT", [1024, TL], BF16)
    kmeta_d = dscr("kmeta", [1024, NMETA], BF16)
    vmeta_d = dscr("vmeta", [1024, NMETA], BF16)
    kg_in = [dscr(f"kg{i}_in", [1024, CH], BF16) for i in range(NLOC)]
    vg_in = [dscr(f"vg{i}_in", [1024, CH], BF16) for i in range(NLOC)]
    kg_out = [dscr(f"kg{i}_out", [NRANK * 1024, CH], BF16) for i in range(NLOC)]
    vg_out = [dscr(f"vg{i}_out", [NRANK * 1024, CH], BF16) for i in range(NLOC)]
    gbT_d = dscr("gbT", [1024, TL], BF16)
    uT_d = dscr("uT", [1024, TL], BF16)
    ut_in = dscr("ut_in", [1024, 2 * NLOC], BF16)
    ut_out = dscr("ut_out", [NRANK * 1024, 2 * NLOC], BF16)
    lf_in = dscr("lf_in", [8, NXL], F32)
    lf_out = dscr("lf_out", [NRANK * 8, NXL], F32)
    lfm_d = dscr("lfm", [8, NMETA], F32)
    yT_d = dscr("yT", [1024, TL], BF16)

    B = {k: Buf() for k in ["hT", "qT", "kmeta", "vmeta", "kg_in", "vg_in", "kg_out", "vg_out", "gbT", "uT",
                            "ut_in", "ut_out", "lf_in", "lf_out", "lfm", "yT", "out", "const"]}
    for i_ in range(NLOC):
        for n_ in ["kg_in", "vg_in", "kg_out", "vg_out"]:
            B[(n_, i_)] = Buf()
    Bw = {(n, l): Buf() for n in ["wib", "wfb", "wob", "wgb", "wub", "wdb"] for l in range(2)}

    es = ExitStack()
    with es:
        def sem(name):
            return Sem(es.enter_context(nc.semaphore(name)))

        PE = Eng("pe", nc.tensor, sem("s_pe"))
        ACT = Eng("act", nc.scalar, sem("s_act"))
        DVE = Eng("dve", nc.vector, sem("s_dve"))
        POOL = Eng("pool", nc.gpsimd, sem("s_pool"))
        SP = Eng("sp", nc.sync, sem("s_sp"))
        SPQ = DmaQ(SP, [sem(f"dq{i}") for i in range(24)])
        PQ = DmaQ(POOL, [sem(f"pq{i}") for i in range(16)])
        conv_jobs = []
        ccsems = [sem(f"cc{i}") for i in range(8)]
        cc_i = [0]

        sb_n = [0]

        def sb(name, shape, dt, stack=None):
            sb_n[0] += 1
            return (stack or es).enter_context(nc.sbuf_tensor(f"sb{sb_n[0]}_{name}", shape, dt))

        pbank = [es.enter_context(nc.psum_tensor(f"pb{i}", [128, 512], F32)) for i in range(8)]
        Bp = [Buf() for _ in range(8)]

        identf = sb("identf", [128, 128], F32)
        identb = sb("identb", [128, 128], BF16)
        tri = sb("tri", [128, 128], BF16)
        vecs = sb("vecs", [128, NV], F32)
        bfv = sb("bfv", [8, 2], F32)
        nbf = sb("nbf", [8, 2], F32)
        oh = sb("oh", [128, 4], F32)
        pen = sb("pen", [128, NLOC, NKT], F32)
        onesD = sb("onesD", [128, 128], BF16)
        onesG = sb("onesG", [128, 128], BF16)
        ones1 = sb("ones1", [128, 128], BF16)
        ones8f = sb("ones8f", [8, 128], F32)
        zerosf = sb("zerosf", [128, 512], F32)
        Bc = B["const"]
        SPQ.dma(identf[:], identf_in.ap(), writes=[Bc])
        SPQ.dma(identb[:], identb_in.ap(), writes=[Bc])
        SPQ.dma(tri[:], tri_in.ap(), writes=[Bc])
        SPQ.dma(vecs[:], vecs_in.ap(), writes=[Bc])
        SPQ.dma(bfv[:], bf_in.ap(), writes=[Bc])
        SPQ.dma(oh[:], oh_in.ap(), writes=[Bc])
        SPQ.dma(pen[:], pen_in.ap().rearrange("p (a b) -> p a b", a=NLOC), writes=[Bc])
        op(DVE, lambda: nc.vector.memset(onesD[:], 1.0 / D), writes=[Bc])
        op(DVE, lambda: nc.vector.memset(onesG[:], 1.0 / 128), writes=[Bc])
        op(DVE, lambda: nc.vector.memset(ones1[:], 1.0), writes=[Bc])
        op(DVE, lambda: nc.vector.memset(ones8f[:], 1.0), writes=[Bc])
        op(DVE, lambda: nc.vector.memset(zerosf[:], 0.0), writes=[Bc])
        op(DVE, lambda: nc.vector.tensor_scalar(nbf[:], bfv[:], -1.0, None, ALU.mult), reads=[Bc], writes=[Bc])

        def vcol(c):
            return vecs[:, c:c + 1]

        def convert(l, defer=False):
            jobs = []

            class _Q:
                @staticmethod
                def dma(out, in_, writes):
                    jobs.append((out, in_, writes))
            PQ_ = _Q
            convert_body(l, PQ_)
            if defer:
                conv_jobs.extend(jobs)
            else:
                for (o_, i_, w_) in jobs:
                    PQ.dma(o_, i_, writes=w_)

        def conv_slice(n):
            for _ in range(min(n, len(conv_jobs))):
                o_, i_, w_ = conv_jobs.pop(0)
                PQ.dma(o_, i_, writes=w_)

        def convert_body(l, PQ):
            for bi, c0 in enumerate(W_IN_COLS):
                PQ.dma(wib[l].ap()[bi], w_in.ap()[l, :, c0:c0 + 128].rearrange("(k p) j -> p k j", p=128),
                       writes=[Bw[("wib", l)]])
            PQ.dma(wfb[l].ap(), w_in.ap()[l, :, 6144:6152].rearrange("(k p) j -> p k j", p=128),
                   writes=[Bw[("wfb", l)]])
            for m in range(16):
                PQ.dma(wob[l].ap()[m], w_out.ap()[l, :, m * 128:(m + 1) * 128].rearrange("(k p) j -> p k j", p=128),
                       writes=[Bw[("wob", l)]])
            for f in range(FC):
                PQ.dma(wgb[l].ap()[f], w_gate.ap()[l, :, f * 128:(f + 1) * 128].rearrange("(k p) j -> p k j", p=128),
                       writes=[Bw[("wgb", l)]])
                PQ.dma(wub[l].ap()[f], w_up.ap()[l, :, f * 128:(f + 1) * 128].rearrange("(k p) j -> p k j", p=128),
                       writes=[Bw[("wub", l)]])
            for m in range(16):
                for half in range(2):
                    PQ.dma(wdb[l].ap()[m, :, half * 22:(half + 1) * 22, :],
                           w_down.ap()[l, half * 2816:(half + 1) * 2816, m * 128:(m + 1) * 128].rearrange(
                               "(k p) j -> p k j", p=128),
                           writes=[Bw[("wdb", l)]])

        convert(0)

        ev_i = [0]

        def evac(out, in_, reads, writes):
            ev_i[0] ^= 1
            if ev_i[0]:
                op(ACT, lambda: nc.scalar.copy(out, in_), reads, writes)
            else:
                op(DVE, lambda: nc.vector.tensor_copy(out, in_), reads, writes)

        def rsqrt_act(out, in_, eps, reads, writes):
            op(ACT, lambda: nc.scalar.activation(out, in_, AF.Ln, bias=eps, scale=1.0), reads, writes)
            op(ACT, lambda: nc.scalar.activation(out, out, AF.Exp, scale=-0.5), writes, writes)

        acc_i = [0]

        def next_acc(banks=(0, 1, 2, 3, 4, 5)):
            acc_i[0] = (acc_i[0] + 1) % len(banks)
            return banks[acc_i[0]]

        def unit_info(u):
            if u == 0:
                return 0, [(0, NMETA), (NMETA, CH)], NMETA + CH
            return NMETA + u * CH, [(0, CH)], CH

        def allgather(src, dst, bsrc, bdst):
            s = ccsems[cc_i[0] % len(ccsems)]
            cc_i[0] += 1
            POOL.wait_t(s, s.val)
            POOL.wait_bufs([bsrc], [bdst])
            nc.gpsimd.collective_compute("AllGather", ALU.bypass, replica_groups=[[0, 1, 2, 3], [4, 5, 6, 7]],
                                         ins=[src.ap().opt()], outs=[dst.ap().opt()]).then_inc(s.h, 1)
            s.val += 1
            _commit(s, s.val, [bsrc], [bdst])

        TUM = NMETA + CH

        def phase1(l):
            with ExitStack() as st:
                hT = sb("p1_hT", [128, KC, TUM], F32, st)
                Bh = [Buf() for _ in range(KC)]
                sq = sb("p1_sq", [128, KC, TUM], BF16, st)
                Bsq = [Buf() for _ in range(KC)]
                hns = [sb(f"p1_hn{i}", [128, KC, TUM], BF16, st) for i in range(2)]
                Bhns = [Buf(), Buf()]
                rstd = sb("p1_rstd", [128, TUM], F32, st)
                Brs = Buf()
                wt = [sb(f"p1_wt{i}", [128, 4, KC, 128], BF16, st) for i in range(2)]
                Bwt = [Buf(), Buf()]
                wf = sb("p1_wf", [128, KC, 8], BF16, st)
                Bwf = Buf()
                ost = [sb(f"p1_ost{i}", [128, TUM], BF16, st) for i in range(3)]
                Bost = [Buf() for _ in range(3)]
                gcs = sb("p1_gcs", [128, TUM], BF16, st)
                Bgcs = Buf()
                lfe = sb("p1_lfe", [8, TUM], F32, st)
                lfs = sb("p1_lfs", [8, TUM], F32, st)
                Blf = Buf()
                if l == 0:
                    xt = sb("p1_xt", [128, 4, D], F32, st)
                    Bxt = [Buf() for _ in range(4)]
                    xm = sb("p1_xm", [NMETA, D], F32, st)
                    Bxm = Buf()
                SPQ.dma(wf[:], wfb[l].ap(), reads=[Bw[("wfb", l)]], writes=[Bwf])
                ost_i = [0]
                def prologue(u):
                    t0, segs, TU = unit_info(u)
                    hn, Bhn = hns[u % 2], Bhns[u % 2]
                    if l == 0:
                        for (so, n) in segs:
                            if n == NMETA:
                                SPQ.dma(xm[:], meta_in.ap(), writes=[Bxm])
                                for kc in range(KC):
                                    op(PE, lambda: nc.tensor.transpose(pbank[6][:, kc * 16:(kc + 1) * 16],
                                                                       xm[:, kc * 128:(kc + 1) * 128],
                                                                       identf[:NMETA, :NMETA]),
                                       reads=[Bxm, Bc], writes=[Bp[6]], sig=(kc == KC - 1))
                                for kc in range(KC):
                                    evac(hT[:, kc, so:so + n], pbank[6][:, kc * 16:(kc + 1) * 16], [Bp[6]], [Bh[kc]])
                            else:
                                for s4 in range(4):
                                    r0 = u * CH + s4 * 128
                                    SPQ.dma(xt[:, s4, :], x_in.ap()[r0:r0 + 128, :], writes=[Bxt[s4]])
                                for kc in range(KC):
                                    bk = next_acc()
                                    for s4 in range(4):
                                        op(PE, lambda: nc.tensor.transpose(pbank[bk][:, s4 * 128:(s4 + 1) * 128],
                                                                           xt[:, s4, kc * 128:(kc + 1) * 128],
                                                                           identf[:]),
                                           reads=[Bxt[s4], Bc], writes=[Bp[bk]], sig=(s4 == 3))
                                    evac(hT[:, kc, so:so + n], pbank[bk][:, :n], [Bp[bk]], [Bh[kc]])
                        SPQ.dma(hT_d.ap()[:, t0:t0 + TU].rearrange("(k p) t -> p k t", p=128), hT[:, :, :TU],
                                reads=Bh, writes=[B["hT"]])
                    else:
                        SPQ.dma(hT[:, :, :TU], hT_d.ap()[:, t0:t0 + TU].rearrange("(k p) t -> p k t", p=128),
                                reads=[B["hT"]], writes=Bh)
                    for kc in range(KC):
                        op(ACT, lambda: nc.scalar.activation(sq[:, kc, :TU], hT[:, kc, :TU], AF.Square),
                           reads=[Bh[kc]], writes=[Bsq[kc]])
                    for (so, n) in segs:
                        for kc in range(KC):
                            op(PE, lambda: nc.tensor.matmul(pbank[6][:, :n], lhsT=onesD[:], rhs=sq[:, kc, so:so + n],
                                                            start=(kc == 0), stop=(kc == KC - 1)),
                               reads=[Bsq[kc], Bc], writes=[Bp[6]], sig=(kc == KC - 1))
                        rsqrt_act(rstd[:, so:so + n], pbank[6][:, :n], EPS, [Bp[6]], [Brs])
                    for kc in range(KC):
                        op(DVE, lambda: nc.vector.scalar_tensor_tensor(hn[:, kc, :TU], hT[:, kc, :TU],
                                                                       vcol(l * 16 + kc), rstd[:, :TU],
                                                                       ALU.mult, ALU.mult),
                           reads=[Bh[kc], Brs, Bc], writes=[Bhn])
                def proj(u):
                    t0, segs, TU = unit_info(u)
                    hn, Bhn = hns[u % 2], Bhns[u % 2]
                    def load_w(bg):
                        SPQ.dma(wt[bg % 2][:], wib[l].ap()[4 * bg:4 * bg + 4].rearrange("b p k j -> p b k j"),
                                reads=[Bw[("wib", l)]], writes=[Bwt[bg % 2]])

                    if u == 0:
                        load_w(0)
                    for bg in range(12):
                        if bg + 1 < 12:
                            load_w(bg + 1)
                        elif u + 1 < NLOC:
                            load_w(0)
                        for b4 in range(4):
                            blk = 4 * bg + b4
                            kind = blk // 8 if blk < 32 else (4 if blk % 2 == 0 else 5)
                            if kind == 4:
                                dst, Bdst = gcs, Bgcs
                            else:
                                oi = ost_i[0] = (ost_i[0] + 1) % 3
                                dst, Bdst = ost[oi], Bost[oi]
                            for (so, n) in segs:
                                bk = next_acc()
                                for kc in range(KC):
                                    op(PE, lambda: nc.tensor.matmul(pbank[bk][:, :n], lhsT=wt[bg % 2][:, b4, kc, :],
                                                                    rhs=hn[:, kc, so:so + n],
                                                                    start=(kc == 0), stop=(kc == KC - 1)),
                                       reads=[Bwt[bg % 2], Bhn], writes=[Bp[bk]], sig=(kc == KC - 1))
                                if kind == 5:
                                    op(DVE, lambda: nc.vector.tensor_tensor(dst[:, so:so + n], pbank[bk][:, :n],
                                                                            gcs[:, so:so + n], ALU.mult),
                                       reads=[Bp[bk], Bgcs], writes=[Bdst])
                                else:
                                    evac(dst[:, so:so + n], pbank[bk][:, :n], [Bp[bk]], [Bdst])
                            if kind == 4:
                                continue
                            xo = u * CH
                            if kind == 0:
                                SPQ.dma(qT_d.ap()[blk * 128:(blk + 1) * 128, t0:t0 + TU], dst[:, :TU],
                                        reads=[Bdst], writes=[B["qT"]])
                            elif kind in (1, 2):
                                hb = blk - 8 * kind
                                md, gd, bm, bgn = ((kmeta_d, kg_in, "kmeta", "kg_in") if kind == 1
                                                   else (vmeta_d, vg_in, "vmeta", "vg_in"))
                                if u == 0:
                                    SPQ.dma(md.ap()[hb * 128:(hb + 1) * 128, :], dst[:, :NMETA],
                                            reads=[Bdst], writes=[B[bm]])
                                SPQ.dma(gd[u].ap()[hb * 128:(hb + 1) * 128, :], dst[:, TU - CH:TU],
                                        reads=[Bdst], writes=[B[(bgn, u)]])
                            elif kind == 3:
                                g = blk - 24
                                SPQ.dma(gbT_d.ap()[g * 128:(g + 1) * 128, t0:t0 + TU], dst[:, :TU],
                                        reads=[Bdst], writes=[B["gbT"]])
                            else:
                                g = (blk - 32) // 2
                                SPQ.dma(uT_d.ap()[g * 128:(g + 1) * 128, t0:t0 + TU], dst[:, :TU],
                                        reads=[Bdst], writes=[B["uT"]])
                                SPQ.dma(ut_in.ap()[g * 128:(g + 1) * 128, 2 * u:2 * u + 2], dst[:, TU - 2:TU],
                                        reads=[Bdst], writes=[B["ut_in"]])
                    for (so, n) in segs:
                        for kc in range(KC):
                            op(PE, lambda: nc.tensor.matmul(pbank[7][:8, :n], lhsT=wf[:, kc, :], rhs=hn[:, kc, so:so + n],
                                                            start=(kc == 0), stop=(kc == KC - 1)),
                               reads=[Bwf, Bhn], writes=[Bp[7]], sig=(kc == KC - 1))
                        op(ACT, lambda: nc.scalar.activation(lfe[:, so:so + n], pbank[7][:8, :n], AF.Exp,
                                                             bias=nbf[:, l:l + 1], scale=-1.0),
                           reads=[Bp[7], Bc], writes=[Blf])
                        op(ACT, lambda: nc.scalar.activation(lfe[:, so:so + n], lfe[:, so:so + n], AF.Ln,
                                                             bias=1.0, scale=1.0),
                           reads=[Blf], writes=[Blf])
                        op(DVE, lambda: nc.vector.tensor_scalar(lfs[:, so:so + n], lfe[:, so:so + n], -1.0, None,
                                                                ALU.mult),
                           reads=[Blf], writes=[Blf])
                    if u == 0:
                        SPQ.dma(lfm_d.ap(), lfs[:, :NMETA], reads=[Blf], writes=[B["lfm"]])
                    SPQ.dma(lf_in.ap()[:, u * CH:(u + 1) * CH], lfs[:, TU - CH:TU], reads=[Blf], writes=[B["lf_in"]])
                    allgather(kg_in[u], kg_out[u], B[("kg_in", u)], B[("kg_out", u)])
                    allgather(vg_in[u], vg_out[u], B[("vg_in", u)], B[("vg_out", u)])
                prologue(0)
                for u in range(NLOC):
                    if u + 1 < NLOC:
                        prologue(u + 1)
                    proj(u)
            allgather(lf_in, lf_out, B["lf_in"], B["lf_out"])
            allgather(ut_in, ut_out, B["ut_in"], B["ut_out"])


        def barrier():
            engs = [PE, ACT, DVE, POOL]
            dsems = SPQ.sems + PQ.sems + ccsems
            for E in engs + [SP]:
                for F_ in engs:
                    if F_ is not E:
                        E.wait_t(F_.sem, F_.cnt)
                for s in dsems:
                    E.wait_t(s, s.val)

        def gcols(jj):
            rho, lam = chunk_owner(jj)
            return rho, lam * CH

        def jmax(lam):
            return 8 * (lam // 2) + (3 if lam % 2 == 0 else 7)

        def cands(lam):
            return [chunk_global(r, lam) for r in range(4)]

        def phase2(l):
            with ExitStack() as st:
                CTn = sb("p2_CTn", [128, NKT, 8], F32, st)
                CTo = sb("p2_CTo", [128, 4 * NLOC, 8], F32, st)
                Rbc = sb("p2_Rbc", [128, 8 * NSUB], F32, st)
                Btab = Buf()
                with ExitStack() as st2:
                    lfF = sb("p2_lfF", [8, NTOK], F32, st2)
                    cF = sb("p2_cF", [8, NTOK], F32, st2)
                    cown = sb("p2_cown", [8, NXL], F32, st2)
                    Rm = sb("p2_Rm", [8, NSUB], F32, st2)
                    Dh = sb("p2_Dh", [8, NSUB], F32, st2)
                    Bl, Bcf, Bco, Brm, Bdh = Buf(), Buf(), Buf(), Buf(), Buf()
                    SPQ.dma(lfF[:, :NMETA], lfm_d.ap(), reads=[B["lfm"]], writes=[Bl])
                    for jj in range(NG):
                        rho, co = gcols(jj)
                        SPQ.dma(lfF[:, NMETA + jj * CH:NMETA + (jj + 1) * CH],
                                lf_out.ap()[rho * 8:(rho + 1) * 8, co:co + CH], reads=[B["lf_out"]], writes=[Bl])
                    pos = 0
                    while pos < NTOK:
                        n = min(2048, NTOK - pos)
                        init = 0.0 if pos == 0 else cF[:, pos - 1:pos]
                        op(DVE, lambda: nc.vector.tensor_tensor_scan(cF[:, pos:pos + n], lfF[:, pos:pos + n],
                                                                     lfF[:, pos:pos + n], init, ALU.add, ALU.min),
                           reads=[Bl, Bcf], writes=[Bcf])
                        pos += n
                    for lam in range(NLOC):
                        cs = cands(lam)
                        dstc = cown[:, lam * CH:(lam + 1) * CH]
                        for r4 in range(4):
                            src = cF[:, NMETA + cs[r4] * CH:NMETA + (cs[r4] + 1) * CH]
                            if r4 == 0:
                                op(DVE, lambda: nc.vector.tensor_scalar(dstc, src, oh[:8, 0:1], None, ALU.mult),
                                   reads=[Bcf, Bc], writes=[Bco])
                            else:
                                op(DVE, lambda: nc.vector.scalar_tensor_tensor(dstc, src, oh[:8, r4:r4 + 1], dstc,
                                                                               ALU.mult, ALU.add),
                                   reads=[Bcf, Bc, Bco], writes=[Bco])
                    op(DVE, lambda: nc.vector.tensor_tensor(Rm[:, 0:1], cF[:, 0:1], cF[:, NMETA - 1:NMETA], ALU.add),
                       reads=[Bcf], writes=[Brm])
                    for s in range(4 * NLOC):
                        a = s * 128
                        op(DVE, lambda: nc.vector.tensor_tensor(Rm[:, 1 + s:2 + s], cown[:, a:a + 1],
                                                                cown[:, a + 127:a + 128], ALU.add),
                           reads=[Bco], writes=[Brm])
                    for h in range(NH):
                        op(DVE, lambda: nc.vector.tensor_scalar(Dh[:], Rm[:], identf[:8, h:h + 1], None, ALU.mult),
                           reads=[Brm, Bc], writes=[Bdh])
                        op(PE, lambda: nc.tensor.matmul(pbank[6][:, h * NSUB:(h + 1) * NSUB], lhsT=ones8f[:], rhs=Dh[:],
                                                        start=True, stop=True),
                           reads=[Bdh, Bc], writes=[Bp[6]])
                    op(DVE, lambda: nc.vector.tensor_scalar(Rbc[:], pbank[6][:, :8 * NSUB], 0.5 / SCALE, None, ALU.mult),
                       reads=[Bp[6]], writes=[Btab])
                    for b0 in range(0, NKT, 64):
                        cnt = min(64, NKT - b0)
                        bk = next_acc()
                        for i in range(cnt):
                            kt = b0 + i
                            k0, nk = (0, NMETA) if kt == 0 else (NMETA + (kt - 1) * 128, 128)
                            op(PE, lambda: nc.tensor.transpose(pbank[bk][:nk, i * 8:(i + 1) * 8], cF[:, k0:k0 + nk],
                                                               identf[:8, :8]),
                               reads=[Bcf, Bc], writes=[Bp[bk]], sig=(i == cnt - 1))
                        op(DVE, lambda: nc.vector.tensor_scalar(
                            CTn[:, b0:b0 + cnt, :], pbank[bk][:, :cnt * 8].rearrange("p (a b) -> p a b", b=8),
                            -1.0 / SCALE, None, ALU.mult), reads=[Bp[bk]], writes=[Btab])
                    bk = next_acc()
                    for i in range(4 * NLOC):
                        op(PE, lambda: nc.tensor.transpose(pbank[bk][:, i * 8:(i + 1) * 8], cown[:, i * 128:(i + 1) * 128],
                                                           identf[:8, :8]),
                           reads=[Bco, Bc], writes=[Bp[bk]], sig=(i == 4 * NLOC - 1))
                    op(DVE, lambda: nc.vector.tensor_scalar(
                        CTo[:], pbank[bk][:, :4 * NLOC * 8].rearrange("p (a b) -> p a b", b=8),
                        -1.0 / SCALE, None, ALU.mult), reads=[Bp[bk]], writes=[Btab])
                    if DEBUG:
                        dC = nc.dram_tensor(f"dbg_CTn{l}", [128, NKT * 8], F32, kind="ExternalOutput")
                        dR = nc.dram_tensor(f"dbg_Rbc{l}", [128, 8 * NSUB], F32, kind="ExternalOutput")
                        dO = nc.dram_tensor(f"dbg_CTo{l}", [128, 4 * NLOC * 8], F32, kind="ExternalOutput")
                        dcF = nc.dram_tensor(f"dbg_cF{l}", [8, NTOK], F32, kind="ExternalOutput")
                        SPQ.dma(dC.ap(), CTn[:].rearrange("p a b -> p (a b)"), reads=[Btab], writes=[B["out"]])
                        SPQ.dma(dR.ap(), Rbc[:], reads=[Btab], writes=[B["out"]])
                        SPQ.dma(dO.ap(), CTo[:].rearrange("p a b -> p (a b)"), reads=[Btab], writes=[B["out"]])
                        SPQ.dma(dcF.ap(), cF[:], reads=[Bcf], writes=[B["out"]])
                    barrier()

                kT = [sb(f"p2_kT{i}", [128, NTOK], BF16, st) for i in range(2)]
                BkT = [Buf(), Buf()]
                kown = [sb(f"p2_ko{i}", [128, NXL], BF16, st) for i in range(2)]
                Bko = [Buf(), Buf()]
                vT = sb("p2_vT", [128, NTOK], BF16, st)
                BvT = Buf()
                vownT = sb("p2_voT", [128, NXL], BF16, st)
                BvoT = Buf()
                V = sb("p2_V", [128, NKT, 128], BF16, st)
                BV = Buf()
                Vo = sb("p2_Vo", [128, 4 * NLOC, 128], BF16, st)
                BVo = Buf()
                qs = [sb(f"p2_q{i}", [128, CH], BF16, st) for i in range(2)]
                Bq = [Buf(), Buf()]
                Rt = [sb(f"p2_Rt{i}", [128, CH], F32, st) for i in range(2)]
                BRt = [Buf(), Buf()]
                CTl = [sb(f"p2_CTl{i}", [128, NKT], F32, st) for i in range(2)]
                BCl = [Buf(), Buf()]
                Tb = [sb(f"p2_T{i}", [128, CH], F32, st) for i in range(4)]
                BT = [Buf() for _ in range(4)]
                Pb = [sb(f"p2_P{i}", [128, CH], BF16, st) for i in range(5)]
                BP = [Buf() for _ in range(5)]
                LA = 3
                pending = []
                Osb = sb("p2_Osb", [128, CH], F32, st)
                d2 = sb("p2_d2", [128, CH], F32, st)
                sqo = sb("p2_sqo", [128, CH], BF16, st)
                uu = sb("p2_uu", [128, CH], F32, st)
                yst = [sb(f"p2_y{i}", [128, CH], BF16, st) for i in range(2)]
                BOs, Bd2, Bsqo, Buu = Buf(), Buf(), Buf(), Buf()
                Byst = [Buf(), Buf()]
                TRB = [7, 6]
                pb67b = [pbank[7][:].bitcast(BF16), pbank[6][:].bitcast(BF16)]

                def head_jobs(h):
                    kb, Bk = kT[h % 2], BkT[h % 2]
                    hs = slice(h * 128, (h + 1) * 128)
                    jobs = []

                    def J(out, in_, r, w):
                        jobs.append(lambda: SPQ.dma(out, in_, reads=r, writes=w))
                    J(vT[:, :NMETA], vmeta_d.ap()[hs, :], [B["vmeta"]], [BvT])
                    for jj in range(NG):
                        rho, lp = chunk_owner(jj)
                        rs_ = slice(rho * 1024 + h * 128, rho * 1024 + (h + 1) * 128)
                        J(vT[:, NMETA + jj * CH:NMETA + (jj + 1) * CH], vg_out[lp].ap()[rs_, :],
                          [B[("vg_out", lp)]], [BvT])
                    for lam in range(NLOC):
                        J(vownT[:, lam * CH:(lam + 1) * CH], vg_in[lam].ap()[hs, :], [B[("vg_in", lam)]], [BvoT])
                    J(kb[:, :NMETA], kmeta_d.ap()[hs, :], [B["kmeta"]], [Bk])
                    for jj in range(NG):
                        rho, lp = chunk_owner(jj)
                        rs_ = slice(rho * 1024 + h * 128, rho * 1024 + (h + 1) * 128)
                        J(kb[:, NMETA + jj * CH:NMETA + (jj + 1) * CH], kg_out[lp].ap()[rs_, :],
                          [B[("kg_out", lp)]], [Bk])
                    for lam in range(NLOC):
                        J(kown[h % 2][:, lam * CH:(lam + 1) * CH], kg_in[lam].ap()[hs, :], [B[("kg_in", lam)]],
                          [Bko[h % 2]])
                    return jobs

                def load_head(h):
                    for j_ in head_jobs(h):
                        j_()

                seglist_all = []
                for u_ in range(NLOC):
                    if u_ == 0:
                        seglist_all.append((u_, "meta", 0, NMETA))
                    seglist_all.append((u_, "chunk", NMETA + u_ * CH, CH))
                nsegs = len(seglist_all)

                def load_q(si_, h_, t0_, nq_):
                    SPQ.dma(qs[si_ % 2][:, :nq_], qT_d.ap()[h_ * 128:(h_ + 1) * 128, t0_:t0_ + nq_], reads=[B["qT"]],
                            writes=[Bq[si_ % 2]])

                tcount = [0]
                segcount = [0]
                load_head(0)
                load_q(1, 0, seglist_all[0][2], seglist_all[0][3])
                for h in range(NH):
                    while pending:
                        pending.pop(0)()
                    ti = 0
                    for b0 in range(0, NKT, 8):
                        cnt = min(8, NKT - b0)
                        pi = ti % 2
                        ti += 1
                        for i in range(cnt):
                            kt = b0 + i
                            k0, nk = (0, NMETA) if kt == 0 else (NMETA + (kt - 1) * 128, 128)
                            op(PE, lambda: nc.tensor.transpose(pb67b[pi][:nk, i * 128:(i + 1) * 128],
                                                               vT[:, k0:k0 + nk], identb[:]),
                               reads=[BvT, Bc], writes=[Bp[TRB[pi]]], sig=(i == cnt - 1))
                        evac(V[:, b0:b0 + cnt, :], pb67b[pi][:, :cnt * 128].rearrange("p (a d) -> p a d", d=128),
                             [Bp[TRB[pi]]], [BV])
                    for b0 in range(0, 4 * NLOC, 8):
                        cnt = min(8, 4 * NLOC - b0)
                        pi = ti % 2
                        ti += 1
                        for i in range(cnt):
                            op(PE, lambda: nc.tensor.transpose(pb67b[pi][:, i * 128:(i + 1) * 128],
                                                               vownT[:, (b0 + i) * 128:(b0 + i + 1) * 128], identb[:]),
                               reads=[BvoT, Bc], writes=[Bp[TRB[pi]]], sig=(i == cnt - 1))
                        evac(Vo[:, b0:b0 + cnt, :], pb67b[pi][:, :cnt * 128].rearrange("p (a d) -> p a d", d=128),
                             [Bp[TRB[pi]]], [BVo])
                    nxt_jobs = head_jobs(h + 1) if h + 1 < NH else []
                    per_seg = -(-len(nxt_jobs) // nsegs)
                    kb, Bk = kT[h % 2], BkT[h % 2]
                    ko, Bkow = kown[h % 2], Bko[h % 2]
                    for sidx, (u, skind, t0, nq) in enumerate(seglist_all):
                        if True:
                            si = segcount[0] = segcount[0] + 1
                            for _ in range(min(per_seg, len(nxt_jobs))):
                                nxt_jobs.pop(0)()
                            if sidx + 1 < nsegs:
                                load_q(si + 1, h, seglist_all[sidx + 1][2], seglist_all[sidx + 1][3])
                            elif h + 1 < NH:
                                load_q(si + 1, h + 1, seglist_all[0][2], seglist_all[0][3])
                            q, Bqq = qs[si % 2], Bq[si % 2]
                            rt, Brt = Rt[si % 2], BRt[si % 2]
                            ctl, Bcl = CTl[si % 2], BCl[si % 2]
                            ob, db = 4, 5
                            tiles = []
                            if skind == "meta":
                                op(DVE, lambda: nc.vector.tensor_scalar(rt[:, :nq], zerosf[:, :nq],
                                                                        Rbc[:, h * NSUB:h * NSUB + 1], None, ALU.add),
                                   reads=[Btab, Bc], writes=[Brt])
                                tiles.append((kb[:, 0:NMETA], V[:NMETA, 0, :], CTn[:NMETA, 0, h:h + 1], NMETA, 0,
                                              [Bk], [BV], [Btab]))
                            else:
                                lam = u
                                for sj in range(4):
                                    s = 1 + 4 * lam + sj
                                    op(DVE, lambda: nc.vector.tensor_scalar(
                                        rt[:, sj * 128:(sj + 1) * 128], zerosf[:, :128],
                                        Rbc[:, h * NSUB + s:h * NSUB + s + 1], None, ALU.add),
                                       reads=[Btab, Bc], writes=[Brt])
                                npast = 1 + 4 * jmax(lam)
                                op(DVE, lambda: nc.vector.tensor_tensor(ctl[:, :npast], CTn[:, :npast, h],
                                                                        pen[:, lam, :npast], ALU.add),
                                   reads=[Btab, Bc], writes=[Bcl])
                                tiles.append((kb[:, 0:NMETA], V[:NMETA, 0, :], ctl[:NMETA, 0:1], NMETA, None,
                                              [Bk], [BV], [Bcl]))
                                for kt in range(1, npast):
                                    k0 = NMETA + (kt - 1) * 128
                                    tiles.append((kb[:, k0:k0 + 128], V[:, kt, :], ctl[:, kt:kt + 1], 128, None,
                                                  [Bk], [BV], [Bcl]))
                                for i in range(4):
                                    k0 = lam * CH + i * 128
                                    tiles.append((ko[:, k0:k0 + 128], Vo[:, 4 * lam + i, :],
                                                  CTo[:, 4 * lam + i, h:h + 1], 128, i, [Bkow], [BVo], [Btab]))
                            nt = len(tiles)

                            def emit_S(i):
                                kap, vap, bcol, nk, dg, rk, rv, rb = tiles[i]
                                sbk = (tcount[0] + i) % 4
                                op(PE, lambda: nc.tensor.matmul(pbank[sbk][:nk, :nq], lhsT=kap, rhs=q[:, :nq],
                                                                start=True, stop=True),
                                   reads=rk + [Bqq], writes=[Bp[sbk]])

                            for i in range(min(LA, nt)):
                                emit_S(i)
                            for i in range(nt):
                                if i + LA < nt:
                                    emit_S(i + LA)
                                kap, vap, bcol, nk, dg, rk, rv, rb = tiles[i]
                                g = tcount[0] + i
                                sbk = g % 4
                                T_, BT_ = Tb[g % 4], BT[g % 4]
                                P_, BP_ = Pb[g % 5], BP[g % 5]
                                if i == min(4, nt - 1) and pending:
                                    pending.pop(0)()
                                c0 = 0 if (dg is None or skind == "meta") else dg * 128
                                op(DVE, lambda: nc.vector.scalar_tensor_tensor(T_[:nk, c0:nq], pbank[sbk][:nk, c0:nq],
                                                                               bcol, rt[:nk, c0:nq], ALU.add, ALU.add),
                                   reads=[Bp[sbk], Brt] + rb, writes=[BT_])
                                op(ACT, lambda: nc.scalar.activation(P_[:nk, c0:nq], T_[:nk, c0:nq], AF.Exp, scale=SCALE),
                                   reads=[BT_], writes=[BP_])
                                if dg is not None:
                                    if c0 > 0:
                                        op(POOL, lambda: nc.gpsimd.memset(P_[:nk, :c0], 0.0), reads=[], writes=[BP_])
                                    w = min(128, nq)
                                    op(POOL, lambda: nc.gpsimd.tensor_tensor(P_[:nk, c0:c0 + w], P_[:nk, c0:c0 + w],
                                                                             tri[:nk, :w], ALU.mult),
                                       reads=[Bc], writes=[BP_])
                                op(PE, lambda: nc.tensor.matmul(pbank[ob][:, :nq], lhsT=vap, rhs=P_[:nk, :nq],
                                                                start=(i == 0), stop=(i == nt - 1)),
                                   reads=rv + [BP_], writes=[Bp[ob]], sig=False)
                                op(PE, lambda: nc.tensor.matmul(pbank[db][:, :nq], lhsT=ones1[:nk, :], rhs=P_[:nk, :nq],
                                                                start=(i == 0), stop=(i == nt - 1)),
                                   reads=[Bc, BP_], writes=[Bp[db]], sig=True)
                            tcount[0] += nt
                            ys, Bys = yst[si % 2], Byst[si % 2]
                            op(ACT, lambda: nc.scalar.activation(d2[:, :nq], pbank[db][:, :nq], AF.Ln), reads=[Bp[db]],
                               writes=[Bd2])
                            op(ACT, lambda: nc.scalar.activation(d2[:, :nq], d2[:, :nq], AF.Exp, scale=-1.0), reads=[Bd2],
                               writes=[Bd2])
                            op(DVE, lambda: nc.vector.tensor_tensor(Osb[:, :nq], pbank[ob][:, :nq], d2[:, :nq], ALU.mult),
                               reads=[Bp[ob], Bd2], writes=[BOs])

                            def ep_tail(ys=ys, Bys=Bys, nq=nq, t0=t0, h=h):
                                op(ACT, lambda: nc.scalar.activation(sqo[:, :nq], Osb[:, :nq], AF.Square),
                                   reads=[BOs], writes=[Bsqo])
                                op(PE, lambda: nc.tensor.matmul(pbank[6][:, :nq], lhsT=onesG[:], rhs=sqo[:, :nq],
                                                                start=True, stop=True),
                                   reads=[Bsqo, Bc], writes=[Bp[6]])
                                rsqrt_act(uu[:, :nq], pbank[6][:, :nq], EPS, [Bp[6]], [Buu])
                                op(DVE, lambda: nc.vector.scalar_tensor_tensor(ys[:, :nq], Osb[:, :nq],
                                                                               vcol(80 + l * 16 + h), uu[:, :nq],
                                                                               ALU.mult, ALU.mult),
                                   reads=[BOs, Buu, Bc], writes=[Bys])
                                SPQ.dma(yT_d.ap()[h * 128:(h + 1) * 128, t0:t0 + nq], ys[:, :nq], reads=[Bys],
                                        writes=[B["yT"]])

                            pending.append(ep_tail)
                            if nt <= 4:
                                while pending:
                                    pending.pop(0)()
                while pending:
                    pending.pop(0)()

        def phase34(l, last):
            with ExitStack() as st:
                hT1 = sb("p3_hT1", [128, KC, TUM], F32, st)
                Bh = [Buf() for _ in range(KC)]
                yTs = sb("p3_yTs", [128, KC, TUM], BF16, st)
                By = Buf()
                aT = sb("p3_aT", [128, FC, TUM], BF16, st)
                Ba = [Buf() for _ in range(FC)]
                wA = [sb(f"p3_wA{i}", [128, KC, 128], BF16, st) for i in range(2)]
                wB_ = [sb(f"p3_wB{i}", [128, KC, 128], BF16, st) for i in range(2)]
                BwA = [Buf(), Buf()]
                BwB = [Buf(), Buf()]
                wD = [sb(f"p3_wD{i}", [128, FC, 128], BF16, st) for i in range(2)]
                BwD = [Buf(), Buf()]
                wO = [sb(f"p3_wO{i}", [128, 2, KC, 128], BF16, st) for i in range(2)]
                BwO = [Buf(), Buf()]
                uh = sb("p3_uh", [128, 8, 2 + CH], BF16, st)
                Buh = Buf()
                gb = sb("p3_gb", [128, 8, TUM], BF16, st)
                Bgb = Buf()
                t1 = sb("p3_t1", [128, CH], F32, st)
                cv = sb("p3_cv", [128, CH], F32, st)
                sqc = sb("p3_sqc", [128, CH], BF16, st)
                rsx = sb("p3_rsx", [128, TUM], F32, st)
                sg = [sb(f"p3_sg{i}", [128, CH], F32, st) for i in range(2)]
                Bt1, Bcv, Bsqc, Brsx = Buf(), Buf(), Buf(), Buf()
                Bsg = [Buf(), Buf()]
                tl = sb("p3_tl", [128, NRANK, 8, 2 * NLOC], BF16, st)
                mt = sb("p3_mt", [128, 8, 2], BF16, st)
                hal = sb("p3_hal", [128, 8, 2], F32, st)
                Btl, Bhal = Buf(), Buf()
                if last:
                    otile = [sb(f"p3_ot{i}", [128, D // 2], F32, st) for i in range(2)]
                    Bot = [Buf(), Buf()]
                SPQ.dma(tl[:], ut_out.ap().rearrange("(r g p) c -> p r g c", r=NRANK, g=8), reads=[B["ut_out"]],
                        writes=[Btl])
                SPQ.dma(mt[:], uT_d.ap()[:, NMETA - 2:NMETA].rearrange("(g p) c -> p g c", p=128), reads=[B["uT"]],
                        writes=[Btl])
                ycv = sb("p3_ycv", [128, 8, TUM], BF16, st)
                Bycv = Buf()
                cwb = 112 + l * 24
                oti = [0]
                sgi = [0]
                def conv(u):
                    t0, segs, TU = unit_info(u)
                    SPQ.dma(gb[:, :, :TU], gbT_d.ap()[:, t0:t0 + TU].rearrange("(k p) t -> p k t", p=128),
                            reads=[B["gbT"]], writes=[Bgb])
                    for (so, n) in segs:
                        SPQ.dma(uh[:, :, 2:2 + n], uT_d.ap()[:, t0 + so:t0 + so + n].rearrange("(k p) t -> p k t", p=128),
                                reads=[B["uT"]], writes=[Buh])
                        if n == NMETA:
                            op(DVE, lambda: nc.vector.memset(uh[:, :, 0:2], 0.0), reads=[], writes=[Buh])
                        else:
                            lam = u
                            for r4 in range(4):
                                j = chunk_global(r4, lam)
                                if j == 0:
                                    cand = mt[:]
                                else:
                                    rho, lp = chunk_owner(j - 1)
                                    cand = tl[:, rho, :, 2 * lp:2 * lp + 2]
                                if r4 == 0:
                                    op(DVE, lambda: nc.vector.tensor_scalar(hal[:], cand, oh[:, 0:1], None, ALU.mult),
                                       reads=[Btl, Bc], writes=[Bhal])
                                else:
                                    op(DVE, lambda: nc.vector.scalar_tensor_tensor(hal[:], cand, oh[:, r4:r4 + 1], hal[:],
                                                                                   ALU.mult, ALU.add),
                                       reads=[Btl, Bc, Bhal], writes=[Bhal])
                            op(DVE, lambda: nc.vector.tensor_copy(uh[:, :, 0:2], hal[:]), reads=[Bhal], writes=[Buh])
                        for g in range(8):
                            op(DVE, lambda: nc.vector.tensor_scalar(t1[:, :n], uh[:, g, 0:n], vcol(cwb + g), None, ALU.mult),
                               reads=[Buh, Bc], writes=[Bt1])
                            op(DVE, lambda: nc.vector.scalar_tensor_tensor(t1[:, :n], uh[:, g, 1:n + 1], vcol(cwb + 8 + g),
                                                                           t1[:, :n], ALU.mult, ALU.add),
                               reads=[Buh, Bc, Bt1], writes=[Bt1])
                            op(DVE, lambda: nc.vector.scalar_tensor_tensor(t1[:, :n], uh[:, g, 2:n + 2], vcol(cwb + 16 + g),
                                                                           t1[:, :n], ALU.mult, ALU.add),
                               reads=[Buh, Bc, Bt1], writes=[Bt1])
                            op(DVE, lambda: nc.vector.tensor_tensor(cv[:, :n], t1[:, :n], gb[:, g, so:so + n], ALU.mult),
                               reads=[Bt1, Bgb], writes=[Bcv])
                            op(ACT, lambda: nc.scalar.activation(sqc[:, :n], cv[:, :n], AF.Square),
                               reads=[Bcv], writes=[Bsqc])
                            op(PE, lambda: nc.tensor.matmul(pbank[6][:, :n], lhsT=onesG[:], rhs=sqc[:, :n],
                                                            start=True, stop=True),
                               reads=[Bsqc, Bc], writes=[Bp[6]])
                            rsqrt_act(rsx[:, :n], pbank[6][:, :n], EPS, [Bp[6]], [Brsx])
                            op(DVE, lambda: nc.vector.scalar_tensor_tensor(ycv[:, g, so:so + n], cv[:, :n],
                                                                           vcol(80 + l * 16 + 8 + g), rsx[:, :n],
                                                                           ALU.mult, ALU.mult),
                               reads=[Bcv, Brsx, Bc], writes=[Bycv])

                conv(0)
                for u in range(NLOC):
                    t0, segs, TU = unit_info(u)
                    SPQ.dma(hT1[:, :, :TU], hT_d.ap()[:, t0:t0 + TU].rearrange("(k p) t -> p k t", p=128),
                            reads=[B["hT"]], writes=Bh)
                    SPQ.dma(yTs[:, 0:8, :TU], yT_d.ap()[:, t0:t0 + TU].rearrange("(k p) t -> p k t", p=128),
                            reads=[B["yT"]], writes=[By])
                    def load_wo(i):
                        SPQ.dma(wO[i % 2][:], wob[l].ap()[2 * i:2 * i + 2].rearrange("b p k j -> p b k j"),
                                reads=[Bw[("wob", l)]], writes=[BwO[i % 2]])

                    load_wo(0)
                    def load_gu(f):
                        SPQ.dma(wA[f % 2][:], wgb[l].ap()[f], reads=[Bw[("wgb", l)]], writes=[BwA[f % 2]])
                        SPQ.dma(wB_[f % 2][:], wub[l].ap()[f], reads=[Bw[("wub", l)]], writes=[BwB[f % 2]])

                    load_gu(0)
                    def load_d(m):
                        SPQ.dma(wD[m % 2][:], wdb[l].ap()[m], reads=[Bw[("wdb", l)]], writes=[BwD[m % 2]])

                    load_d(0)
                    for i in range(8):
                        if i + 1 < 8:
                            load_wo(i + 1)
                        for b2 in range(2):
                            m = 2 * i + b2
                            for (so, n) in segs:
                                bk = next_acc()
                                for kc in range(KC):
                                    rhs_ = yTs[:, kc, so:so + n] if kc < 8 else ycv[:, kc - 8, so:so + n]
                                    op(PE, lambda: nc.tensor.matmul(pbank[bk][:, :n], lhsT=wO[i % 2][:, b2, kc, :],
                                                                    rhs=rhs_,
                                                                    start=(kc == 0), stop=(kc == KC - 1)),
                                       reads=[BwO[i % 2], By if kc < 8 else Bycv], writes=[Bp[bk]],
                                       sig=(kc == KC - 1))
                                op(DVE, lambda: nc.vector.tensor_tensor(hT1[:, m, so:so + n], hT1[:, m, so:so + n],
                                                                        pbank[bk][:, :n], ALU.add),
                                   reads=[Bp[bk]], writes=[Bh[m]])

                    def rms_stats():
                        for kc in range(KC):
                            op(ACT, lambda: nc.scalar.activation(aT[:, kc, :TU], hT1[:, kc, :TU], AF.Square),
                               reads=[Bh[kc]], writes=[Ba[kc]])
                        for (so, n) in segs:
                            for kc in range(KC):
                                op(PE, lambda: nc.tensor.matmul(pbank[6][:, :n], lhsT=onesD[:], rhs=aT[:, kc, so:so + n],
                                                                start=(kc == 0), stop=(kc == KC - 1)),
                                   reads=[Ba[kc], Bc], writes=[Bp[6]], sig=(kc == KC - 1))
                            rsqrt_act(rsx[:, so:so + n], pbank[6][:, :n], EPS, [Bp[6]], [Brsx])

                    rms_stats()
                    for kc in range(KC):
                        op(DVE, lambda: nc.vector.scalar_tensor_tensor(yTs[:, kc, :TU], hT1[:, kc, :TU],
                                                                       vcol(32 + l * 16 + kc), rsx[:, :TU],
                                                                       ALU.mult, ALU.mult),
                           reads=[Bh[kc], Brsx, Bc], writes=[By])
                    for f in range(FC):
                        conv_slice(1)
                        if f + 1 < FC:
                            load_gu(f + 1)
                        for (so, n) in segs:
                            bg_, bu_ = next_acc(), next_acc()
                            for kc in range(KC):
                                op(PE, lambda: nc.tensor.matmul(pbank[bg_][:, :n], lhsT=wA[f % 2][:, kc, :],
                                                                rhs=yTs[:, kc, so:so + n], start=(kc == 0),
                                                                stop=(kc == KC - 1)),
                                   reads=[BwA[f % 2], By], writes=[Bp[bg_]], sig=(kc == KC - 1))
                            for kc in range(KC):
                                op(PE, lambda: nc.tensor.matmul(pbank[bu_][:, :n], lhsT=wB_[f % 2][:, kc, :],
                                                                rhs=yTs[:, kc, so:so + n], start=(kc == 0),
                                                                stop=(kc == KC - 1)),
                                   reads=[BwB[f % 2], By], writes=[Bp[bu_]], sig=(kc == KC - 1))
                            k_ = sgi[0] = (sgi[0] + 1) % 2
                            op(ACT, lambda: nc.scalar.activation(sg[k_][:, :n], pbank[bg_][:, :n], AF.Silu),
                               reads=[Bp[bg_]], writes=[Bsg[k_]])
                            op(DVE, lambda: nc.vector.tensor_tensor(aT[:, f, so:so + n], sg[k_][:, :n], pbank[bu_][:, :n],
                                                                    ALU.mult),
                               reads=[Bsg[k_], Bp[bu_]], writes=[Ba[f]])
                    if u + 1 < NLOC:
                        conv(u + 1)
                    for m in range(16):
                        if m + 1 < 16:
                            load_d(m + 1)
                        for (so, n) in segs:
                            bk = next_acc()
                            for f in range(FC):
                                op(PE, lambda: nc.tensor.matmul(pbank[bk][:, :n], lhsT=wD[m % 2][:, f, :],
                                                                rhs=aT[:, f, so:so + n], start=(f == 0), stop=(f == FC - 1)),
                                   reads=[BwD[m % 2], Ba[f]], writes=[Bp[bk]], sig=(f == FC - 1))
                            op(DVE, lambda: nc.vector.tensor_tensor(hT1[:, m, so:so + n], hT1[:, m, so:so + n],
                                                                    pbank[bk][:, :n], ALU.add),
                               reads=[Bp[bk]], writes=[Bh[m]])
                    if not last:
                        SPQ.dma(hT_d.ap()[:, t0:t0 + TU].rearrange("(k p) t -> p k t", p=128), hT1[:, :, :TU],
                                reads=Bh, writes=[B["hT"]])
                    else:
                        rms_stats()
                        for kc in range(KC):
                            op(DVE, lambda: nc.vector.scalar_tensor_tensor(hT1[:, kc, :TU], hT1[:, kc, :TU],
                                                                           vcol(64 + kc), rsx[:, :TU], ALU.mult, ALU.mult),
                               reads=[Brsx, Bc], writes=[Bh[kc]])
                        so = TU - CH
                        for s4 in range(4):
                            for half in range(2):
                                oi = oti[0] = (oti[0] + 1) % 2
                                for k2 in range(2):
                                    k4 = 2 * half + k2
                                    bk = next_acc()
                                    for kk in range(4):
                                        kc = 4 * k4 + kk
                                        op(PE, lambda: nc.tensor.transpose(pbank[bk][:, kk * 128:(kk + 1) * 128],
                                                                           hT1[:, kc, so + s4 * 128:so + (s4 + 1) * 128],
                                                                           identf[:]),
                                           reads=[Bh[kc], Bc], writes=[Bp[bk]], sig=(kk == 3))
                                    evac(otile[oi][:, k2 * 512:(k2 + 1) * 512], pbank[bk][:, :], [Bp[bk]], [Bot[oi]])
                                r0 = u * CH + s4 * 128
                                SPQ.dma(out_d.ap()[r0:r0 + 128, half * 1024:(half + 1) * 1024], otile[oi][:],
                                        reads=[Bot[oi]], writes=[B["out"]])

        for l in range(NLAYERS):
            phase1(l)
            barrier()
            phase2(l)
            if l == 0 and NLAYERS > 1:
                convert(1, defer=True)
            barrier()
            phase34(l, l == NLAYERS - 1)
            conv_slice(10 ** 6)
            barrier()
    return nc


_PROG_CACHE = {}


def _run(inputs, NLOC, NLAYERS):
    f32 = np.float32
    x = np.asarray(inputs["x"], f32)
    meta = np.ascontiguousarray(np.asarray(inputs["meta"], f32))
    norm_mix = np.asarray(inputs["norm_mix"], f32)
    b_f = np.asarray(inputs["b_f"], f32)
    conv_w = np.asarray(inputs["conv_w"], f32)
    out_gain = np.asarray(inputs["out_gain"], f32)
    norm_ffn = np.asarray(inputs["norm_ffn"], f32)
    final_norm = np.asarray(inputs["final_norm"], f32)
    wts = {k: np.ascontiguousarray(np.asarray(inputs[k], f32)) for k in ["w_in", "w_out", "w_gate", "w_up", "w_down"]}
    NG = 4 * NLOC
    NKT = 1 + 4 * NG
    vecs = np.zeros((128, NV), f32)
    for l in range(2):
        vecs[:, l * 16:(l + 1) * 16] = norm_mix[l].reshape(16, 128).T
        vecs[:, 32 + l * 16:32 + (l + 1) * 16] = norm_ffn[l].reshape(16, 128).T
        vecs[:, 80 + l * 16:80 + (l + 1) * 16] = out_gain[l].reshape(16, 128).T
        for k in range(3):
            vecs[:, 112 + l * 24 + k * 8:112 + l * 24 + (k + 1) * 8] = conv_w[l, k].reshape(8, 128).T
    vecs[:, 64:80] = final_norm.reshape(16, 128).T
    bf = np.ascontiguousarray(b_f.T)
    identf = np.eye(128, dtype=f32)
    identb = np.eye(128).astype(ml_dtypes.bfloat16)
    tri = np.triu(np.ones((128, 128))).astype(ml_dtypes.bfloat16)
    in_maps = []
    for c in range(8):
        b, r = divmod(c, 4)
        xl = np.concatenate([x[b, chunk_global(r, lam) * CH:(chunk_global(r, lam) + 1) * CH] for lam in range(NLOC)], 0)
        ohm = np.zeros((128, 4), f32)
        ohm[:, r] = 1.0
        penm = np.zeros((NLOC, NKT), f32)
        for lam in range(NLOC):
            j = chunk_global(r, lam)
            for kt in range(1, NKT):
                if (kt - 1) // 4 >= j:
                    penm[lam, kt] = -BIG
        penb = np.ascontiguousarray(np.broadcast_to(penm.reshape(1, -1), (128, NLOC * NKT)))
        m = {"x": np.ascontiguousarray(xl), "meta": meta, "vecs": vecs, "bf": bf, "identf": identf, "identb": identb,
             "tri": tri, "oh": ohm, "pen": penb}
        m.update(wts)
        in_maps.append(m)
    key = (NLOC, NLAYERS)
    if key not in _PROG_CACHE:
        _PROG_CACHE[key] = build(NLOC, NLAYERS)
    nc = _PROG_CACHE[key]
    res = run_bass_kernel_spmd(nc, in_maps, core_ids=list(range(8)))
    if DEBUG:
        _run.last = res.results
    out = np.zeros((2, NG * CH, D), f32)
    for c in range(8):
        b, r = divmod(c, 4)
        o = np.asarray(res.results[c]["out"])
        for lam in range(NLOC):
            j = chunk_global(r, lam)
            out[b, j * CH:(j + 1) * CH] = o[lam * CH:(lam + 1) * CH]
    return out


def kernel(**inputs):
    return _run(inputs, 8, 2)
```

```python
from contextlib import ExitStack
import numpy as np
import ml_dtypes
import concourse.bass as bass
import concourse.mybir as mybir
from concourse.bass_utils import run_bass_kernel_spmd

F32, BF16 = mybir.dt.float32, mybir.dt.bfloat16
AF = mybir.ActivationFunctionType
ALU = mybir.AluOpType

D = 2048
KC = 16
NH = 8
DFF = 5632
FC = 44
DIN = 6152
NMETA = 16
CH = 512
EPS = 1e-6
SCALE = 128 ** -0.5
NRANK = 4


class Sem:
    def __init__(self, h):
        self.h = h
        self.val = 0


class Buf:
    __slots__ = ("w", "r")

    def __init__(self):
        self.w = {}
        self.r = {}


class Eng:
    def __init__(self, name, eng, sem):
        self.name, self.eng, self.sem, self.cnt, self.seen = name, eng, sem, 0, {}
        self.is_pe = name == "pe"

    def wait_t(self, s, v):
        if v <= 0:
            return
        if s is self.sem:
            if self.is_pe or v > self.cnt:
                return
        if self.seen.get(s, 0) >= v:
            return
        self.eng.wait_ge(s.h, v)
        self.seen[s] = v

    def wait_bufs(self, reads, writes):
        for b in reads:
            for s, v in b.w.items():
                self.wait_t(s, v)
        for b in writes:
            for s, v in b.w.items():
                self.wait_t(s, v)
            for s, v in b.r.items():
                self.wait_t(s, v)


def _commit(s, t, reads, writes):
    for b in writes:
        if b.w.get(s, 0) < t:
            b.w[s] = t
    for b in reads:
        if b.r.get(s, 0) < t:
            b.r[s] = t


def op(E, fn, reads=(), writes=(), sig=True):
    E.wait_bufs(reads, writes)
    ins = fn()
    if sig:
        ins.then_inc(E.sem.h, 1)
        E.cnt += 1
        t = E.cnt
    else:
        t = E.cnt + 1
    _commit(E.sem, t, reads, writes)


class DmaQ:
    def __init__(self, E, sems):
        self.E, self.sems, self.i = E, sems, 0

    def dma(self, out, in_, reads=(), writes=()):
        s = self.sems[self.i]
        self.i = (self.i + 1) % len(self.sems)
        self.E.wait_t(s, s.val)
        self.E.wait_bufs(reads, writes)
        self.E.eng.dma_start(out=out, in_=in_).then_inc(s.h, 16)
        s.val += 16
        _commit(s, s.val, reads, writes)


def chunk_owner(j):
    i, pos = divmod(j, 8)
    if pos < 4:
        return pos, 2 * i
    return 7 - pos, 2 * i + 1


def chunk_global(r, lam):
    return 8 * (lam // 2) + (r if lam % 2 == 0 else 7 - r)


W_IN_COLS = ([128 * i for i in range(8)] + [1024 + 128 * i for i in range(8)]
             + [2048 + 128 * i for i in range(8)] + [3072 + 128 * i for i in range(8)])
for _g in range(8):
    W_IN_COLS += [4096 + 128 * _g, 5120 + 128 * _g]
NV = 160
BIG = 1.0e5 / SCALE


DEBUG = False


def build(NLOC=8, NLAYERS=2):
    nc = bass.Bass("TRN2", target_bir_lowering=False)
    TL = NMETA + NLOC * CH
    NG = 4 * NLOC
    NTOK = NMETA + NG * CH
    NKT = 1 + 4 * NG
    NSUB = 1 + 4 * NLOC
    NXL = NLOC * CH

    def din(name, shape, dt=F32):
        return nc.dram_tensor(name, shape, dt, kind="ExternalInput")

    x_in = din("x", [NXL, D])
    meta_in = din("meta", [NMETA, D])
    w_in = din("w_in", [2, D, DIN])
    w_out = din("w_out", [2, D, D])
    w_gate = din("w_gate", [2, D, DFF])
    w_up = din("w_up", [2, D, DFF])
    w_down = din("w_down", [2, DFF, D])
    vecs_in = din("vecs", [128, NV])
    bf_in = din("bf", [8, 2])
    identf_in = din("identf", [128, 128])
    identb_in = din("identb", [128, 128], BF16)
    tri_in = din("tri", [128, 128], BF16)
    oh_in = din("oh", [128, 4])
    pen_in = din("pen", [128, NLOC * NKT])
    out_d = nc.dram_tensor("out", [NXL, D], F32, kind="ExternalOutput")

    def dscr(name, shape, dt):
        if DEBUG and not name.startswith("w") and not name.endswith("_in") and not name.endswith("_out"):
            return nc.dram_tensor(name, shape, dt, kind="ExternalOutput")
        return nc.dram_tensor(name, shape, dt)

    wib = [dscr(f"wib{l}", [48, 128, KC, 128], BF16) for l in range(2)]
    wfb = [dscr(f"wfb{l}", [128, KC, 8], BF16) for l in range(2)]
    wob = [dscr(f"wob{l}", [16, 128, KC, 128], BF16) for l in range(2)]
    wgb = [dscr(f"wgb{l}", [FC, 128, KC, 128], BF16) for l in range(2)]
    wub = [dscr(f"wub{l}", [FC, 128, KC, 128], BF16) for l in range(2)]
    wdb = [dscr(f"wdb{l}", [16, 128, FC, 128], BF16) for l in range(2)]
    hT_d = dscr("hT", [D, TL], F32)
    qT_d = dscr("qT", [1024, TL], BF16)
    kmeta_d = dscr("kmeta", [1024, NMETA], BF16)
    vmeta_d = dscr("vmeta", [1024, NMETA], BF16)
    kg_in = [dscr(f"kg{i}_in", [1024, CH], BF16) for i in range(NLOC)]
    vg_in = [dscr(f"vg{i}_in", [1024, CH], BF16) for i in range(NLOC)]
    kg_out = [dscr(f"kg{i}_out", [NRANK * 1024, CH], BF16) for i in range(NLOC)]
    vg_out = [dscr(f"vg{i}_out", [NRANK * 1024, CH], BF16) for i in range(NLOC)]
    gbT_d = dscr("gbT", [1024, TL], BF16)
    uT_d = dscr("uT", [1024, TL], BF16)
    ut_in = dscr("ut_in", [1024, 2 * NLOC], BF16)
    ut_out = dscr("ut_out", [NRANK * 1024, 2 * NLOC], BF16)
    lf_in = dscr("lf_in", [8, NXL], F32)
    lf_out = dscr("lf_out", [NRANK * 8, NXL], F32)
    lfm_d = dscr("lfm", [8, NMETA], F32)
    yT_d = dscr("yT", [1024, TL], BF16)

    B = {k: Buf() for k in ["hT", "qT", "kmeta", "vmeta", "kg_in", "vg_in", "kg_out", "vg_out", "gbT", "uT",
                            "ut_in", "ut_out", "lf_in", "lf_out", "lfm", "yT", "out", "const"]}
    for i_ in range(NLOC):
        for n_ in ["kg_in", "vg_in", "kg_out", "vg_out"]:
            B[(n_, i_)] = Buf()
    Bw = {(n, l): Buf() for n in ["wib", "wfb", "wob", "wgb", "wub", "wdb"] for l in range(2)}

    es = ExitStack()
    with es:
        def sem(name):
            return Sem(es.enter_context(nc.semaphore(name)))

        PE = Eng("pe", nc.tensor, sem("s_pe"))
        ACT = Eng("act", nc.scalar, sem("s_act"))
        DVE = Eng("dve", nc.vector, sem("s_dve"))
        POOL = Eng("pool", nc.gpsimd, sem("s_pool"))
        SP = Eng("sp", nc.sync, sem("s_sp"))
        SPQ = DmaQ(SP, [sem(f"dq{i}") for i in range(24)])
        PQ = DmaQ(POOL, [sem(f"pq{i}") for i in range(16)])
        conv_jobs = []
        ccsems = [sem(f"cc{i}") for i in range(8)]
        cc_i = [0]

        sb_n = [0]

        def sb(name, shape, dt, stack=None):
            sb_n[0] += 1
            return (stack or es).enter_context(nc.sbuf_tensor(f"sb{sb_n[0]}_{name}", shape, dt))

        pbank = [es.enter_context(nc.psum_tensor(f"pb{i}", [128, 512], F32)) for i in range(8)]
        Bp = [Buf() for _ in range(8)]

        identf = sb("identf", [128, 128], F32)
        identb = sb("identb", [128, 128], BF16)
        tri = sb("tri", [128, 128], BF16)
        vecs = sb("vecs", [128, NV], F32)
        bfv = sb("bfv", [8, 2], F32)
        nbf = sb("nbf", [8, 2], F32)
        oh = sb("oh", [128, 4], F32)
        pen = sb("pen", [128, NLOC, NKT], F32)
        onesD = sb("onesD", [128, 128], BF16)
        onesG = sb("onesG", [128, 128], BF16)
        ones1 = sb("ones1", [128, 128], BF16)
        ones8f = sb("ones8f", [8, 128], F32)
        zerosf = sb("zerosf", [128, 512], F32)
        Bc = B["const"]
        SPQ.dma(identf[:], identf_in.ap(), writes=[Bc])
        SPQ.dma(identb[:], identb_in.ap(), writes=[Bc])
        SPQ.dma(tri[:], tri_in.ap(), writes=[Bc])
        SPQ.dma(vecs[:], vecs_in.ap(), writes=[Bc])
        SPQ.dma(bfv[:], bf_in.ap(), writes=[Bc])
        SPQ.dma(oh[:], oh_in.ap(), writes=[Bc])
        SPQ.dma(pen[:], pen_in.ap().rearrange("p (a b) -> p a b", a=NLOC), writes=[Bc])
        op(DVE, lambda: nc.vector.memset(onesD[:], 1.0 / D), writes=[Bc])
        op(DVE, lambda: nc.vector.memset(onesG[:], 1.0 / 128), writes=[Bc])
        op(DVE, lambda: nc.vector.memset(ones1[:], 1.0), writes=[Bc])
        op(DVE, lambda: nc.vector.memset(ones8f[:], 1.0), writes=[Bc])
        op(DVE, lambda: nc.vector.memset(zerosf[:], 0.0), writes=[Bc])
        op(DVE, lambda: nc.vector.tensor_scalar(nbf[:], bfv[:], -1.0, None, ALU.mult), reads=[Bc], writes=[Bc])

        def vcol(c):
            return vecs[:, c:c + 1]

        def convert(l, defer=False):
            jobs = []

            class _Q:
                @staticmethod
                def dma(out, in_, writes):
                    jobs.append((out, in_, writes))
            PQ_ = _Q
            convert_body(l, PQ_)
            if defer:
                conv_jobs.extend(jobs)
            else:
                for (o_, i_, w_) in jobs[:49]:
                    PQ.dma(o_, i_, writes=w_)
                conv_jobs.extend(jobs[49:])

        def conv_slice(n):
            for _ in range(min(n, len(conv_jobs))):
                o_, i_, w_ = conv_jobs.pop(0)
                PQ.dma(o_, i_, writes=w_)

        def convert_body(l, PQ):
            for bi, c0 in enumerate(W_IN_COLS):
                PQ.dma(wib[l].ap()[bi], w_in.ap()[l, :, c0:c0 + 128].rearrange("(k p) j -> p k j", p=128),
                       writes=[Bw[("wib", l)]])
            PQ.dma(wfb[l].ap(), w_in.ap()[l, :, 6144:6152].rearrange("(k p) j -> p k j", p=128),
                   writes=[Bw[("wfb", l)]])
            for m in range(16):
                PQ.dma(wob[l].ap()[m], w_out.ap()[l, :, m * 128:(m + 1) * 128].rearrange("(k p) j -> p k j", p=128),
                       writes=[Bw[("wob", l)]])
            for f in range(FC):
                PQ.dma(wgb[l].ap()[f], w_gate.ap()[l, :, f * 128:(f + 1) * 128].rearrange("(k p) j -> p k j", p=128),
                       writes=[Bw[("wgb", l)]])
                PQ.dma(wub[l].ap()[f], w_up.ap()[l, :, f * 128:(f + 1) * 128].rearrange("(k p) j -> p k j", p=128),
                       writes=[Bw[("wub", l)]])
            for m in range(16):
                for half in range(2):
                    PQ.dma(wdb[l].ap()[m, :, half * 22:(half + 1) * 22, :],
                           w_down.ap()[l, half * 2816:(half + 1) * 2816, m * 128:(m + 1) * 128].rearrange(
                               "(k p) j -> p k j", p=128),
                           writes=[Bw[("wdb", l)]])

        convert(0)

        ev_i = [0]

        def evac(out, in_, reads, writes):
            ev_i[0] ^= 1
            if ev_i[0]:
                op(ACT, lambda: nc.scalar.copy(out, in_), reads, writes)
            else:
                op(DVE, lambda: nc.vector.tensor_copy(out, in_), reads, writes)

        def rsqrt_act(out, in_, eps, reads, writes):
            op(ACT, lambda: nc.scalar.activation(out, in_, AF.Ln, bias=eps, scale=1.0), reads, writes)
            op(ACT, lambda: nc.scalar.activation(out, out, AF.Exp, scale=-0.5), writes, writes)

        acc_i = [0]

        def next_acc(banks=(0, 1, 2, 3, 4, 5)):
            acc_i[0] = (acc_i[0] + 1) % len(banks)
            return banks[acc_i[0]]

        def unit_info(u):
            if u == 0:
                return 0, [(0, NMETA), (NMETA, CH)], NMETA + CH
            return NMETA + u * CH, [(0, CH)], CH

        def allgather(src, dst, bsrc, bdst):
            s = ccsems[cc_i[0] % len(ccsems)]
            cc_i[0] += 1
            POOL.wait_t(s, s.val)
            POOL.wait_bufs([bsrc], [bdst])
            nc.gpsimd.collective_compute("AllGather", ALU.bypass, replica_groups=[[0, 1, 2, 3], [4, 5, 6, 7]],
                                         ins=[src.ap().opt()], outs=[dst.ap().opt()]).then_inc(s.h, 1)
            s.val += 1
            _commit(s, s.val, [bsrc], [bdst])

        TUM = NMETA + CH

        def phase1(l):
            with ExitStack() as st:
                hT = sb("p1_hT", [128, KC, TUM], F32, st)
                Bh = [Buf() for _ in range(KC)]
                sq = sb("p1_sq", [128, KC, TUM], BF16, st)
                Bsq = [Buf() for _ in range(KC)]
                hns = [sb(f"p1_hn{i}", [128, KC, TUM], BF16, st) for i in range(2)]
                Bhns = [Buf(), Buf()]
                rstd = sb("p1_rstd", [128, TUM], F32, st)
                Brs = Buf()
                wt = [sb(f"p1_wt{i}", [128, 4, KC, 128], BF16, st) for i in range(2)]
                Bwt = [Buf(), Buf()]
                wf = sb("p1_wf", [128, KC, 8], BF16, st)
                Bwf = Buf()
                ost = [sb(f"p1_ost{i}", [128, TUM], BF16, st) for i in range(3)]
                Bost = [Buf() for _ in range(3)]
                gcs = sb("p1_gcs", [128, TUM], BF16, st)
                Bgcs = Buf()
                lfe = sb("p1_lfe", [8, TUM], F32, st)
                lfs = sb("p1_lfs", [8, TUM], F32, st)
                Blf = Buf()
                if l == 0:
                    xt = sb("p1_xt", [128, 4, D], F32, st)
                    Bxt = [Buf() for _ in range(4)]
                    xm = sb("p1_xm", [NMETA, D], F32, st)
                    Bxm = Buf()
                SPQ.dma(wf[:], wfb[l].ap(), reads=[Bw[("wfb", l)]], writes=[Bwf])
                ost_i = [0]
                def prologue(u):
                    t0, segs, TU = unit_info(u)
                    hn, Bhn = hns[u % 2], Bhns[u % 2]
                    if l == 0:
                        for (so, n) in segs:
                            if n == NMETA:
                                SPQ.dma(xm[:], meta_in.ap(), writes=[Bxm])
                                for kc in range(KC):
                                    op(PE, lambda: nc.tensor.transpose(pbank[6][:, kc * 16:(kc + 1) * 16],
                                                                       xm[:, kc * 128:(kc + 1) * 128],
                                                                       identf[:NMETA, :NMETA]),
                                       reads=[Bxm, Bc], writes=[Bp[6]], sig=(kc == KC - 1))
                                for kc in range(KC):
                                    evac(hT[:, kc, so:so + n], pbank[6][:, kc * 16:(kc + 1) * 16], [Bp[6]], [Bh[kc]])
                            else:
                                for s4 in range(4):
                                    r0 = u * CH + s4 * 128
                                    SPQ.dma(xt[:, s4, :], x_in.ap()[r0:r0 + 128, :], writes=[Bxt[s4]])
                                for kc in range(KC):
                                    bk = next_acc()
                                    for s4 in range(4):
                                        op(PE, lambda: nc.tensor.transpose(pbank[bk][:, s4 * 128:(s4 + 1) * 128],
                                                                           xt[:, s4, kc * 128:(kc + 1) * 128],
                                                                           identf[:]),
                                           reads=[Bxt[s4], Bc], writes=[Bp[bk]], sig=(s4 == 3))
                                    evac(hT[:, kc, so:so + n], pbank[bk][:, :n], [Bp[bk]], [Bh[kc]])
                        SPQ.dma(hT_d.ap()[:, t0:t0 + TU].rearrange("(k p) t -> p k t", p=128), hT[:, :, :TU],
                                reads=Bh, writes=[B["hT"]])
                    else:
                        SPQ.dma(hT[:, :, :TU], hT_d.ap()[:, t0:t0 + TU].rearrange("(k p) t -> p k t", p=128),
                                reads=[B["hT"]], writes=Bh)
                    for kc in range(KC):
                        op(ACT, lambda: nc.scalar.activation(sq[:, kc, :TU], hT[:, kc, :TU], AF.Square),
                           reads=[Bh[kc]], writes=[Bsq[kc]])
                    for (so, n) in segs:
                        for kc in range(KC):
                            op(PE, lambda: nc.tensor.matmul(pbank[6][:, :n], lhsT=onesD[:], rhs=sq[:, kc, so:so + n],
                                                            start=(kc == 0), stop=(kc == KC - 1)),
                               reads=[Bsq[kc], Bc], writes=[Bp[6]], sig=(kc == KC - 1))
                        rsqrt_act(rstd[:, so:so + n], pbank[6][:, :n], EPS, [Bp[6]], [Brs])
                    for kc in range(KC):
                        op(DVE, lambda: nc.vector.scalar_tensor_tensor(hn[:, kc, :TU], hT[:, kc, :TU],
                                                                       vcol(l * 16 + kc), rstd[:, :TU],
                                                                       ALU.mult, ALU.mult),
                           reads=[Bh[kc], Brs, Bc], writes=[Bhn])
                def proj(u):
                    t0, segs, TU = unit_info(u)
                    hn, Bhn = hns[u % 2], Bhns[u % 2]
                    def load_w(bg):
                        SPQ.dma(wt[bg % 2][:], wib[l].ap()[4 * bg:4 * bg + 4].rearrange("b p k j -> p b k j"),
                                reads=[Bw[("wib", l)]], writes=[Bwt[bg % 2]])

                    if u == 0:
                        load_w(0)
                    for bg in range(12):
                        if bg + 1 < 12:
                            load_w(bg + 1)
                        elif u + 1 < NLOC:
                            load_w(0)
                        for b4 in range(4):
                            blk = 4 * bg + b4
                            kind = blk // 8 if blk < 32 else (4 if blk % 2 == 0 else 5)
                            if kind == 4:
                                dst, Bdst = gcs, Bgcs
                            else:
                                oi = ost_i[0] = (ost_i[0] + 1) % 3
                                dst, Bdst = ost[oi], Bost[oi]
                            for (so, n) in segs:
                                bk = next_acc()
                                for kc in range(KC):
                                    op(PE, lambda: nc.tensor.matmul(pbank[bk][:, :n], lhsT=wt[bg % 2][:, b4, kc, :],
                                                                    rhs=hn[:, kc, so:so + n],
                                                                    start=(kc == 0), stop=(kc == KC - 1)),
                                       reads=[Bwt[bg % 2], Bhn], writes=[Bp[bk]], sig=(kc == KC - 1))
                                if kind == 5:
                                    op(DVE, lambda: nc.vector.tensor_tensor(dst[:, so:so + n], pbank[bk][:, :n],
                                                                            gcs[:, so:so + n], ALU.mult),
                                       reads=[Bp[bk], Bgcs], writes=[Bdst])
                                else:
                                    evac(dst[:, so:so + n], pbank[bk][:, :n], [Bp[bk]], [Bdst])
                            if kind == 4:
                                continue
                            xo = u * CH
                            if kind == 0:
                                SPQ.dma(qT_d.ap()[blk * 128:(blk + 1) * 128, t0:t0 + TU], dst[:, :TU],
                                        reads=[Bdst], writes=[B["qT"]])
                            elif kind in (1, 2):
                                hb = blk - 8 * kind
                                md, gd, bm, bgn = ((kmeta_d, kg_in, "kmeta", "kg_in") if kind == 1
                                                   else (vmeta_d, vg_in, "vmeta", "vg_in"))
                                if u == 0:
                                    SPQ.dma(md.ap()[hb * 128:(hb + 1) * 128, :], dst[:, :NMETA],
                                            reads=[Bdst], writes=[B[bm]])
                                SPQ.dma(gd[u].ap()[hb * 128:(hb + 1) * 128, :], dst[:, TU - CH:TU],
                                        reads=[Bdst], writes=[B[(bgn, u)]])
                            elif kind == 3:
                                g = blk - 24
                                SPQ.dma(gbT_d.ap()[g * 128:(g + 1) * 128, t0:t0 + TU], dst[:, :TU],
                                        reads=[Bdst], writes=[B["gbT"]])
                            else:
                                g = (blk - 32) // 2
                                SPQ.dma(uT_d.ap()[g * 128:(g + 1) * 128, t0:t0 + TU], dst[:, :TU],
                                        reads=[Bdst], writes=[B["uT"]])
                                SPQ.dma(ut_in.ap()[g * 128:(g + 1) * 128, 2 * u:2 * u + 2], dst[:, TU - 2:TU],
                                        reads=[Bdst], writes=[B["ut_in"]])
                    for (so, n) in segs:
                        for kc in range(KC):
                            op(PE, lambda: nc.tensor.matmul(pbank[7][:8, :n], lhsT=wf[:, kc, :], rhs=hn[:, kc, so:so + n],
                                                            start=(kc == 0), stop=(kc == KC - 1)),
                               reads=[Bwf, Bhn], writes=[Bp[7]], sig=(kc == KC - 1))
                        op(ACT, lambda: nc.scalar.activation(lfe[:, so:so + n], pbank[7][:8, :n], AF.Exp,
                                                             bias=nbf[:, l:l + 1], scale=-1.0),
                           reads=[Bp[7], Bc], writes=[Blf])
                        op(ACT, lambda: nc.scalar.activation(lfe[:, so:so + n], lfe[:, so:so + n], AF.Ln,
                                                             bias=1.0, scale=1.0),
                           reads=[Blf], writes=[Blf])
                        op(DVE, lambda: nc.vector.tensor_scalar(lfs[:, so:so + n], lfe[:, so:so + n], -1.0, None,
                                                                ALU.mult),
                           reads=[Blf], writes=[Blf])
                    if u == 0:
                        SPQ.dma(lfm_d.ap(), lfs[:, :NMETA], reads=[Blf], writes=[B["lfm"]])
                    SPQ.dma(lf_in.ap()[:, u * CH:(u + 1) * CH], lfs[:, TU - CH:TU], reads=[Blf], writes=[B["lf_in"]])
                    allgather(kg_in[u], kg_out[u], B[("kg_in", u)], B[("kg_out", u)])
                    allgather(vg_in[u], vg_out[u], B[("vg_in", u)], B[("vg_out", u)])
                prologue(0)
                for u in range(NLOC):
                    if u + 1 < NLOC:
                        prologue(u + 1)
                    proj(u)
            allgather(lf_in, lf_out, B["lf_in"], B["lf_out"])
            allgather(ut_in, ut_out, B["ut_in"], B["ut_out"])


        def barrier():
            engs = [PE, ACT, DVE, POOL]
            dsems = SPQ.sems + PQ.sems + ccsems
            for E in engs + [SP]:
                for F_ in engs:
                    if F_ is not E:
                        E.wait_t(F_.sem, F_.cnt)
                for s in dsems:
                    E.wait_t(s, s.val)

        def gcols(jj):
            rho, lam = chunk_owner(jj)
            return rho, lam * CH

        def jmax(lam):
            return 8 * (lam // 2) + (3 if lam % 2 == 0 else 7)

        def cands(lam):
            return [chunk_global(r, lam) for r in range(4)]

        def phase2(l):
            with ExitStack() as st:
                CTn = sb("p2_CTn", [128, NKT, 8], F32, st)
                CTo = sb("p2_CTo", [128, 4 * NLOC, 8], F32, st)
                Rbc = sb("p2_Rbc", [128, 8 * NSUB], F32, st)
                Btab = Buf()
                with ExitStack() as st2:
                    lfF = sb("p2_lfF", [8, NTOK], F32, st2)
                    cF = sb("p2_cF", [8, NTOK], F32, st2)
                    cown = sb("p2_cown", [8, NXL], F32, st2)
                    Rm = sb("p2_Rm", [8, NSUB], F32, st2)
                    Dh = sb("p2_Dh", [8, NSUB], F32, st2)
                    Bl, Bcf, Bco, Brm, Bdh = Buf(), Buf(), Buf(), Buf(), Buf()
                    SPQ.dma(lfF[:, :NMETA], lfm_d.ap(), reads=[B["lfm"]], writes=[Bl])
                    for jj in range(NG):
                        rho, co = gcols(jj)
                        SPQ.dma(lfF[:, NMETA + jj * CH:NMETA + (jj + 1) * CH],
                                lf_out.ap()[rho * 8:(rho + 1) * 8, co:co + CH], reads=[B["lf_out"]], writes=[Bl])
                    pos = 0
                    while pos < NTOK:
                        n = min(2048, NTOK - pos)
                        init = 0.0 if pos == 0 else cF[:, pos - 1:pos]
                        op(DVE, lambda: nc.vector.tensor_tensor_scan(cF[:, pos:pos + n], lfF[:, pos:pos + n],
                                                                     lfF[:, pos:pos + n], init, ALU.add, ALU.min),
                           reads=[Bl, Bcf], writes=[Bcf])
                        pos += n
                    for lam in range(NLOC):
                        cs = cands(lam)
                        dstc = cown[:, lam * CH:(lam + 1) * CH]
                        for r4 in range(4):
                            src = cF[:, NMETA + cs[r4] * CH:NMETA + (cs[r4] + 1) * CH]
                            if r4 == 0:
                                op(DVE, lambda: nc.vector.tensor_scalar(dstc, src, oh[:8, 0:1], None, ALU.mult),
                                   reads=[Bcf, Bc], writes=[Bco])
                            else:
                                op(DVE, lambda: nc.vector.scalar_tensor_tensor(dstc, src, oh[:8, r4:r4 + 1], dstc,
                                                                               ALU.mult, ALU.add),
                                   reads=[Bcf, Bc, Bco], writes=[Bco])
                    op(DVE, lambda: nc.vector.tensor_tensor(Rm[:, 0:1], cF[:, 0:1], cF[:, NMETA - 1:NMETA], ALU.add),
                       reads=[Bcf], writes=[Brm])
                    for s in range(4 * NLOC):
                        a = s * 128
                        op(DVE, lambda: nc.vector.tensor_tensor(Rm[:, 1 + s:2 + s], cown[:, a:a + 1],
                                                                cown[:, a + 127:a + 128], ALU.add),
                           reads=[Bco], writes=[Brm])
                    for h in range(NH):
                        op(DVE, lambda: nc.vector.tensor_scalar(Dh[:], Rm[:], identf[:8, h:h + 1], None, ALU.mult),
                           reads=[Brm, Bc], writes=[Bdh])
                        op(PE, lambda: nc.tensor.matmul(pbank[6][:, h * NSUB:(h + 1) * NSUB], lhsT=ones8f[:], rhs=Dh[:],
                                                        start=True, stop=True),
                           reads=[Bdh, Bc], writes=[Bp[6]])
                    op(DVE, lambda: nc.vector.tensor_scalar(Rbc[:], pbank[6][:, :8 * NSUB], 0.5 / SCALE, None, ALU.mult),
                       reads=[Bp[6]], writes=[Btab])
                    for b0 in range(0, NKT, 64):
                        cnt = min(64, NKT - b0)
                        bk = next_acc()
                        for i in range(cnt):
                            kt = b0 + i
                            k0, nk = (0, NMETA) if kt == 0 else (NMETA + (kt - 1) * 128, 128)
                            op(PE, lambda: nc.tensor.transpose(pbank[bk][:nk, i * 8:(i + 1) * 8], cF[:, k0:k0 + nk],
                                                               identf[:8, :8]),
                               reads=[Bcf, Bc], writes=[Bp[bk]], sig=(i == cnt - 1))
                        op(DVE, lambda: nc.vector.tensor_scalar(
                            CTn[:, b0:b0 + cnt, :], pbank[bk][:, :cnt * 8].rearrange("p (a b) -> p a b", b=8),
                            -1.0 / SCALE, None, ALU.mult), reads=[Bp[bk]], writes=[Btab])
                    bk = next_acc()
                    for i in range(4 * NLOC):
                        op(PE, lambda: nc.tensor.transpose(pbank[bk][:, i * 8:(i + 1) * 8], cown[:, i * 128:(i + 1) * 128],
                                                           identf[:8, :8]),
                           reads=[Bco, Bc], writes=[Bp[bk]], sig=(i == 4 * NLOC - 1))
                    op(DVE, lambda: nc.vector.tensor_scalar(
                        CTo[:], pbank[bk][:, :4 * NLOC * 8].rearrange("p (a b) -> p a b", b=8),
                        -1.0 / SCALE, None, ALU.mult), reads=[Bp[bk]], writes=[Btab])
                    if DEBUG:
                        dC = nc.dram_tensor(f"dbg_CTn{l}", [128, NKT * 8], F32, kind="ExternalOutput")
                        dR = nc.dram_tensor(f"dbg_Rbc{l}", [128, 8 * NSUB], F32, kind="ExternalOutput")
                        dO = nc.dram_tensor(f"dbg_CTo{l}", [128, 4 * NLOC * 8], F32, kind="ExternalOutput")
                        dcF = nc.dram_tensor(f"dbg_cF{l}", [8, NTOK], F32, kind="ExternalOutput")
                        SPQ.dma(dC.ap(), CTn[:].rearrange("p a b -> p (a b)"), reads=[Btab], writes=[B["out"]])
                        SPQ.dma(dR.ap(), Rbc[:], reads=[Btab], writes=[B["out"]])
                        SPQ.dma(dO.ap(), CTo[:].rearrange("p a b -> p (a b)"), reads=[Btab], writes=[B["out"]])
                        SPQ.dma(dcF.ap(), cF[:], reads=[Bcf], writes=[B["out"]])
                    barrier()

                kT = [sb(f"p2_kT{i}", [128, NTOK], BF16, st) for i in range(2)]
                BkT = [Buf(), Buf()]
                kown = [sb(f"p2_ko{i}", [128, NXL], BF16, st) for i in range(2)]
                Bko = [Buf(), Buf()]
                vT = sb("p2_vT", [128, NTOK], BF16, st)
                BvT = Buf()
                vownT = sb("p2_voT", [128, NXL], BF16, st)
                BvoT = Buf()
                V = sb("p2_V", [128, NKT, 128], BF16, st)
                BV = Buf()
                Vo = sb("p2_Vo", [128, 4 * NLOC, 128], BF16, st)
                BVo = Buf()
                qs = [sb(f"p2_q{i}", [128, CH], BF16, st) for i in range(2)]
                Bq = [Buf(), Buf()]
                Rt = [sb(f"p2_Rt{i}", [128, CH], F32, st) for i in range(2)]
                BRt = [Buf(), Buf()]
                CTl = [sb(f"p2_CTl{i}", [128, NKT], F32, st) for i in range(2)]
                BCl = [Buf(), Buf()]
                Tb = [sb(f"p2_T{i}", [128, CH], F32, st) for i in range(4)]
                BT = [Buf() for _ in range(4)]
                Pb = [sb(f"p2_P{i}", [128, CH], BF16, st) for i in range(5)]
                BP = [Buf() for _ in range(5)]
                LA = 3
                pending = []
                Osb = sb("p2_Osb", [128, CH], F32, st)
                d2 = sb("p2_d2", [128, CH], F32, st)
                sqo = sb("p2_sqo", [128, CH], BF16, st)
                uu = sb("p2_uu", [128, CH], F32, st)
                yst = [sb(f"p2_y{i}", [128, CH], BF16, st) for i in range(2)]
                BOs, Bd2, Bsqo, Buu = Buf(), Buf(), Buf(), Buf()
                Byst = [Buf(), Buf()]
                TRB = [7, 6]
                pb67b = [pbank[7][:].bitcast(BF16), pbank[6][:].bitcast(BF16)]

                def head_jobs(h):
                    kb, Bk = kT[h % 2], BkT[h % 2]
                    hs = slice(h * 128, (h + 1) * 128)
                    jobs = []

                    def J(out, in_, r, w):
                        jobs.append(lambda: SPQ.dma(out, in_, reads=r, writes=w))
                    J(vT[:, :NMETA], vmeta_d.ap()[hs, :], [B["vmeta"]], [BvT])
                    for jj in range(NG):
                        rho, lp = chunk_owner(jj)
                        rs_ = slice(rho * 1024 + h * 128, rho * 1024 + (h + 1) * 128)
                        J(vT[:, NMETA + jj * CH:NMETA + (jj + 1) * CH], vg_out[lp].ap()[rs_, :],
                          [B[("vg_out", lp)]], [BvT])
                    for lam in range(NLOC):
                        J(vownT[:, lam * CH:(lam + 1) * CH], vg_in[lam].ap()[hs, :], [B[("vg_in", lam)]], [BvoT])
                    J(kb[:, :NMETA], kmeta_d.ap()[hs, :], [B["kmeta"]], [Bk])
                    for jj in range(NG):
                        rho, lp = chunk_owner(jj)
                        rs_ = slice(rho * 1024 + h * 128, rho * 1024 + (h + 1) * 128)
                        J(kb[:, NMETA + jj * CH:NMETA + (jj + 1) * CH], kg_out[lp].ap()[rs_, :],
                          [B[("kg_out", lp)]], [Bk])
                    for lam in range(NLOC):
                        J(kown[h % 2][:, lam * CH:(lam + 1) * CH], kg_in[lam].ap()[hs, :], [B[("kg_in", lam)]],
                          [Bko[h % 2]])
                    return jobs

                def load_head(h):
                    for j_ in head_jobs(h):
                        j_()

                seglist_all = []
                for u_ in range(NLOC):
                    if u_ == 0:
                        seglist_all.append((u_, "meta", 0, NMETA))
                    seglist_all.append((u_, "chunk", NMETA + u_ * CH, CH))
                nsegs = len(seglist_all)

                def load_q(si_, h_, t0_, nq_):
                    SPQ.dma(qs[si_ % 2][:, :nq_], qT_d.ap()[h_ * 128:(h_ + 1) * 128, t0_:t0_ + nq_], reads=[B["qT"]],
                            writes=[Bq[si_ % 2]])

                tcount = [0]
                segcount = [0]
                load_head(0)
                load_q(1, 0, seglist_all[0][2], seglist_all[0][3])
                for h in range(NH):
                    while pending:
                        pending.pop(0)()
                    ti = 0
                    for b0 in range(0, NKT, 8):
                        cnt = min(8, NKT - b0)
                        pi = ti % 2
                        ti += 1
                        for i in range(cnt):
                            kt = b0 + i
                            k0, nk = (0, NMETA) if kt == 0 else (NMETA + (kt - 1) * 128, 128)
                            op(PE, lambda: nc.tensor.transpose(pb67b[pi][:nk, i * 128:(i + 1) * 128],
                                                               vT[:, k0:k0 + nk], identb[:]),
                               reads=[BvT, Bc], writes=[Bp[TRB[pi]]], sig=(i == cnt - 1))
                        evac(V[:, b0:b0 + cnt, :], pb67b[pi][:, :cnt * 128].rearrange("p (a d) -> p a d", d=128),
                             [Bp[TRB[pi]]], [BV])
                    for b0 in range(0, 4 * NLOC, 8):
                        cnt = min(8, 4 * NLOC - b0)
                        pi = ti % 2
                        ti += 1
                        for i in range(cnt):
                            op(PE, lambda: nc.tensor.transpose(pb67b[pi][:, i * 128:(i + 1) * 128],
                                                               vownT[:, (b0 + i) * 128:(b0 + i + 1) * 128], identb[:]),
                               reads=[BvoT, Bc], writes=[Bp[TRB[pi]]], sig=(i == cnt - 1))
                        evac(Vo[:, b0:b0 + cnt, :], pb67b[pi][:, :cnt * 128].rearrange("p (a d) -> p a d", d=128),
                             [Bp[TRB[pi]]], [BVo])
                    nxt_jobs = head_jobs(h + 1) if h + 1 < NH else []
                    per_seg = -(-len(nxt_jobs) // nsegs)
                    kb, Bk = kT[h % 2], BkT[h % 2]
                    ko, Bkow = kown[h % 2], Bko[h % 2]
                    for sidx, (u, skind, t0, nq) in enumerate(seglist_all):
                        if True:
                            si = segcount[0] = segcount[0] + 1
                            conv_slice(2)
                            for _ in range(min(per_seg, len(nxt_jobs))):
                                nxt_jobs.pop(0)()
                            if sidx + 1 < nsegs:
                                load_q(si + 1, h, seglist_all[sidx + 1][2], seglist_all[sidx + 1][3])
                            elif h + 1 < NH:
                                load_q(si + 1, h + 1, seglist_all[0][2], seglist_all[0][3])
                            q, Bqq = qs[si % 2], Bq[si % 2]
                            rt, Brt = Rt[si % 2], BRt[si % 2]
                            ctl, Bcl = CTl[si % 2], BCl[si % 2]
                            ob, db = 4, 5
                            tiles = []
                            if skind == "meta":
                                op(DVE, lambda: nc.vector.tensor_scalar(rt[:, :nq], zerosf[:, :nq],
                                                                        Rbc[:, h * NSUB:h * NSUB + 1], None, ALU.add),
                                   reads=[Btab, Bc], writes=[Brt])
                                tiles.append((kb[:, 0:NMETA], V[:NMETA, 0, :], CTn[:NMETA, 0, h:h + 1], NMETA, 0,
                                              [Bk], [BV], [Btab]))
                            else:
                                lam = u
                                for sj in range(4):
                                    s = 1 + 4 * lam + sj
                                    op(DVE, lambda: nc.vector.tensor_scalar(
                                        rt[:, sj * 128:(sj + 1) * 128], zerosf[:, :128],
                                        Rbc[:, h * NSUB + s:h * NSUB + s + 1], None, ALU.add),
                                       reads=[Btab, Bc], writes=[Brt])
                                npast = 1 + 4 * jmax(lam)
                                op(DVE, lambda: nc.vector.tensor_tensor(ctl[:, :npast], CTn[:, :npast, h],
                                                                        pen[:, lam, :npast], ALU.add),
                                   reads=[Btab, Bc], writes=[Bcl])
                                tiles.append((kb[:, 0:NMETA], V[:NMETA, 0, :], ctl[:NMETA, 0:1], NMETA, None,
                                              [Bk], [BV], [Bcl]))
                                for kt in range(1, npast):
                                    k0 = NMETA + (kt - 1) * 128
                                    tiles.append((kb[:, k0:k0 + 128], V[:, kt, :], ctl[:, kt:kt + 1], 128, None,
                                                  [Bk], [BV], [Bcl]))
                                for i in range(4):
                                    k0 = lam * CH + i * 128
                                    tiles.append((ko[:, k0:k0 + 128], Vo[:, 4 * lam + i, :],
                                                  CTo[:, 4 * lam + i, h:h + 1], 128, i, [Bkow], [BVo], [Btab]))
                            nt = len(tiles)

                            def emit_S(i):
                                kap, vap, bcol, nk, dg, rk, rv, rb = tiles[i]
                                sbk = (tcount[0] + i) % 4
                                op(PE, lambda: nc.tensor.matmul(pbank[sbk][:nk, :nq], lhsT=kap, rhs=q[:, :nq],
                                                                start=True, stop=True),
                                   reads=rk + [Bqq], writes=[Bp[sbk]])

                            for i in range(min(LA, nt)):
                                emit_S(i)
                            for i in range(nt):
                                if i + LA < nt:
                                    emit_S(i + LA)
                                kap, vap, bcol, nk, dg, rk, rv, rb = tiles[i]
                                g = tcount[0] + i
                                sbk = g % 4
                                T_, BT_ = Tb[g % 4], BT[g % 4]
                                P_, BP_ = Pb[g % 5], BP[g % 5]
                                if i == min(4, nt - 1) and pending:
                                    pending.pop(0)()
                                c0 = 0 if (dg is None or skind == "meta") else dg * 128
                                op(DVE, lambda: nc.vector.scalar_tensor_tensor(T_[:nk, c0:nq], pbank[sbk][:nk, c0:nq],
                                                                               bcol, rt[:nk, c0:nq], ALU.add, ALU.add),
                                   reads=[Bp[sbk], Brt] + rb, writes=[BT_])
                                op(ACT, lambda: nc.scalar.activation(P_[:nk, c0:nq], T_[:nk, c0:nq], AF.Exp, scale=SCALE),
                                   reads=[BT_], writes=[BP_])
                                if dg is not None:
                                    if c0 > 0:
                                        op(POOL, lambda: nc.gpsimd.memset(P_[:nk, :c0], 0.0), reads=[], writes=[BP_])
                                    w = min(128, nq)
                                    op(POOL, lambda: nc.gpsimd.tensor_tensor(P_[:nk, c0:c0 + w], P_[:nk, c0:c0 + w],
                                                                             tri[:nk, :w], ALU.mult),
                                       reads=[Bc], writes=[BP_])
                                op(PE, lambda: nc.tensor.matmul(pbank[ob][:, :nq], lhsT=vap, rhs=P_[:nk, :nq],
                                                                start=(i == 0), stop=(i == nt - 1)),
                                   reads=rv + [BP_], writes=[Bp[ob]], sig=False)
                                op(PE, lambda: nc.tensor.matmul(pbank[db][:, :nq], lhsT=ones1[:nk, :], rhs=P_[:nk, :nq],
                                                                start=(i == 0), stop=(i == nt - 1)),
                                   reads=[Bc, BP_], writes=[Bp[db]], sig=True)
                            tcount[0] += nt
                            ys, Bys = yst[si % 2], Byst[si % 2]
                            op(ACT, lambda: nc.scalar.activation(d2[:, :nq], pbank[db][:, :nq], AF.Ln), reads=[Bp[db]],
                               writes=[Bd2])
                            op(ACT, lambda: nc.scalar.activation(d2[:, :nq], d2[:, :nq], AF.Exp, scale=-1.0), reads=[Bd2],
                               writes=[Bd2])
                            op(DVE, lambda: nc.vector.tensor_tensor(Osb[:, :nq], pbank[ob][:, :nq], d2[:, :nq], ALU.mult),
                               reads=[Bp[ob], Bd2], writes=[BOs])

                            def ep_tail(ys=ys, Bys=Bys, nq=nq, t0=t0, h=h):
                                op(ACT, lambda: nc.scalar.activation(sqo[:, :nq], Osb[:, :nq], AF.Square),
                                   reads=[BOs], writes=[Bsqo])
                                op(PE, lambda: nc.tensor.matmul(pbank[6][:, :nq], lhsT=onesG[:], rhs=sqo[:, :nq],
                                                                start=True, stop=True),
                                   reads=[Bsqo, Bc], writes=[Bp[6]])
                                rsqrt_act(uu[:, :nq], pbank[6][:, :nq], EPS, [Bp[6]], [Buu])
                                op(DVE, lambda: nc.vector.scalar_tensor_tensor(ys[:, :nq], Osb[:, :nq],
                                                                               vcol(80 + l * 16 + h), uu[:, :nq],
                                                                               ALU.mult, ALU.mult),
                                   reads=[BOs, Buu, Bc], writes=[Bys])
                                SPQ.dma(yT_d.ap()[h * 128:(h + 1) * 128, t0:t0 + nq], ys[:, :nq], reads=[Bys],
                                        writes=[B["yT"]])

                            pending.append(ep_tail)
                            if nt <= 4:
                                while pending:
                                    pending.pop(0)()
                while pending:
                    pending.pop(0)()

        def phase34(l, last):
            with ExitStack() as st:
                hT1 = sb("p3_hT1", [128, KC, TUM], F32, st)
                Bh = [Buf() for _ in range(KC)]
                yTs = sb("p3_yTs", [128, KC, TUM], BF16, st)
                By = Buf()
                aT = sb("p3_aT", [128, FC, TUM], BF16, st)
                Ba = [Buf() for _ in range(FC)]
                wA = [sb(f"p3_wA{i}", [128, KC, 128], BF16, st) for i in range(2)]
                wB_ = [sb(f"p3_wB{i}", [128, KC, 128], BF16, st) for i in range(2)]
                BwA = [Buf(), Buf()]
                BwB = [Buf(), Buf()]
                wD = [sb(f"p3_wD{i}", [128, FC, 128], BF16, st) for i in range(2)]
                BwD = [Buf(), Buf()]
                wO = [sb(f"p3_wO{i}", [128, 2, KC, 128], BF16, st) for i in range(2)]
                BwO = [Buf(), Buf()]
                uh = sb("p3_uh", [128, 8, 2 + CH], BF16, st)
                Buh = Buf()
                gb = sb("p3_gb", [128, 8, TUM], BF16, st)
                Bgb = Buf()
                t1 = sb("p3_t1", [128, CH], F32, st)
                cv = sb("p3_cv", [128, CH], F32, st)
                sqc = sb("p3_sqc", [128, CH], BF16, st)
                rsx = sb("p3_rsx", [128, TUM], F32, st)
                sg = [sb(f"p3_sg{i}", [128, CH], F32, st) for i in range(2)]
                Bt1, Bcv, Bsqc, Brsx = Buf(), Buf(), Buf(), Buf()
                Bsg = [Buf(), Buf()]
                tl = sb("p3_tl", [128, NRANK, 8, 2 * NLOC], BF16, st)
                mt = sb("p3_mt", [128, 8, 2], BF16, st)
                hal = sb("p3_hal", [128, 8, 2], F32, st)
                Btl, Bhal = Buf(), Buf()
                if last:
                    otile = [sb(f"p3_ot{i}", [128, D // 2], F32, st) for i in range(2)]
                    Bot = [Buf(), Buf()]
                SPQ.dma(tl[:], ut_out.ap().rearrange("(r g p) c -> p r g c", r=NRANK, g=8), reads=[B["ut_out"]],
                        writes=[Btl])
                SPQ.dma(mt[:], uT_d.ap()[:, NMETA - 2:NMETA].rearrange("(g p) c -> p g c", p=128), reads=[B["uT"]],
                        writes=[Btl])
                ycv = sb("p3_ycv", [128, 8, TUM], BF16, st)
                Bycv = Buf()
                cwb = 112 + l * 24
                oti = [0]
                sgi = [0]
                def conv_gb(u):
                    t0, segs, TU = unit_info(u)
                    SPQ.dma(gb[:, :, :TU], gbT_d.ap()[:, t0:t0 + TU].rearrange("(k p) t -> p k t", p=128),
                            reads=[B["gbT"]], writes=[Bgb])

                def conv_pre(u, so, n):
                    t0, segs, TU = unit_info(u)
                    if True:
                        SPQ.dma(uh[:, :, 2:2 + n], uT_d.ap()[:, t0 + so:t0 + so + n].rearrange("(k p) t -> p k t", p=128),
                                reads=[B["uT"]], writes=[Buh])
                        if n == NMETA:
                            op(DVE, lambda: nc.vector.memset(uh[:, :, 0:2], 0.0), reads=[], writes=[Buh])
                        else:
                            lam = u
                            for r4 in range(4):
                                j = chunk_global(r4, lam)
                                if j == 0:
                                    cand = mt[:]
                                else:
                                    rho, lp = chunk_owner(j - 1)
                                    cand = tl[:, rho, :, 2 * lp:2 * lp + 2]
                                if r4 == 0:
                                    op(DVE, lambda: nc.vector.tensor_scalar(hal[:], cand, oh[:, 0:1], None, ALU.mult),
                                       reads=[Btl, Bc], writes=[Bhal])
                                else:
                                    op(DVE, lambda: nc.vector.scalar_tensor_tensor(hal[:], cand, oh[:, r4:r4 + 1], hal[:],
                                                                                   ALU.mult, ALU.add),
                                       reads=[Btl, Bc, Bhal], writes=[Bhal])
                            op(DVE, lambda: nc.vector.tensor_copy(uh[:, :, 0:2], hal[:]), reads=[Bhal], writes=[Buh])

                def conv_a(u, so, n, g):
                    if True:
                        if True:
                            op(DVE, lambda: nc.vector.tensor_scalar(t1[:, :n], uh[:, g, 0:n], vcol(cwb + g), None, ALU.mult),
                               reads=[Buh, Bc], writes=[Bt1])
                            op(DVE, lambda: nc.vector.scalar_tensor_tensor(t1[:, :n], uh[:, g, 1:n + 1], vcol(cwb + 8 + g),
                                                                           t1[:, :n], ALU.mult, ALU.add),
                               reads=[Buh, Bc, Bt1], writes=[Bt1])
                            op(DVE, lambda: nc.vector.scalar_tensor_tensor(t1[:, :n], uh[:, g, 2:n + 2], vcol(cwb + 16 + g),
                                                                           t1[:, :n], ALU.mult, ALU.add),
                               reads=[Buh, Bc, Bt1], writes=[Bt1])
                            op(DVE, lambda: nc.vector.tensor_tensor(cv[:, :n], t1[:, :n], gb[:, g, so:so + n], ALU.mult),
                               reads=[Bt1, Bgb], writes=[Bcv])
                            op(ACT, lambda: nc.scalar.activation(sqc[:, :n], cv[:, :n], AF.Square),
                               reads=[Bcv], writes=[Bsqc])

                def conv_b(u, so, n, g):
                    if True:
                        if True:
                            op(PE, lambda: nc.tensor.matmul(pbank[6][:, :n], lhsT=onesG[:], rhs=sqc[:, :n],
                                                            start=True, stop=True),
                               reads=[Bsqc, Bc], writes=[Bp[6]])
                            rsqrt_act(rsx[:, :n], pbank[6][:, :n], EPS, [Bp[6]], [Brsx])
                            op(DVE, lambda: nc.vector.scalar_tensor_tensor(ycv[:, g, so:so + n], cv[:, :n],
                                                                           vcol(80 + l * 16 + 8 + g), rsx[:, :n],
                                                                           ALU.mult, ALU.mult),
                               reads=[Bcv, Brsx, Bc], writes=[Bycv])

                def conv(u):
                    conv_gb(u)
                    for (so_, n_) in unit_info(u)[1]:
                        conv_pre(u, so_, n_)
                        for g_ in range(8):
                            conv_a(u, so_, n_, g_)
                            conv_b(u, so_, n_, g_)

                conv(0)
                for u in range(NLOC):
                    t0, segs, TU = unit_info(u)
                    SPQ.dma(hT1[:, :, :TU], hT_d.ap()[:, t0:t0 + TU].rearrange("(k p) t -> p k t", p=128),
                            reads=[B["hT"]], writes=Bh)
                    if u == 0:
                        SPQ.dma(yTs[:, 0:8, :TU], yT_d.ap()[:, t0:t0 + TU].rearrange("(k p) t -> p k t", p=128),
                                reads=[B["yT"]], writes=[By])

                    def load_wo(i):
                        SPQ.dma(wO[i % 2][:], wob[l].ap()[2 * i:2 * i + 2].rearrange("b p k j -> p b k j"),
                                reads=[Bw[("wob", l)]], writes=[BwO[i % 2]])

                    load_wo(0)
                    def load_gu(f):
                        SPQ.dma(wA[f % 2][:], wgb[l].ap()[f], reads=[Bw[("wgb", l)]], writes=[BwA[f % 2]])
                        SPQ.dma(wB_[f % 2][:], wub[l].ap()[f], reads=[Bw[("wub", l)]], writes=[BwB[f % 2]])

                    load_gu(0)
                    def load_d(m):
                        SPQ.dma(wD[m % 2][:], wdb[l].ap()[m], reads=[Bw[("wdb", l)]], writes=[BwD[m % 2]])

                    load_d(0)
                    for i in range(8):
                        if i + 1 < 8:
                            load_wo(i + 1)
                        for b2 in range(2):
                            m = 2 * i + b2
                            for (so, n) in segs:
                                bk = next_acc()
                                for kc in range(KC):
                                    rhs_ = yTs[:, kc, so:so + n] if kc < 8 else ycv[:, kc - 8, so:so + n]
                                    op(PE, lambda: nc.tensor.matmul(pbank[bk][:, :n], lhsT=wO[i % 2][:, b2, kc, :],
                                                                    rhs=rhs_,
                                                                    start=(kc == 0), stop=(kc == KC - 1)),
                                       reads=[BwO[i % 2], By if kc < 8 else Bycv], writes=[Bp[bk]],
                                       sig=(kc == KC - 1))
                                op(DVE, lambda: nc.vector.tensor_tensor(hT1[:, m, so:so + n], hT1[:, m, so:so + n],
                                                                        pbank[bk][:, :n], ALU.add),
                                   reads=[Bp[bk]], writes=[Bh[m]])

                    def rms_stats():
                        for kc in range(KC):
                            op(ACT, lambda: nc.scalar.activation(aT[:, kc, :TU], hT1[:, kc, :TU], AF.Square),
                               reads=[Bh[kc]], writes=[Ba[kc]])
                        for (so, n) in segs:
                            for kc in range(KC):
                                op(PE, lambda: nc.tensor.matmul(pbank[6][:, :n], lhsT=onesD[:], rhs=aT[:, kc, so:so + n],
                                                                start=(kc == 0), stop=(kc == KC - 1)),
                                   reads=[Ba[kc], Bc], writes=[Bp[6]], sig=(kc == KC - 1))
                            rsqrt_act(rsx[:, so:so + n], pbank[6][:, :n], EPS, [Bp[6]], [Brsx])

                    rms_stats()
                    for kc in range(KC):
                        op(DVE, lambda: nc.vector.scalar_tensor_tensor(yTs[:, kc, :TU], hT1[:, kc, :TU],
                                                                       vcol(32 + l * 16 + kc), rsx[:, :TU],
                                                                       ALU.mult, ALU.mult),
                           reads=[Bh[kc], Brsx, Bc], writes=[By])
                    for f in range(FC):
                        conv_slice(1)
                        if f + 1 < FC:
                            load_gu(f + 1)
                        for (so, n) in segs:
                            bg_, bu_ = next_acc(), next_acc()
                            for kc in range(KC):
                                op(PE, lambda: nc.tensor.matmul(pbank[bg_][:, :n], lhsT=wA[f % 2][:, kc, :],
                                                                rhs=yTs[:, kc, so:so + n], start=(kc == 0),
                                                                stop=(kc == KC - 1)),
                                   reads=[BwA[f % 2], By], writes=[Bp[bg_]], sig=(kc == KC - 1))
                            for kc in range(KC):
                                op(PE, lambda: nc.tensor.matmul(pbank[bu_][:, :n], lhsT=wB_[f % 2][:, kc, :],
                                                                rhs=yTs[:, kc, so:so + n], start=(kc == 0),
                                                                stop=(kc == KC - 1)),
                                   reads=[BwB[f % 2], By], writes=[Bp[bu_]], sig=(kc == KC - 1))
                            k_ = sgi[0] = (sgi[0] + 1) % 2
                            op(ACT, lambda: nc.scalar.activation(sg[k_][:, :n], pbank[bg_][:, :n], AF.Silu),
                               reads=[Bp[bg_]], writes=[Bsg[k_]])
                            op(DVE, lambda: nc.vector.tensor_tensor(aT[:, f, so:so + n], sg[k_][:, :n], pbank[bu_][:, :n],
                                                                    ALU.mult),
                               reads=[Bsg[k_], Bp[bu_]], writes=[Ba[f]])
                    ovl = u + 1 < NLOC
                    if ovl:
                        t0n, segsn, TUn = unit_info(u + 1)
                        SPQ.dma(yTs[:, 0:8, :TUn], yT_d.ap()[:, t0n:t0n + TUn].rearrange("(k p) t -> p k t", p=128),
                                reads=[B["yT"]], writes=[By])
                        conv_gb(u + 1)
                        conv_pre(u + 1, 0, CH)
                    for m in range(16):
                        if m + 1 < 16:
                            load_d(m + 1)
                        if ovl and m % 2 == 0:
                            conv_a(u + 1, 0, CH, m // 2)
                        if ovl and m % 2 == 1:
                            conv_b(u + 1, 0, CH, m // 2)
                        for (so, n) in segs:
                            bk = next_acc()
                            for f in range(FC):
                                op(PE, lambda: nc.tensor.matmul(pbank[bk][:, :n], lhsT=wD[m % 2][:, f, :],
                                                                rhs=aT[:, f, so:so + n], start=(f == 0), stop=(f == FC - 1)),
                                   reads=[BwD[m % 2], Ba[f]], writes=[Bp[bk]], sig=(f == FC - 1))
                            op(DVE, lambda: nc.vector.tensor_tensor(hT1[:, m, so:so + n], hT1[:, m, so:so + n],
                                                                    pbank[bk][:, :n], ALU.add),
                               reads=[Bp[bk]], writes=[Bh[m]])
                    if not last:
                        SPQ.dma(hT_d.ap()[:, t0:t0 + TU].rearrange("(k p) t -> p k t", p=128), hT1[:, :, :TU],
                                reads=Bh, writes=[B["hT"]])
                    else:
                        rms_stats()
                        for kc in range(KC):
                            op(DVE, lambda: nc.vector.scalar_tensor_tensor(hT1[:, kc, :TU], hT1[:, kc, :TU],
                                                                           vcol(64 + kc), rsx[:, :TU], ALU.mult, ALU.mult),
                               reads=[Brsx, Bc], writes=[Bh[kc]])
                        so = TU - CH
                        for s4 in range(4):
                            for half in range(2):
                                oi = oti[0] = (oti[0] + 1) % 2
                                for k2 in range(2):
                                    k4 = 2 * half + k2
                                    bk = next_acc()
                                    for kk in range(4):
                                        kc = 4 * k4 + kk
                                        op(PE, lambda: nc.tensor.transpose(pbank[bk][:, kk * 128:(kk + 1) * 128],
                                                                           hT1[:, kc, so + s4 * 128:so + (s4 + 1) * 128],
                                                                           identf[:]),
                                           reads=[Bh[kc], Bc], writes=[Bp[bk]], sig=(kk == 3))
                                    evac(otile[oi][:, k2 * 512:(k2 + 1) * 512], pbank[bk][:, :], [Bp[bk]], [Bot[oi]])
                                r0 = u * CH + s4 * 128
                                SPQ.dma(out_d.ap()[r0:r0 + 128, half * 1024:(half + 1) * 1024], otile[oi][:],
                                        reads=[Bot[oi]], writes=[B["out"]])

        for l in range(NLAYERS):
            phase1(l)
            barrier()
            phase2(l)
            conv_slice(10 ** 6)
            if l == 0 and NLAYERS > 1:
                convert(1, defer=True)
            barrier()
            phase34(l, l == NLAYERS - 1)
            conv_slice(10 ** 6)
            barrier()
    return nc


_PROG_CACHE = {}


def _run(inputs, NLOC, NLAYERS):
    f32 = np.float32
    x = np.asarray(inputs["x"], f32)
    meta = np.ascontiguousarray(np.asarray(inputs["meta"], f32))
    norm_mix = np.asarray(inputs["norm_mix"], f32)
    b_f = np.asarray(inputs["b_f"], f32)
    conv_w = np.asarray(inputs["conv_w"], f32)
    out_gain = np.asarray(inputs["out_gain"], f32)
    norm_ffn = np.asarray(inputs["norm_ffn"], f32)
    final_norm = np.asarray(inputs["final_norm"], f32)
    wts = {k: np.ascontiguousarray(np.asarray(inputs[k], f32)) for k in ["w_in", "w_out", "w_gate", "w_up", "w_down"]}
    NG = 4 * NLOC
    NKT = 1 + 4 * NG
    vecs = np.zeros((128, NV), f32)
    for l in range(2):
        vecs[:, l * 16:(l + 1) * 16] = norm_mix[l].reshape(16, 128).T
        vecs[:, 32 + l * 16:32 + (l + 1) * 16] = norm_ffn[l].reshape(16, 128).T
        vecs[:, 80 + l * 16:80 + (l + 1) * 16] = out_gain[l].reshape(16, 128).T
        for k in range(3):
            vecs[:, 112 + l * 24 + k * 8:112 + l * 24 + (k + 1) * 8] = conv_w[l, k].reshape(8, 128).T
    vecs[:, 64:80] = final_norm.reshape(16, 128).T
    bf = np.ascontiguousarray(b_f.T)
    identf = np.eye(128, dtype=f32)
    identb = np.eye(128).astype(ml_dtypes.bfloat16)
    tri = np.triu(np.ones((128, 128))).astype(ml_dtypes.bfloat16)
    in_maps = []
    for c in range(8):
        b, r = divmod(c, 4)
        xl = np.concatenate([x[b, chunk_global(r, lam) * CH:(chunk_global(r, lam) + 1) * CH] for lam in range(NLOC)], 0)
        ohm = np.zeros((128, 4), f32)
        ohm[:, r] = 1.0
        penm = np.zeros((NLOC, NKT), f32)
        for lam in range(NLOC):
            j = chunk_global(r, lam)
            for kt in range(1, NKT):
                if (kt - 1) // 4 >= j:
                    penm[lam, kt] = -BIG
        penb = np.ascontiguousarray(np.broadcast_to(penm.reshape(1, -1), (128, NLOC * NKT)))
        m = {"x": np.ascontiguousarray(xl), "meta": meta, "vecs": vecs, "bf": bf, "identf": identf, "identb": identb,
             "tri": tri, "oh": ohm, "pen": penb}
        m.update(wts)
        in_maps.append(m)
    key = (NLOC, NLAYERS)
    if key not in _PROG_CACHE:
        _PROG_CACHE[key] = build(NLOC, NLAYERS)
    nc = _PROG_CACHE[key]
    res = run_bass_kernel_spmd(nc, in_maps, core_ids=list(range(8)))
    if DEBUG:
        _run.last = res.results
    out = np.zeros((2, NG * CH, D), f32)
    for c in range(8):
        b, r = divmod(c, 4)
        o = np.asarray(res.results[c]["out"])
        for lam in range(NLOC):
            j = chunk_global(r, lam)
            out[b, j * CH:(j + 1) * CH] = o[lam * CH:(lam + 1) * CH]
    return out


def kernel(**inputs):
    return _run(inputs, 8, 2)
```

```python
from contextlib import ExitStack
import numpy as np
import ml_dtypes
import concourse.bass as bass
import concourse.mybir as mybir
from concourse.bass_utils import run_bass_kernel_spmd

F32, BF16 = mybir.dt.float32, mybir.dt.bfloat16
AF = mybir.ActivationFunctionType
ALU = mybir.AluOpType

D = 2048
KC = 16
NH = 8
DFF = 5632
FC = 44
DIN = 6152
NMETA = 16
CH = 512
EPS = 1e-6
SCALE = 128 ** -0.5
NRANK = 4


class Sem:
    def __init__(self, h):
        self.h = h
        self.val = 0


class Buf:
    __slots__ = ("w", "r")

    def __init__(self):
        self.w = {}
        self.r = {}


class Eng:
    def __init__(self, name, eng, sem):
        self.name, self.eng, self.sem, self.cnt, self.seen = name, eng, sem, 0, {}
        self.is_pe = name == "pe"

    def wait_t(self, s, v):
        if v <= 0:
            return
        if s is self.sem:
            if self.is_pe or v > self.cnt:
                return
        if self.seen.get(s, 0) >= v:
            return
        self.eng.wait_ge(s.h, v)
        self.seen[s] = v

    def wait_bufs(self, reads, writes):
        for b in reads:
            for s, v in b.w.items():
                self.wait_t(s, v)
        for b in writes:
            for s, v in b.w.items():
                self.wait_t(s, v)
            for s, v in b.r.items():
                self.wait_t(s, v)


def _commit(s, t, reads, writes):
    for b in writes:
        if b.w.get(s, 0) < t:
            b.w[s] = t
    for b in reads:
        if b.r.get(s, 0) < t:
            b.r[s] = t


def op(E, fn, reads=(), writes=(), sig=True):
    E.wait_bufs(reads, writes)
    ins = fn()
    if sig:
        ins.then_inc(E.sem.h, 1)
        E.cnt += 1
        t = E.cnt
    else:
        t = E.cnt + 1
    _commit(E.sem, t, reads, writes)


class DmaQ:
    def __init__(self, E, sems):
        self.E, self.sems, self.i = E, sems, 0

    def dma(self, out, in_, reads=(), writes=()):
        s = self.sems[self.i]
        self.i = (self.i + 1) % len(self.sems)
        self.E.wait_t(s, s.val)
        self.E.wait_bufs(reads, writes)
        self.E.eng.dma_start(out=out, in_=in_).then_inc(s.h, 16)
        s.val += 16
        _commit(s, s.val, reads, writes)


def chunk_owner(j):
    i, pos = divmod(j, 8)
    if pos < 4:
        return pos, 2 * i
    return 7 - pos, 2 * i + 1


def chunk_global(r, lam):
    return 8 * (lam // 2) + (r if lam % 2 == 0 else 7 - r)


W_IN_COLS = ([128 * i for i in range(8)] + [1024 + 128 * i for i in range(8)]
             + [2048 + 128 * i for i in range(8)] + [3072 + 128 * i for i in range(8)])
for _g in range(8):
    W_IN_COLS += [4096 + 128 * _g, 5120 + 128 * _g]
NV = 160
BIG = 1.0e5 / SCALE


DEBUG = False


def build(NLOC=8, NLAYERS=2):
    nc = bass.Bass("TRN2", target_bir_lowering=False)
    TL = NMETA + NLOC * CH
    NG = 4 * NLOC
    NTOK = NMETA + NG * CH
    NKT = 1 + 4 * NG
    NSUB = 1 + 4 * NLOC
    NXL = NLOC * CH

    def din(name, shape, dt=F32):
        return nc.dram_tensor(name, shape, dt, kind="ExternalInput")

    x_in = din("x", [NXL, D])
    meta_in = din("meta", [NMETA, D])
    w_in = din("w_in", [2, D, DIN])
    w_out = din("w_out", [2, D, D])
    w_gate = din("w_gate", [2, D, DFF])
    w_up = din("w_up", [2, D, DFF])
    w_down = din("w_down", [2, DFF, D])
    vecs_in = din("vecs", [128, NV])
    bf_in = din("bf", [8, 2])
    identf_in = din("identf", [128, 128])
    identb_in = din("identb", [128, 128], BF16)
    tri_in = din("tri", [128, 128], BF16)
    oh_in = din("oh", [128, 4])
    pen_in = din("pen", [128, NLOC * NKT])
    out_d = nc.dram_tensor("out", [NXL, D], F32, kind="ExternalOutput")

    def dscr(name, shape, dt):
        if DEBUG and not name.startswith("w") and not name.endswith("_in") and not name.endswith("_out"):
            return nc.dram_tensor(name, shape, dt, kind="ExternalOutput")
        return nc.dram_tensor(name, shape, dt)

    wib = [dscr(f"wib{l}", [48, 128, KC, 128], BF16) for l in range(2)]
    wfb = [dscr(f"wfb{l}", [128, KC, 8], BF16) for l in range(2)]
    wob = [dscr(f"wob{l}", [16, 128, KC, 128], BF16) for l in range(2)]
    wgb = [dscr(f"wgb{l}", [FC, 128, KC, 128], BF16) for l in range(2)]
    wub = [dscr(f"wub{l}", [FC, 128, KC, 128], BF16) for l in range(2)]
    wdb = [dscr(f"wdb{l}", [16, 128, FC, 128], BF16) for l in range(2)]
    hT_d = dscr("hT", [D, TL], F32)
    qT_d = dscr("qT", [1024, TL], BF16)
    kmeta_d = dscr("kmeta", [1024, NMETA], BF16)
    vmeta_d = dscr("vmeta", [1024, NMETA], BF16)
    kg_in = [dscr(f"kg{i}_in", [1024, CH], BF16) for i in range(NLOC)]
    vg_in = [dscr(f"vg{i}_in", [1024, CH], BF16) for i in range(NLOC)]
    kg_out = [dscr(f"kg{i}_out", [NRANK * 1024, CH], BF16) for i in range(NLOC)]
    vg_out = [dscr(f"vg{i}_out", [NRANK * 1024, CH], BF16) for i in range(NLOC)]
    gbT_d = dscr("gbT", [1024, TL], BF16)
    uT_d = dscr("uT", [1024, TL], BF16)
    ut_in = dscr("ut_in", [1024, 2 * NLOC], BF16)
    ut_out = dscr("ut_out", [NRANK * 1024, 2 * NLOC], BF16)
    lf_in = dscr("lf_in", [8, NXL], F32)
    lf_out = dscr("lf_out", [NRANK * 8, NXL], F32)
    lfm_d = dscr("lfm", [8, NMETA], F32)
    yT_d = dscr("yT", [1024, TL], BF16)

    B = {k: Buf() for k in ["hT", "qT", "kmeta", "vmeta", "kg_in", "vg_in", "kg_out", "vg_out", "gbT", "uT",
                            "ut_in", "ut_out", "lf_in", "lf_out", "lfm", "yT", "out", "const"]}
    for i_ in range(NLOC):
        for n_ in ["kg_in", "vg_in", "kg_out", "vg_out"]:
            B[(n_, i_)] = Buf()
    Bwg = {(l, g): Buf() for l in range(2) for g in range(12)}
    Bw = {(n, l): Buf() for n in ["wib", "wfb", "wob", "wgb", "wub", "wdb"] for l in range(2)}

    es = ExitStack()
    with es:
        def sem(name):
            return Sem(es.enter_context(nc.semaphore(name)))

        PE = Eng("pe", nc.tensor, sem("s_pe"))
        ACT = Eng("act", nc.scalar, sem("s_act"))
        DVE = Eng("dve", nc.vector, sem("s_dve"))
        POOL = Eng("pool", nc.gpsimd, sem("s_pool"))
        SP = Eng("sp", nc.sync, sem("s_sp"))
        SPQ = DmaQ(SP, [sem(f"dq{i}") for i in range(24)])
        PQ = DmaQ(POOL, [sem(f"pq{i}") for i in range(16)])
        conv_jobs = []
        ccsems = [sem(f"cc{i}") for i in range(8)]
        cc_i = [0]

        sb_n = [0]

        def sb(name, shape, dt, stack=None):
            sb_n[0] += 1
            return (stack or es).enter_context(nc.sbuf_tensor(f"sb{sb_n[0]}_{name}", shape, dt))

        pbank = [es.enter_context(nc.psum_tensor(f"pb{i}", [128, 512], F32)) for i in range(8)]
        Bp = [Buf() for _ in range(8)]

        identf = sb("identf", [128, 128], F32)
        identb = sb("identb", [128, 128], BF16)
        tri = sb("tri", [128, 128], BF16)
        vecs = sb("vecs", [128, NV], F32)
        bfv = sb("bfv", [8, 2], F32)
        nbf = sb("nbf", [8, 2], F32)
        oh = sb("oh", [128, 4], F32)
        pen = sb("pen", [128, NLOC, NKT], F32)
        onesD = sb("onesD", [128, 128], BF16)
        onesG = sb("onesG", [128, 128], BF16)
        ones1 = sb("ones1", [128, 128], BF16)
        ones8f = sb("ones8f", [8, 128], F32)
        zerosf = sb("zerosf", [128, 512], F32)
        Bc = B["const"]
        SPQ.dma(identf[:], identf_in.ap(), writes=[Bc])
        SPQ.dma(identb[:], identb_in.ap(), writes=[Bc])
        SPQ.dma(tri[:], tri_in.ap(), writes=[Bc])
        SPQ.dma(vecs[:], vecs_in.ap(), writes=[Bc])
        SPQ.dma(bfv[:], bf_in.ap(), writes=[Bc])
        SPQ.dma(oh[:], oh_in.ap(), writes=[Bc])
        SPQ.dma(pen[:], pen_in.ap().rearrange("p (a b) -> p a b", a=NLOC), writes=[Bc])
        op(DVE, lambda: nc.vector.memset(onesD[:], 1.0 / D), writes=[Bc])
        op(DVE, lambda: nc.vector.memset(onesG[:], 1.0 / 128), writes=[Bc])
        op(DVE, lambda: nc.vector.memset(ones1[:], 1.0), writes=[Bc])
        op(DVE, lambda: nc.vector.memset(ones8f[:], 1.0), writes=[Bc])
        op(DVE, lambda: nc.vector.memset(zerosf[:], 0.0), writes=[Bc])
        op(DVE, lambda: nc.vector.tensor_scalar(nbf[:], bfv[:], -1.0, None, ALU.mult), reads=[Bc], writes=[Bc])

        def vcol(c):
            return vecs[:, c:c + 1]

        def convert(l, defer=False):
            jobs = []

            class _Q:
                @staticmethod
                def dma(out, in_, writes):
                    jobs.append((out, in_, writes))
            PQ_ = _Q
            convert_body(l, PQ_)
            if defer:
                conv_jobs.extend(jobs)
            else:
                for (o_, i_, w_) in jobs[:49]:
                    PQ.dma(o_, i_, writes=w_)
                conv_jobs.extend(jobs[49:])

        def conv_slice(n):
            for _ in range(min(n, len(conv_jobs))):
                o_, i_, w_ = conv_jobs.pop(0)
                PQ.dma(o_, i_, writes=w_)

        def convert_body(l, PQ):
            for bi, c0 in enumerate(W_IN_COLS):
                PQ.dma(wib[l].ap()[bi], w_in.ap()[l, :, c0:c0 + 128].rearrange("(k p) j -> p k j", p=128),
                       writes=[Bwg[(l, bi // 4)]])
            PQ.dma(wfb[l].ap(), w_in.ap()[l, :, 6144:6152].rearrange("(k p) j -> p k j", p=128),
                   writes=[Bw[("wfb", l)]])
            for m in range(16):
                PQ.dma(wob[l].ap()[m], w_out.ap()[l, :, m * 128:(m + 1) * 128].rearrange("(k p) j -> p k j", p=128),
                       writes=[Bw[("wob", l)]])
            for f in range(FC):
                PQ.dma(wgb[l].ap()[f], w_gate.ap()[l, :, f * 128:(f + 1) * 128].rearrange("(k p) j -> p k j", p=128),
                       writes=[Bw[("wgb", l)]])
                PQ.dma(wub[l].ap()[f], w_up.ap()[l, :, f * 128:(f + 1) * 128].rearrange("(k p) j -> p k j", p=128),
                       writes=[Bw[("wub", l)]])
            for m in range(16):
                for half in range(2):
                    PQ.dma(wdb[l].ap()[m, :, half * 22:(half + 1) * 22, :],
                           w_down.ap()[l, half * 2816:(half + 1) * 2816, m * 128:(m + 1) * 128].rearrange(
                               "(k p) j -> p k j", p=128),
                           writes=[Bw[("wdb", l)]])

        convert(0)

        ev_i = [0]

        def evac(out, in_, reads, writes):
            ev_i[0] ^= 1
            if ev_i[0]:
                op(ACT, lambda: nc.scalar.copy(out, in_), reads, writes)
            else:
                op(DVE, lambda: nc.vector.tensor_copy(out, in_), reads, writes)

        def rsqrt_act(out, in_, eps, reads, writes):
            op(ACT, lambda: nc.scalar.activation(out, in_, AF.Ln, bias=eps, scale=1.0), reads, writes)
            op(ACT, lambda: nc.scalar.activation(out, out, AF.Exp, scale=-0.5), writes, writes)

        acc_i = [0]

        def next_acc(banks=(0, 1, 2, 3, 4, 5)):
            acc_i[0] = (acc_i[0] + 1) % len(banks)
            return banks[acc_i[0]]

        def unit_info(u):
            if u == 0:
                return 0, [(0, NMETA), (NMETA, CH)], NMETA + CH
            return NMETA + u * CH, [(0, CH)], CH

        def allgather(src, dst, bsrc, bdst):
            s = ccsems[cc_i[0] % len(ccsems)]
            cc_i[0] += 1
            POOL.wait_t(s, s.val)
            POOL.wait_bufs([bsrc], [bdst])
            nc.gpsimd.collective_compute("AllGather", ALU.bypass, replica_groups=[[0, 1, 2, 3], [4, 5, 6, 7]],
                                         ins=[src.ap().opt()], outs=[dst.ap().opt()]).then_inc(s.h, 1)
            s.val += 1
            _commit(s, s.val, [bsrc], [bdst])

        TUM = NMETA + CH

        def phase1(l):
            with ExitStack() as st:
                hT = sb("p1_hT", [128, KC, TUM], F32, st)
                Bh = [Buf() for _ in range(KC)]
                sq = sb("p1_sq", [128, KC, TUM], BF16, st)
                Bsq = [Buf() for _ in range(KC)]
                hns = [sb(f"p1_hn{i}", [128, KC, TUM], BF16, st) for i in range(2)]
                Bhns = [Buf(), Buf()]
                rstd = sb("p1_rstd", [128, TUM], F32, st)
                Brs = Buf()
                wt = [sb(f"p1_wt{i}", [128, 4, KC, 128], BF16, st) for i in range(2)]
                Bwt = [Buf(), Buf()]
                wf = sb("p1_wf", [128, KC, 8], BF16, st)
                Bwf = Buf()
                ost = [sb(f"p1_ost{i}", [128, TUM], BF16, st) for i in range(3)]
                Bost = [Buf() for _ in range(3)]
                gcs = sb("p1_gcs", [128, TUM], BF16, st)
                Bgcs = Buf()
                lfe = sb("p1_lfe", [8, TUM], F32, st)
                lfs = sb("p1_lfs", [8, TUM], F32, st)
                Blf = Buf()
                if l == 0:
                    xt = sb("p1_xt", [128, 4, D], F32, st)
                    Bxt = [Buf() for _ in range(4)]
                    xm = sb("p1_xm", [NMETA, D], F32, st)
                    Bxm = Buf()
                SPQ.dma(wf[:], wfb[l].ap(), reads=[Bw[("wfb", l)]], writes=[Bwf])
                ost_i = [0]
                def pro_load(u):
                    t0, segs, TU = unit_info(u)
                    if l == 0:
                        for (so, n) in segs:
                            if n == NMETA:
                                SPQ.dma(xm[:], meta_in.ap(), writes=[Bxm])
                            else:
                                for s4 in range(4):
                                    r0 = u * CH + s4 * 128
                                    SPQ.dma(xt[:, s4, :], x_in.ap()[r0:r0 + 128, :], writes=[Bxt[s4]])
                    else:
                        SPQ.dma(hT[:, :, :TU], hT_d.ap()[:, t0:t0 + TU].rearrange("(k p) t -> p k t", p=128),
                                reads=[B["hT"]], writes=Bh)

                def prologue(u):
                    t0, segs, TU = unit_info(u)
                    hn, Bhn = hns[u % 2], Bhns[u % 2]
                    if l == 0:
                        for (so, n) in segs:
                            if n == NMETA:
                                for kc in range(KC):
                                    op(PE, lambda: nc.tensor.transpose(pbank[6][:, kc * 16:(kc + 1) * 16],
                                                                       xm[:, kc * 128:(kc + 1) * 128],
                                                                       identf[:NMETA, :NMETA]),
                                       reads=[Bxm, Bc], writes=[Bp[6]], sig=(kc == KC - 1))
                                for kc in range(KC):
                                    evac(hT[:, kc, so:so + n], pbank[6][:, kc * 16:(kc + 1) * 16], [Bp[6]], [Bh[kc]])
                            else:
                                for kc in range(KC):
                                    bk = next_acc()
                                    for s4 in range(4):
                                        op(PE, lambda: nc.tensor.transpose(pbank[bk][:, s4 * 128:(s4 + 1) * 128],
                                                                           xt[:, s4, kc * 128:(kc + 1) * 128],
                                                                           identf[:]),
                                           reads=[Bxt[s4], Bc], writes=[Bp[bk]], sig=(s4 == 3))
                                    evac(hT[:, kc, so:so + n], pbank[bk][:, :n], [Bp[bk]], [Bh[kc]])
                        SPQ.dma(hT_d.ap()[:, t0:t0 + TU].rearrange("(k p) t -> p k t", p=128), hT[:, :, :TU],
                                reads=Bh, writes=[B["hT"]])
                    for kc in range(KC):
                        op(ACT, lambda: nc.scalar.activation(sq[:, kc, :TU], hT[:, kc, :TU], AF.Square),
                           reads=[Bh[kc]], writes=[Bsq[kc]])
                    for (so, n) in segs:
                        for kc in range(KC):
                            op(PE, lambda: nc.tensor.matmul(pbank[6][:, :n], lhsT=onesD[:], rhs=sq[:, kc, so:so + n],
                                                            start=(kc == 0), stop=(kc == KC - 1)),
                               reads=[Bsq[kc], Bc], writes=[Bp[6]], sig=(kc == KC - 1))
                        rsqrt_act(rstd[:, so:so + n], pbank[6][:, :n], EPS, [Bp[6]], [Brs])
                    for kc in range(KC):
                        op(DVE, lambda: nc.vector.scalar_tensor_tensor(hn[:, kc, :TU], hT[:, kc, :TU],
                                                                       vcol(l * 16 + kc), rstd[:, :TU],
                                                                       ALU.mult, ALU.mult),
                           reads=[Bh[kc], Brs, Bc], writes=[Bhn])
                def proj(u):
                    t0, segs, TU = unit_info(u)
                    hn, Bhn = hns[u % 2], Bhns[u % 2]
                    def load_w(bg):
                        SPQ.dma(wt[bg % 2][:], wib[l].ap()[4 * bg:4 * bg + 4].rearrange("b p k j -> p b k j"),
                                reads=[Bwg[(l, bg)]], writes=[Bwt[bg % 2]])

                    if u == 0:
                        load_w(0)
                    for bg in range(12):
                        if bg + 1 < 12:
                            load_w(bg + 1)
                        elif u + 1 < NLOC:
                            load_w(0)
                        if bg == 5 and u + 2 < NLOC:
                            pro_load(u + 2)
                        for b4 in range(4):
                            blk = 4 * bg + b4
                            kind = blk // 8 if blk < 32 else (4 if blk % 2 == 0 else 5)
                            if kind == 4:
                                dst, Bdst = gcs, Bgcs
                            else:
                                oi = ost_i[0] = (ost_i[0] + 1) % 3
                                dst, Bdst = ost[oi], Bost[oi]
                            for (so, n) in segs:
                                bk = next_acc()
                                for kc in range(KC):
                                    op(PE, lambda: nc.tensor.matmul(pbank[bk][:, :n], lhsT=wt[bg % 2][:, b4, kc, :],
                                                                    rhs=hn[:, kc, so:so + n],
                                                                    start=(kc == 0), stop=(kc == KC - 1)),
                                       reads=[Bwt[bg % 2], Bhn], writes=[Bp[bk]], sig=(kc == KC - 1))
                                if kind == 5:
                                    op(DVE, lambda: nc.vector.tensor_tensor(dst[:, so:so + n], pbank[bk][:, :n],
                                                                            gcs[:, so:so + n], ALU.mult),
                                       reads=[Bp[bk], Bgcs], writes=[Bdst])
                                else:
                                    evac(dst[:, so:so + n], pbank[bk][:, :n], [Bp[bk]], [Bdst])
                            if kind == 4:
                                continue
                            xo = u * CH
                            if kind == 0:
                                SPQ.dma(qT_d.ap()[blk * 128:(blk + 1) * 128, t0:t0 + TU], dst[:, :TU],
                                        reads=[Bdst], writes=[B["qT"]])
                            elif kind in (1, 2):
                                hb = blk - 8 * kind
                                md, gd, bm, bgn = ((kmeta_d, kg_in, "kmeta", "kg_in") if kind == 1
                                                   else (vmeta_d, vg_in, "vmeta", "vg_in"))
                                if u == 0:
                                    SPQ.dma(md.ap()[hb * 128:(hb + 1) * 128, :], dst[:, :NMETA],
                                            reads=[Bdst], writes=[B[bm]])
                                SPQ.dma(gd[u].ap()[hb * 128:(hb + 1) * 128, :], dst[:, TU - CH:TU],
                                        reads=[Bdst], writes=[B[(bgn, u)]])
                            elif kind == 3:
                                g = blk - 24
                                SPQ.dma(gbT_d.ap()[g * 128:(g + 1) * 128, t0:t0 + TU], dst[:, :TU],
                                        reads=[Bdst], writes=[B["gbT"]])
                            else:
                                g = (blk - 32) // 2
                                SPQ.dma(uT_d.ap()[g * 128:(g + 1) * 128, t0:t0 + TU], dst[:, :TU],
                                        reads=[Bdst], writes=[B["uT"]])
                                SPQ.dma(ut_in.ap()[g * 128:(g + 1) * 128, 2 * u:2 * u + 2], dst[:, TU - 2:TU],
                                        reads=[Bdst], writes=[B["ut_in"]])
                    for (so, n) in segs:
                        for kc in range(KC):
                            op(PE, lambda: nc.tensor.matmul(pbank[7][:8, :n], lhsT=wf[:, kc, :], rhs=hn[:, kc, so:so + n],
                                                            start=(kc == 0), stop=(kc == KC - 1)),
                               reads=[Bwf, Bhn], writes=[Bp[7]], sig=(kc == KC - 1))
                        op(ACT, lambda: nc.scalar.activation(lfe[:, so:so + n], pbank[7][:8, :n], AF.Exp,
                                                             bias=nbf[:, l:l + 1], scale=-1.0),
                           reads=[Bp[7], Bc], writes=[Blf])
                        op(ACT, lambda: nc.scalar.activation(lfe[:, so:so + n], lfe[:, so:so + n], AF.Ln,
                                                             bias=1.0, scale=1.0),
                           reads=[Blf], writes=[Blf])
                        op(DVE, lambda: nc.vector.tensor_scalar(lfs[:, so:so + n], lfe[:, so:so + n], -1.0, None,
                                                                ALU.mult),
                           reads=[Blf], writes=[Blf])
                    if u == 0:
                        SPQ.dma(lfm_d.ap(), lfs[:, :NMETA], reads=[Blf], writes=[B["lfm"]])
                    SPQ.dma(lf_in.ap()[:, u * CH:(u + 1) * CH], lfs[:, TU - CH:TU], reads=[Blf], writes=[B["lf_in"]])
                    allgather(kg_in[u], kg_out[u], B[("kg_in", u)], B[("kg_out", u)])
                    allgather(vg_in[u], vg_out[u], B[("vg_in", u)], B[("vg_out", u)])
                pro_load(0)
                prologue(0)
                if NLOC > 1:
                    pro_load(1)
                for u in range(NLOC):
                    if u + 1 < NLOC:
                        prologue(u + 1)
                    proj(u)
            allgather(lf_in, lf_out, B["lf_in"], B["lf_out"])
            allgather(ut_in, ut_out, B["ut_in"], B["ut_out"])


        def barrier():
            engs = [PE, ACT, DVE, POOL]
            dsems = SPQ.sems + PQ.sems + ccsems
            for E in engs + [SP]:
                for F_ in engs:
                    if F_ is not E:
                        E.wait_t(F_.sem, F_.cnt)
                for s in dsems:
                    E.wait_t(s, s.val)

        def gcols(jj):
            rho, lam = chunk_owner(jj)
            return rho, lam * CH

        def jmax(lam):
            return 8 * (lam // 2) + (3 if lam % 2 == 0 else 7)

        def cands(lam):
            return [chunk_global(r, lam) for r in range(4)]

        def phase2(l):
            with ExitStack() as st:
                CTn = sb("p2_CTn", [128, NKT, 8], F32, st)
                CTo = sb("p2_CTo", [128, 4 * NLOC, 8], F32, st)
                Rbc = sb("p2_Rbc", [128, 8 * NSUB], F32, st)
                Btab = Buf()
                with ExitStack() as st2:
                    lfF = sb("p2_lfF", [8, NTOK], F32, st2)
                    cF = sb("p2_cF", [8, NTOK], F32, st2)
                    cown = sb("p2_cown", [8, NXL], F32, st2)
                    Rm = sb("p2_Rm", [8, NSUB], F32, st2)
                    Dh = sb("p2_Dh", [8, NSUB], F32, st2)
                    Bl, Bcf, Bco, Brm, Bdh = Buf(), Buf(), Buf(), Buf(), Buf()
                    SPQ.dma(lfF[:, :NMETA], lfm_d.ap(), reads=[B["lfm"]], writes=[Bl])
                    for jj in range(NG):
                        rho, co = gcols(jj)
                        SPQ.dma(lfF[:, NMETA + jj * CH:NMETA + (jj + 1) * CH],
                                lf_out.ap()[rho * 8:(rho + 1) * 8, co:co + CH], reads=[B["lf_out"]], writes=[Bl])
                    pos = 0
                    while pos < NTOK:
                        n = min(2048, NTOK - pos)
                        init = 0.0 if pos == 0 else cF[:, pos - 1:pos]
                        op(DVE, lambda: nc.vector.tensor_tensor_scan(cF[:, pos:pos + n], lfF[:, pos:pos + n],
                                                                     lfF[:, pos:pos + n], init, ALU.add, ALU.min),
                           reads=[Bl, Bcf], writes=[Bcf])
                        pos += n
                    for lam in range(NLOC):
                        cs = cands(lam)
                        dstc = cown[:, lam * CH:(lam + 1) * CH]
                        for r4 in range(4):
                            src = cF[:, NMETA + cs[r4] * CH:NMETA + (cs[r4] + 1) * CH]
                            if r4 == 0:
                                op(DVE, lambda: nc.vector.tensor_scalar(dstc, src, oh[:8, 0:1], None, ALU.mult),
                                   reads=[Bcf, Bc], writes=[Bco])
                            else:
                                op(DVE, lambda: nc.vector.scalar_tensor_tensor(dstc, src, oh[:8, r4:r4 + 1], dstc,
                                                                               ALU.mult, ALU.add),
                                   reads=[Bcf, Bc, Bco], writes=[Bco])
                    op(DVE, lambda: nc.vector.tensor_tensor(Rm[:, 0:1], cF[:, 0:1], cF[:, NMETA - 1:NMETA], ALU.add),
                       reads=[Bcf], writes=[Brm])
                    for s in range(4 * NLOC):
                        a = s * 128
                        op(DVE, lambda: nc.vector.tensor_tensor(Rm[:, 1 + s:2 + s], cown[:, a:a + 1],
                                                                cown[:, a + 127:a + 128], ALU.add),
                           reads=[Bco], writes=[Brm])
                    for h in range(NH):
                        op(DVE, lambda: nc.vector.tensor_scalar(Dh[:], Rm[:], identf[:8, h:h + 1], None, ALU.mult),
                           reads=[Brm, Bc], writes=[Bdh])
                        op(PE, lambda: nc.tensor.matmul(pbank[6][:, h * NSUB:(h + 1) * NSUB], lhsT=ones8f[:], rhs=Dh[:],
                                                        start=True, stop=True),
                           reads=[Bdh, Bc], writes=[Bp[6]])
                    op(DVE, lambda: nc.vector.tensor_scalar(Rbc[:], pbank[6][:, :8 * NSUB], 0.5 / SCALE, None, ALU.mult),
                       reads=[Bp[6]], writes=[Btab])
                    for b0 in range(0, NKT, 64):
                        cnt = min(64, NKT - b0)
                        bk = next_acc()
                        for i in range(cnt):
                            kt = b0 + i
                            k0, nk = (0, NMETA) if kt == 0 else (NMETA + (kt - 1) * 128, 128)
                            op(PE, lambda: nc.tensor.transpose(pbank[bk][:nk, i * 8:(i + 1) * 8], cF[:, k0:k0 + nk],
                                                               identf[:8, :8]),
                               reads=[Bcf, Bc], writes=[Bp[bk]], sig=(i == cnt - 1))
                        op(DVE, lambda: nc.vector.tensor_scalar(
                            CTn[:, b0:b0 + cnt, :], pbank[bk][:, :cnt * 8].rearrange("p (a b) -> p a b", b=8),
                            -1.0 / SCALE, None, ALU.mult), reads=[Bp[bk]], writes=[Btab])
                    bk = next_acc()
                    for i in range(4 * NLOC):
                        op(PE, lambda: nc.tensor.transpose(pbank[bk][:, i * 8:(i + 1) * 8], cown[:, i * 128:(i + 1) * 128],
                                                           identf[:8, :8]),
                           reads=[Bco, Bc], writes=[Bp[bk]], sig=(i == 4 * NLOC - 1))
                    op(DVE, lambda: nc.vector.tensor_scalar(
                        CTo[:], pbank[bk][:, :4 * NLOC * 8].rearrange("p (a b) -> p a b", b=8),
                        -1.0 / SCALE, None, ALU.mult), reads=[Bp[bk]], writes=[Btab])
                    if DEBUG:
                        dC = nc.dram_tensor(f"dbg_CTn{l}", [128, NKT * 8], F32, kind="ExternalOutput")
                        dR = nc.dram_tensor(f"dbg_Rbc{l}", [128, 8 * NSUB], F32, kind="ExternalOutput")
                        dO = nc.dram_tensor(f"dbg_CTo{l}", [128, 4 * NLOC * 8], F32, kind="ExternalOutput")
                        dcF = nc.dram_tensor(f"dbg_cF{l}", [8, NTOK], F32, kind="ExternalOutput")
                        SPQ.dma(dC.ap(), CTn[:].rearrange("p a b -> p (a b)"), reads=[Btab], writes=[B["out"]])
                        SPQ.dma(dR.ap(), Rbc[:], reads=[Btab], writes=[B["out"]])
                        SPQ.dma(dO.ap(), CTo[:].rearrange("p a b -> p (a b)"), reads=[Btab], writes=[B["out"]])
                        SPQ.dma(dcF.ap(), cF[:], reads=[Bcf], writes=[B["out"]])
                    barrier()

                kT = [sb(f"p2_kT{i}", [128, NTOK], BF16, st) for i in range(2)]
                BkT = [Buf(), Buf()]
                kown = [sb(f"p2_ko{i}", [128, NXL], BF16, st) for i in range(2)]
                Bko = [Buf(), Buf()]
                vT = sb("p2_vT", [128, NTOK], BF16, st)
                BvT = Buf()
                vownT = sb("p2_voT", [128, NXL], BF16, st)
                BvoT = Buf()
                V = sb("p2_V", [128, NKT, 128], BF16, st)
                BV = Buf()
                Vo = sb("p2_Vo", [128, 4 * NLOC, 128], BF16, st)
                BVo = Buf()
                qs = [sb(f"p2_q{i}", [128, CH], BF16, st) for i in range(2)]
                Bq = [Buf(), Buf()]
                Rt = [sb(f"p2_Rt{i}", [128, CH], F32, st) for i in range(2)]
                BRt = [Buf(), Buf()]
                CTl = [sb(f"p2_CTl{i}", [128, NKT], F32, st) for i in range(2)]
                BCl = [Buf(), Buf()]
                Tb = [sb(f"p2_T{i}", [128, CH], F32, st) for i in range(4)]
                BT = [Buf() for _ in range(4)]
                Pb = [sb(f"p2_P{i}", [128, CH], BF16, st) for i in range(5)]
                BP = [Buf() for _ in range(5)]
                LA = 3
                pending = []
                Osb = sb("p2_Osb", [128, CH], F32, st)
                d2 = sb("p2_d2", [128, CH], F32, st)
                sqo = sb("p2_sqo", [128, CH], BF16, st)
                uu = sb("p2_uu", [128, CH], F32, st)
                yst = [sb(f"p2_y{i}", [128, CH], BF16, st) for i in range(2)]
                BOs, Bd2, Bsqo, Buu = Buf(), Buf(), Buf(), Buf()
                Byst = [Buf(), Buf()]
                TRB = [7, 4]
                pb67b = [pbank[7][:].bitcast(BF16), pbank[4][:].bitcast(BF16)]

                def head_jobs(h):
                    kb, Bk = kT[h % 2], BkT[h % 2]
                    hs = slice(h * 128, (h + 1) * 128)
                    jobs = []

                    def J(out, in_, r, w):
                        jobs.append(lambda: SPQ.dma(out, in_, reads=r, writes=w))
                    J(vT[:, :NMETA], vmeta_d.ap()[hs, :], [B["vmeta"]], [BvT])
                    for jj in range(NG):
                        rho, lp = chunk_owner(jj)
                        rs_ = slice(rho * 1024 + h * 128, rho * 1024 + (h + 1) * 128)
                        J(vT[:, NMETA + jj * CH:NMETA + (jj + 1) * CH], vg_out[lp].ap()[rs_, :],
                          [B[("vg_out", lp)]], [BvT])
                    for lam in range(NLOC):
                        J(vownT[:, lam * CH:(lam + 1) * CH], vg_in[lam].ap()[hs, :], [B[("vg_in", lam)]], [BvoT])
                    J(kb[:, :NMETA], kmeta_d.ap()[hs, :], [B["kmeta"]], [Bk])
                    for jj in range(NG):
                        rho, lp = chunk_owner(jj)
                        rs_ = slice(rho * 1024 + h * 128, rho * 1024 + (h + 1) * 128)
                        J(kb[:, NMETA + jj * CH:NMETA + (jj + 1) * CH], kg_out[lp].ap()[rs_, :],
                          [B[("kg_out", lp)]], [Bk])
                    for lam in range(NLOC):
                        J(kown[h % 2][:, lam * CH:(lam + 1) * CH], kg_in[lam].ap()[hs, :], [B[("kg_in", lam)]],
                          [Bko[h % 2]])
                    return jobs

                def load_head(h):
                    for j_ in head_jobs(h):
                        j_()

                seglist_all = []
                for u_ in range(NLOC):
                    if u_ == 0:
                        seglist_all.append((u_, "meta", 0, NMETA))
                    seglist_all.append((u_, "chunk", NMETA + u_ * CH, CH))
                nsegs = len(seglist_all)

                def load_q(si_, h_, t0_, nq_):
                    SPQ.dma(qs[si_ % 2][:, :nq_], qT_d.ap()[h_ * 128:(h_ + 1) * 128, t0_:t0_ + nq_], reads=[B["qT"]],
                            writes=[Bq[si_ % 2]])

                tcount = [0]
                segcount = [0]
                load_head(0)
                load_q(1, 0, seglist_all[0][2], seglist_all[0][3])
                for h in range(NH):
                    while pending:
                        pending.pop(0)()
                    TRB[1] = 4 + (segcount[0] + 1) % 2
                    pb67b[1] = pbank[TRB[1]][:].bitcast(BF16)
                    ti = 0
                    for b0 in range(0, NKT, 8):
                        cnt = min(8, NKT - b0)
                        pi = ti % 2
                        ti += 1
                        for i in range(cnt):
                            kt = b0 + i
                            k0, nk = (0, NMETA) if kt == 0 else (NMETA + (kt - 1) * 128, 128)
                            op(PE, lambda: nc.tensor.transpose(pb67b[pi][:nk, i * 128:(i + 1) * 128],
                                                               vT[:, k0:k0 + nk], identb[:]),
                               reads=[BvT, Bc], writes=[Bp[TRB[pi]]], sig=(i == cnt - 1))
                        evac(V[:, b0:b0 + cnt, :], pb67b[pi][:, :cnt * 128].rearrange("p (a d) -> p a d", d=128),
                             [Bp[TRB[pi]]], [BV])
                    for b0 in range(0, 4 * NLOC, 8):
                        cnt = min(8, 4 * NLOC - b0)
                        pi = ti % 2
                        ti += 1
                        for i in range(cnt):
                            op(PE, lambda: nc.tensor.transpose(pb67b[pi][:, i * 128:(i + 1) * 128],
                                                               vownT[:, (b0 + i) * 128:(b0 + i + 1) * 128], identb[:]),
                               reads=[BvoT, Bc], writes=[Bp[TRB[pi]]], sig=(i == cnt - 1))
                        evac(Vo[:, b0:b0 + cnt, :], pb67b[pi][:, :cnt * 128].rearrange("p (a d) -> p a d", d=128),
                             [Bp[TRB[pi]]], [BVo])
                    nxt_jobs = head_jobs(h + 1) if h + 1 < NH else []
                    per_seg = -(-len(nxt_jobs) // nsegs)
                    kb, Bk = kT[h % 2], BkT[h % 2]
                    ko, Bkow = kown[h % 2], Bko[h % 2]
                    for sidx, (u, skind, t0, nq) in enumerate(seglist_all):
                        if True:
                            si = segcount[0] = segcount[0] + 1
                            conv_slice(2)
                            for _ in range(min(per_seg, len(nxt_jobs))):
                                nxt_jobs.pop(0)()
                            if sidx + 1 < nsegs:
                                load_q(si + 1, h, seglist_all[sidx + 1][2], seglist_all[sidx + 1][3])
                            elif h + 1 < NH:
                                load_q(si + 1, h + 1, seglist_all[0][2], seglist_all[0][3])
                            q, Bqq = qs[si % 2], Bq[si % 2]
                            rt, Brt = Rt[si % 2], BRt[si % 2]
                            ctl, Bcl = CTl[si % 2], BCl[si % 2]
                            ob, db = 4 + si % 2, 6
                            tiles = []
                            if skind == "meta":
                                op(DVE, lambda: nc.vector.tensor_scalar(rt[:, :nq], zerosf[:, :nq],
                                                                        Rbc[:, h * NSUB:h * NSUB + 1], None, ALU.add),
                                   reads=[Btab, Bc], writes=[Brt])
                                tiles.append((kb[:, 0:NMETA], V[:NMETA, 0, :], CTn[:NMETA, 0, h:h + 1], NMETA, 0,
                                              [Bk], [BV], [Btab]))
                            else:
                                lam = u
                                for sj in range(4):
                                    s = 1 + 4 * lam + sj
                                    op(DVE, lambda: nc.vector.tensor_scalar(
                                        rt[:, sj * 128:(sj + 1) * 128], zerosf[:, :128],
                                        Rbc[:, h * NSUB + s:h * NSUB + s + 1], None, ALU.add),
                                       reads=[Btab, Bc], writes=[Brt])
                                npast = 1 + 4 * jmax(lam)
                                op(DVE, lambda: nc.vector.tensor_tensor(ctl[:, :npast], CTn[:, :npast, h],
                                                                        pen[:, lam, :npast], ALU.add),
                                   reads=[Btab, Bc], writes=[Bcl])
                                tiles.append((kb[:, 0:NMETA], V[:NMETA, 0, :], ctl[:NMETA, 0:1], NMETA, None,
                                              [Bk], [BV], [Bcl]))
                                for kt in range(1, npast):
                                    k0 = NMETA + (kt - 1) * 128
                                    tiles.append((kb[:, k0:k0 + 128], V[:, kt, :], ctl[:, kt:kt + 1], 128, None,
                                                  [Bk], [BV], [Bcl]))
                                for i in range(4):
                                    k0 = lam * CH + i * 128
                                    tiles.append((ko[:, k0:k0 + 128], Vo[:, 4 * lam + i, :],
                                                  CTo[:, 4 * lam + i, h:h + 1], 128, i, [Bkow], [BVo], [Btab]))
                            nt = len(tiles)

                            def emit_S(i):
                                kap, vap, bcol, nk, dg, rk, rv, rb = tiles[i]
                                sbk = (tcount[0] + i) % 4
                                op(PE, lambda: nc.tensor.matmul(pbank[sbk][:nk, :nq], lhsT=kap, rhs=q[:, :nq],
                                                                start=True, stop=True),
                                   reads=rk + [Bqq], writes=[Bp[sbk]])

                            for i in range(min(LA, nt)):
                                emit_S(i)
                            for i in range(nt):
                                if i + LA < nt:
                                    emit_S(i + LA)
                                kap, vap, bcol, nk, dg, rk, rv, rb = tiles[i]
                                g = tcount[0] + i
                                sbk = g % 4
                                T_, BT_ = Tb[g % 4], BT[g % 4]
                                P_, BP_ = Pb[g % 5], BP[g % 5]
                                if i == min(4, nt - 1) and pending:
                                    pending.pop(0)()
                                c0 = 0 if (dg is None or skind == "meta") else dg * 128
                                op(DVE, lambda: nc.vector.scalar_tensor_tensor(T_[:nk, c0:nq], pbank[sbk][:nk, c0:nq],
                                                                               bcol, rt[:nk, c0:nq], ALU.add, ALU.add),
                                   reads=[Bp[sbk], Brt] + rb, writes=[BT_])
                                op(ACT, lambda: nc.scalar.activation(P_[:nk, c0:nq], T_[:nk, c0:nq], AF.Exp, scale=SCALE),
                                   reads=[BT_], writes=[BP_])
                                if dg is not None:
                                    if c0 > 0:
                                        op(POOL, lambda: nc.gpsimd.memset(P_[:nk, :c0], 0.0), reads=[], writes=[BP_])
                                    w = min(128, nq)
                                    op(POOL, lambda: nc.gpsimd.tensor_tensor(P_[:nk, c0:c0 + w], P_[:nk, c0:c0 + w],
                                                                             tri[:nk, :w], ALU.mult),
                                       reads=[Bc], writes=[BP_])
                                op(PE, lambda: nc.tensor.matmul(pbank[ob][:, :nq], lhsT=vap, rhs=P_[:nk, :nq],
                                                                start=(i == 0), stop=(i == nt - 1)),
                                   reads=rv + [BP_], writes=[Bp[ob]], sig=False)
                                op(PE, lambda: nc.tensor.matmul(pbank[db][:, :nq], lhsT=ones1[:nk, :], rhs=P_[:nk, :nq],
                                                                start=(i == 0), stop=(i == nt - 1)),
                                   reads=[Bc, BP_], writes=[Bp[db]], sig=True)
                            tcount[0] += nt
                            ys, Bys = yst[si % 2], Byst[si % 2]
                            op(ACT, lambda: nc.scalar.activation(d2[:, :nq], pbank[db][:, :nq], AF.Ln), reads=[Bp[db]],
                               writes=[Bd2])
                            op(ACT, lambda: nc.scalar.activation(d2[:, :nq], d2[:, :nq], AF.Exp, scale=-1.0), reads=[Bd2],
                               writes=[Bd2])
                            op(DVE, lambda: nc.vector.tensor_tensor(Osb[:, :nq], pbank[ob][:, :nq], d2[:, :nq], ALU.mult),
                               reads=[Bp[ob], Bd2], writes=[BOs])

                            def ep_tail(ys=ys, Bys=Bys, nq=nq, t0=t0, h=h):
                                op(ACT, lambda: nc.scalar.activation(sqo[:, :nq], Osb[:, :nq], AF.Square),
                                   reads=[BOs], writes=[Bsqo])
                                op(PE, lambda: nc.tensor.matmul(pbank[7][:, :nq], lhsT=onesG[:], rhs=sqo[:, :nq],
                                                                start=True, stop=True),
                                   reads=[Bsqo, Bc], writes=[Bp[7]])
                                rsqrt_act(uu[:, :nq], pbank[7][:, :nq], EPS, [Bp[7]], [Buu])
                                op(DVE, lambda: nc.vector.scalar_tensor_tensor(ys[:, :nq], Osb[:, :nq],
                                                                               vcol(80 + l * 16 + h), uu[:, :nq],
                                                                               ALU.mult, ALU.mult),
                                   reads=[BOs, Buu, Bc], writes=[Bys])
                                SPQ.dma(yT_d.ap()[h * 128:(h + 1) * 128, t0:t0 + nq], ys[:, :nq], reads=[Bys],
                                        writes=[B["yT"]])

                            pending.append(ep_tail)
                            if nt <= 4:
                                while pending:
                                    pending.pop(0)()
                while pending:
                    pending.pop(0)()

        def phase34(l, last):
            with ExitStack() as st:
                hT1 = sb("p3_hT1", [128, KC, TUM], F32, st)
                Bh = [Buf() for _ in range(KC)]
                yTs = sb("p3_yTs", [128, KC, TUM], BF16, st)
                By = Buf()
                aT = sb("p3_aT", [128, FC, TUM], BF16, st)
                Ba = [Buf() for _ in range(FC)]
                wA = [sb(f"p3_wA{i}", [128, KC, 128], BF16, st) for i in range(2)]
                wB_ = [sb(f"p3_wB{i}", [128, KC, 128], BF16, st) for i in range(2)]
                BwA = [Buf(), Buf()]
                BwB = [Buf(), Buf()]
                wD = [sb(f"p3_wD{i}", [128, FC, 128], BF16, st) for i in range(2)]
                BwD = [Buf(), Buf()]
                wO = [sb(f"p3_wO{i}", [128, 2, KC, 128], BF16, st) for i in range(2)]
                BwO = [Buf(), Buf()]
                uh = sb("p3_uh", [128, 8, 2 + CH], BF16, st)
                Buh = Buf()
                gb = sb("p3_gb", [128, 8, TUM], BF16, st)
                Bgb = Buf()
                t1 = sb("p3_t1", [128, CH], F32, st)
                cv = sb("p3_cv", [128, CH], F32, st)
                sqc = sb("p3_sqc", [128, CH], BF16, st)
                rsx = sb("p3_rsx", [128, TUM], F32, st)
                sg = [sb(f"p3_sg{i}", [128, CH], F32, st) for i in range(2)]
                Bt1, Bcv, Bsqc, Brsx = Buf(), Buf(), Buf(), Buf()
                Bsg = [Buf(), Buf()]
                tl = sb("p3_tl", [128, NRANK, 8, 2 * NLOC], BF16, st)
                mt = sb("p3_mt", [128, 8, 2], BF16, st)
                hal = sb("p3_hal", [128, 8, 2], F32, st)
                Btl, Bhal = Buf(), Buf()
                if last:
                    otile = [sb(f"p3_ot{i}", [128, D // 2], F32, st) for i in range(2)]
                    Bot = [Buf(), Buf()]
                SPQ.dma(tl[:], ut_out.ap().rearrange("(r g p) c -> p r g c", r=NRANK, g=8), reads=[B["ut_out"]],
                        writes=[Btl])
                SPQ.dma(mt[:], uT_d.ap()[:, NMETA - 2:NMETA].rearrange("(g p) c -> p g c", p=128), reads=[B["uT"]],
                        writes=[Btl])
                ycv = sb("p3_ycv", [128, 8, TUM], BF16, st)
                Bycv = Buf()
                cwb = 112 + l * 24
                oti = [0]
                sgi = [0]
                def conv_gb(u):
                    t0, segs, TU = unit_info(u)
                    SPQ.dma(gb[:, :, :TU], gbT_d.ap()[:, t0:t0 + TU].rearrange("(k p) t -> p k t", p=128),
                            reads=[B["gbT"]], writes=[Bgb])

                def conv_pre(u, so, n):
                    t0, segs, TU = unit_info(u)
                    if True:
                        SPQ.dma(uh[:, :, 2:2 + n], uT_d.ap()[:, t0 + so:t0 + so + n].rearrange("(k p) t -> p k t", p=128),
                                reads=[B["uT"]], writes=[Buh])
                        if n == NMETA:
                            op(DVE, lambda: nc.vector.memset(uh[:, :, 0:2], 0.0), reads=[], writes=[Buh])
                        else:
                            lam = u
                            for r4 in range(4):
                                j = chunk_global(r4, lam)
                                if j == 0:
                                    cand = mt[:]
                                else:
                                    rho, lp = chunk_owner(j - 1)
                                    cand = tl[:, rho, :, 2 * lp:2 * lp + 2]
                                if r4 == 0:
                                    op(DVE, lambda: nc.vector.tensor_scalar(hal[:], cand, oh[:, 0:1], None, ALU.mult),
                                       reads=[Btl, Bc], writes=[Bhal])
                                else:
                                    op(DVE, lambda: nc.vector.scalar_tensor_tensor(hal[:], cand, oh[:, r4:r4 + 1], hal[:],
                                                                                   ALU.mult, ALU.add),
                                       reads=[Btl, Bc, Bhal], writes=[Bhal])
                            op(DVE, lambda: nc.vector.tensor_copy(uh[:, :, 0:2], hal[:]), reads=[Bhal], writes=[Buh])

                def conv_a(u, so, n, g):
                    if True:
                        if True:
                            op(DVE, lambda: nc.vector.tensor_scalar(t1[:, :n], uh[:, g, 0:n], vcol(cwb + g), None, ALU.mult),
                               reads=[Buh, Bc], writes=[Bt1])
                            op(DVE, lambda: nc.vector.scalar_tensor_tensor(t1[:, :n], uh[:, g, 1:n + 1], vcol(cwb + 8 + g),
                                                                           t1[:, :n], ALU.mult, ALU.add),
                               reads=[Buh, Bc, Bt1], writes=[Bt1])
                            op(DVE, lambda: nc.vector.scalar_tensor_tensor(t1[:, :n], uh[:, g, 2:n + 2], vcol(cwb + 16 + g),
                                                                           t1[:, :n], ALU.mult, ALU.add),
                               reads=[Buh, Bc, Bt1], writes=[Bt1])
                            op(DVE, lambda: nc.vector.tensor_tensor(cv[:, :n], t1[:, :n], gb[:, g, so:so + n], ALU.mult),
                               reads=[Bt1, Bgb], writes=[Bcv])
                            op(ACT, lambda: nc.scalar.activation(sqc[:, :n], cv[:, :n], AF.Square),
                               reads=[Bcv], writes=[Bsqc])

                def conv_b(u, so, n, g):
                    if True:
                        if True:
                            op(PE, lambda: nc.tensor.matmul(pbank[6][:, :n], lhsT=onesG[:], rhs=sqc[:, :n],
                                                            start=True, stop=True),
                               reads=[Bsqc, Bc], writes=[Bp[6]])
                            rsqrt_act(rsx[:, :n], pbank[6][:, :n], EPS, [Bp[6]], [Brsx])
                            op(DVE, lambda: nc.vector.scalar_tensor_tensor(ycv[:, g, so:so + n], cv[:, :n],
                                                                           vcol(80 + l * 16 + 8 + g), rsx[:, :n],
                                                                           ALU.mult, ALU.mult),
                               reads=[Bcv, Brsx, Bc], writes=[Bycv])

                def conv(u):
                    conv_gb(u)
                    for (so_, n_) in unit_info(u)[1]:
                        conv_pre(u, so_, n_)
                        for g_ in range(8):
                            conv_a(u, so_, n_, g_)
                            conv_b(u, so_, n_, g_)

                conv(0)
                for u in range(NLOC):
                    t0, segs, TU = unit_info(u)
                    SPQ.dma(hT1[:, :, :TU], hT_d.ap()[:, t0:t0 + TU].rearrange("(k p) t -> p k t", p=128),
                            reads=[B["hT"]], writes=Bh)
                    if u == 0:
                        SPQ.dma(yTs[:, 0:8, :TU], yT_d.ap()[:, t0:t0 + TU].rearrange("(k p) t -> p k t", p=128),
                                reads=[B["yT"]], writes=[By])

                    def load_wo(i):
                        SPQ.dma(wO[i % 2][:], wob[l].ap()[2 * i:2 * i + 2].rearrange("b p k j -> p b k j"),
                                reads=[Bw[("wob", l)]], writes=[BwO[i % 2]])

                    load_wo(0)
                    def load_gu(f):
                        SPQ.dma(wA[f % 2][:], wgb[l].ap()[f], reads=[Bw[("wgb", l)]], writes=[BwA[f % 2]])
                        SPQ.dma(wB_[f % 2][:], wub[l].ap()[f], reads=[Bw[("wub", l)]], writes=[BwB[f % 2]])

                    load_gu(0)
                    def load_d(m):
                        SPQ.dma(wD[m % 2][:], wdb[l].ap()[m], reads=[Bw[("wdb", l)]], writes=[BwD[m % 2]])

                    load_d(0)
                    for i in range(8):
                        if i + 1 < 8:
                            load_wo(i + 1)
                        for b2 in range(2):
                            m = 2 * i + b2
                            for (so, n) in segs:
                                bk = next_acc()
                                for kc in range(KC):
                                    rhs_ = yTs[:, kc, so:so + n] if kc < 8 else ycv[:, kc - 8, so:so + n]
                                    op(PE, lambda: nc.tensor.matmul(pbank[bk][:, :n], lhsT=wO[i % 2][:, b2, kc, :],
                                                                    rhs=rhs_,
                                                                    start=(kc == 0), stop=(kc == KC - 1)),
                                       reads=[BwO[i % 2], By if kc < 8 else Bycv], writes=[Bp[bk]],
                                       sig=(kc == KC - 1))
                                op(DVE, lambda: nc.vector.tensor_tensor(hT1[:, m, so:so + n], hT1[:, m, so:so + n],
                                                                        pbank[bk][:, :n], ALU.add),
                                   reads=[Bp[bk]], writes=[Bh[m]])

                    def rms_stats():
                        for kc in range(KC):
                            op(ACT, lambda: nc.scalar.activation(aT[:, kc, :TU], hT1[:, kc, :TU], AF.Square),
                               reads=[Bh[kc]], writes=[Ba[kc]])
                        for (so, n) in segs:
                            for kc in range(KC):
                                op(PE, lambda: nc.tensor.matmul(pbank[6][:, :n], lhsT=onesD[:], rhs=aT[:, kc, so:so + n],
                                                                start=(kc == 0), stop=(kc == KC - 1)),
                                   reads=[Ba[kc], Bc], writes=[Bp[6]], sig=(kc == KC - 1))
                            rsqrt_act(rsx[:, so:so + n], pbank[6][:, :n], EPS, [Bp[6]], [Brsx])

                    rms_stats()
                    for kc in range(KC):
                        op(DVE, lambda: nc.vector.scalar_tensor_tensor(yTs[:, kc, :TU], hT1[:, kc, :TU],
                                                                       vcol(32 + l * 16 + kc), rsx[:, :TU],
                                                                       ALU.mult, ALU.mult),
                           reads=[Bh[kc], Brsx, Bc], writes=[By])
                    for f in range(FC):
                        conv_slice(1)
                        if f + 1 < FC:
                            load_gu(f + 1)
                        for (so, n) in segs:
                            bg_, bu_ = next_acc(), next_acc()
                            for kc in range(KC):
                                op(PE, lambda: nc.tensor.matmul(pbank[bg_][:, :n], lhsT=wA[f % 2][:, kc, :],
                                                                rhs=yTs[:, kc, so:so + n], start=(kc == 0),
                                                                stop=(kc == KC - 1)),
                                   reads=[BwA[f % 2], By], writes=[Bp[bg_]], sig=(kc == KC - 1))
                            for kc in range(KC):
                                op(PE, lambda: nc.tensor.matmul(pbank[bu_][:, :n], lhsT=wB_[f % 2][:, kc, :],
                                                                rhs=yTs[:, kc, so:so + n], start=(kc == 0),
                                                                stop=(kc == KC - 1)),
                                   reads=[BwB[f % 2], By], writes=[Bp[bu_]], sig=(kc == KC - 1))
                            k_ = sgi[0] = (sgi[0] + 1) % 2
                            op(ACT, lambda: nc.scalar.activation(sg[k_][:, :n], pbank[bg_][:, :n], AF.Silu),
                               reads=[Bp[bg_]], writes=[Bsg[k_]])
                            op(DVE, lambda: nc.vector.tensor_tensor(aT[:, f, so:so + n], sg[k_][:, :n], pbank[bu_][:, :n],
                                                                    ALU.mult),
                               reads=[Bsg[k_], Bp[bu_]], writes=[Ba[f]])
                    ovl = u + 1 < NLOC
                    if ovl:
                        t0n, segsn, TUn = unit_info(u + 1)
                        SPQ.dma(yTs[:, 0:8, :TUn], yT_d.ap()[:, t0n:t0n + TUn].rearrange("(k p) t -> p k t", p=128),
                                reads=[B["yT"]], writes=[By])
                        conv_gb(u + 1)
                        conv_pre(u + 1, 0, CH)
                    for m in range(16):
                        if m + 1 < 16:
                            load_d(m + 1)
                        if ovl and m % 2 == 0:
                            conv_a(u + 1, 0, CH, m // 2)
                        if ovl and m % 2 == 1:
                            conv_b(u + 1, 0, CH, m // 2)
                        for (so, n) in segs:
                            bk = next_acc()
                            for f in range(FC):
                                op(PE, lambda: nc.tensor.matmul(pbank[bk][:, :n], lhsT=wD[m % 2][:, f, :],
                                                                rhs=aT[:, f, so:so + n], start=(f == 0), stop=(f == FC - 1)),
                                   reads=[BwD[m % 2], Ba[f]], writes=[Bp[bk]], sig=(f == FC - 1))
                            op(DVE, lambda: nc.vector.tensor_tensor(hT1[:, m, so:so + n], hT1[:, m, so:so + n],
                                                                    pbank[bk][:, :n], ALU.add),
                               reads=[Bp[bk]], writes=[Bh[m]])
                    if not last:
                        SPQ.dma(hT_d.ap()[:, t0:t0 + TU].rearrange("(k p) t -> p k t", p=128), hT1[:, :, :TU],
                                reads=Bh, writes=[B["hT"]])
                    else:
                        rms_stats()
                        for kc in range(KC):
                            op(DVE, lambda: nc.vector.scalar_tensor_tensor(hT1[:, kc, :TU], hT1[:, kc, :TU],
                                                                           vcol(64 + kc), rsx[:, :TU], ALU.mult, ALU.mult),
                               reads=[Brsx, Bc], writes=[Bh[kc]])
                        so = TU - CH
                        for s4 in range(4):
                            for half in range(2):
                                oi = oti[0] = (oti[0] + 1) % 2
                                for k2 in range(2):
                                    k4 = 2 * half + k2
                                    bk = next_acc()
                                    for kk in range(4):
                                        kc = 4 * k4 + kk
                                        op(PE, lambda: nc.tensor.transpose(pbank[bk][:, kk * 128:(kk + 1) * 128],
                                                                           hT1[:, kc, so + s4 * 128:so + (s4 + 1) * 128],
                                                                           identf[:]),
                                           reads=[Bh[kc], Bc], writes=[Bp[bk]], sig=(kk == 3))
                                    evac(otile[oi][:, k2 * 512:(k2 + 1) * 512], pbank[bk][:, :], [Bp[bk]], [Bot[oi]])
                                r0 = u * CH + s4 * 128
                                SPQ.dma(out_d.ap()[r0:r0 + 128, half * 1024:(half + 1) * 1024], otile[oi][:],
                                        reads=[Bot[oi]], writes=[B["out"]])

        for l in range(NLAYERS):
            phase1(l)
            barrier()
            phase2(l)
            conv_slice(10 ** 6)
            if l == 0 and NLAYERS > 1:
                convert(1, defer=True)
            barrier()
            phase34(l, l == NLAYERS - 1)
            conv_slice(10 ** 6)
            barrier()
    return nc


_PROG_CACHE = {}


def _run(inputs, NLOC, NLAYERS):
    f32 = np.float32
    x = np.asarray(inputs["x"], f32)
    meta = np.ascontiguousarray(np.asarray(inputs["meta"], f32))
    norm_mix = np.asarray(inputs["norm_mix"], f32)
    b_f = np.asarray(inputs["b_f"], f32)
    conv_w = np.asarray(inputs["conv_w"], f32)
    out_gain = np.asarray(inputs["out_gain"], f32)
    norm_ffn = np.asarray(inputs["norm_ffn"], f32)
    final_norm = np.asarray(inputs["final_norm"], f32)
    wts = {k: np.ascontiguousarray(np.asarray(inputs[k], f32)) for k in ["w_in", "w_out", "w_gate", "w_up", "w_down"]}
    NG = 4 * NLOC
    NKT = 1 + 4 * NG
    vecs = np.zeros((128, NV), f32)
    for l in range(2):
        vecs[:, l * 16:(l + 1) * 16] = norm_mix[l].reshape(16, 128).T
        vecs[:, 32 + l * 16:32 + (l + 1) * 16] = norm_ffn[l].reshape(16, 128).T
        vecs[:, 80 + l * 16:80 + (l + 1) * 16] = out_gain[l].reshape(16, 128).T
        for k in range(3):
            vecs[:, 112 + l * 24 + k * 8:112 + l * 24 + (k + 1) * 8] = conv_w[l, k].reshape(8, 128).T
    vecs[:, 64:80] = final_norm.reshape(16, 128).T
    bf = np.ascontiguousarray(b_f.T)
    identf = np.eye(128, dtype=f32)
    identb = np.eye(128).astype(ml_dtypes.bfloat16)
    tri = np.triu(np.ones((128, 128))).astype(ml_dtypes.bfloat16)
    in_maps = []
    for c in range(8):
        b, r = divmod(c, 4)
        xl = np.concatenate([x[b, chunk_global(r, lam) * CH:(chunk_global(r, lam) + 1) * CH] for lam in range(NLOC)], 0)
        ohm = np.zeros((128, 4), f32)
        ohm[:, r] = 1.0
        penm = np.zeros((NLOC, NKT), f32)
        for lam in range(NLOC):
            j = chunk_global(r, lam)
            for kt in range(1, NKT):
                if (kt - 1) // 4 >= j:
                    penm[lam, kt] = -BIG
        penb = np.ascontiguousarray(np.broadcast_to(penm.reshape(1, -1), (128, NLOC * NKT)))
        m = {"x": np.ascontiguousarray(xl), "meta": meta, "vecs": vecs, "bf": bf, "identf": identf, "identb": identb,
             "tri": tri, "oh": ohm, "pen": penb}
        m.update(wts)
        in_maps.append(m)
    key = (NLOC, NLAYERS)
    if key not in _PROG_CACHE:
        _PROG_CACHE[key] = build(NLOC, NLAYERS)
    nc = _PROG_CACHE[key]
    res = run_bass_kernel_spmd(nc, in_maps, core_ids=list(range(8)))
    if DEBUG:
        _run.last = res.results
    out = np.zeros((2, NG * CH, D), f32)
    for c in range(8):
        b, r = divmod(c, 4)
        o = np.asarray(res.results[c]["out"])
        for lam in range(NLOC):
            j = chunk_global(r, lam)
            out[b, j * CH:(j + 1) * CH] = o[lam * CH:(lam + 1) * CH]
    return out


def kernel(**inputs):
    return _run(inputs, 8, 2)
```

```python
from contextlib import ExitStack
import numpy as np
import ml_dtypes
import concourse.bass as bass
import concourse.mybir as mybir
from concourse.bass_utils import run_bass_kernel_spmd

F32, BF16 = mybir.dt.float32, mybir.dt.bfloat16
AF = mybir.ActivationFunctionType
ALU = mybir.AluOpType

D = 2048
KC = 16
NH = 8
DFF = 5632
FC = 44
DIN = 6152
NMETA = 16
CH = 512
EPS = 1e-6
SCALE = 128 ** -0.5
NRANK = 4


class Sem:
    def __init__(self, h):
        self.h = h
        self.val = 0


class Buf:
    __slots__ = ("w", "r")

    def __init__(self):
        self.w = {}
        self.r = {}


class Eng:
    def __init__(self, name, eng, sem):
        self.name, self.eng, self.sem, self.cnt, self.seen = name, eng, sem, 0, {}
        self.is_pe = name == "pe"

    def wait_t(self, s, v):
        if v <= 0:
            return
        if s is self.sem:
            if self.is_pe or v > self.cnt:
                return
        if self.seen.get(s, 0) >= v:
            return
        self.eng.wait_ge(s.h, v)
        self.seen[s] = v

    def wait_bufs(self, reads, writes):
        for b in reads:
            for s, v in b.w.items():
                self.wait_t(s, v)
        for b in writes:
            for s, v in b.w.items():
                self.wait_t(s, v)
            for s, v in b.r.items():
                self.wait_t(s, v)


def _commit(s, t, reads, writes):
    for b in writes:
        if b.w.get(s, 0) < t:
            b.w[s] = t
    for b in reads:
        if b.r.get(s, 0) < t:
            b.r[s] = t


def op(E, fn, reads=(), writes=(), sig=True):
    E.wait_bufs(reads, writes)
    ins = fn()
    if sig:
        ins.then_inc(E.sem.h, 1)
        E.cnt += 1
        t = E.cnt
    else:
        t = E.cnt + 1
    _commit(E.sem, t, reads, writes)


class DmaQ:
    def __init__(self, E, sems):
        self.E, self.sems, self.i = E, sems, 0

    def dma(self, out, in_, reads=(), writes=()):
        s = self.sems[self.i]
        self.i = (self.i + 1) % len(self.sems)
        self.E.wait_t(s, s.val)
        self.E.wait_bufs(reads, writes)
        self.E.eng.dma_start(out=out, in_=in_).then_inc(s.h, 16)
        s.val += 16
        _commit(s, s.val, reads, writes)


def chunk_owner(j):
    i, pos = divmod(j, 8)
    if pos < 4:
        return pos, 2 * i
    return 7 - pos, 2 * i + 1


def chunk_global(r, lam):
    return 8 * (lam // 2) + (r if lam % 2 == 0 else 7 - r)


W_IN_COLS = ([128 * i for i in range(8)] + [1024 + 128 * i for i in range(8)]
             + [2048 + 128 * i for i in range(8)] + [3072 + 128 * i for i in range(8)])
for _g in range(8):
    W_IN_COLS += [4096 + 128 * _g, 5120 + 128 * _g]
NV = 160
BIG = 1.0e5 / SCALE


DEBUG = False


def build(NLOC=8, NLAYERS=2):
    nc = bass.Bass("TRN2", target_bir_lowering=False)
    TL = NMETA + NLOC * CH
    NG = 4 * NLOC
    NTOK = NMETA + NG * CH
    NKT = 1 + 4 * NG
    NSUB = 1 + 4 * NLOC
    NXL = NLOC * CH

    def din(name, shape, dt=F32):
        return nc.dram_tensor(name, shape, dt, kind="ExternalInput")

    x_in = din("x", [NXL, D])
    meta_in = din("meta", [NMETA, D])
    w_in = din("w_in", [2, D, DIN])
    w_out = din("w_out", [2, D, D])
    w_gate = din("w_gate", [2, D, DFF])
    w_up = din("w_up", [2, D, DFF])
    w_down = din("w_down", [2, DFF, D])
    vecs_in = din("vecs", [128, NV])
    bf_in = din("bf", [8, 2])
    identf_in = din("identf", [128, 128])
    identb_in = din("identb", [128, 128], BF16)
    tri_in = din("tri", [128, 128], BF16)
    oh_in = din("oh", [128, 4])
    pen_in = din("pen", [128, NLOC * NKT])
    out_d = nc.dram_tensor("out", [NXL, D], F32, kind="ExternalOutput")

    def dscr(name, shape, dt):
        if DEBUG and not name.startswith("w") and not name.endswith("_in") and not name.endswith("_out"):
            return nc.dram_tensor(name, shape, dt, kind="ExternalOutput")
        return nc.dram_tensor(name, shape, dt)

    wib = [dscr(f"wib{l}", [48, 128, KC, 128], BF16) for l in range(2)]
    wfb = [dscr(f"wfb{l}", [128, KC, 8], BF16) for l in range(2)]
    wob = [dscr(f"wob{l}", [16, 128, KC, 128], BF16) for l in range(2)]
    wgb = [dscr(f"wgb{l}", [FC, 128, KC, 128], BF16) for l in range(2)]
    wub = [dscr(f"wub{l}", [FC, 128, KC, 128], BF16) for l in range(2)]
    wdb = [dscr(f"wdb{l}", [16, 128, FC, 128], BF16) for l in range(2)]
    hT_d = dscr("hT", [D, TL], F32)
    qT_d = dscr("qT", [1024, TL], BF16)
    kmeta_d = dscr("kmeta", [1024, NMETA], BF16)
    vmeta_d = dscr("vmeta", [1024, NMETA], BF16)
    kg_in = [dscr(f"kg{i}_in", [1024, CH], BF16) for i in range(NLOC)]
    vg_in = [dscr(f"vg{i}_in", [1024, CH], BF16) for i in range(NLOC)]
    kg_out = [dscr(f"kg{i}_out", [NRANK * 1024, CH], BF16) for i in range(NLOC)]
    vg_out = [dscr(f"vg{i}_out", [NRANK * 1024, CH], BF16) for i in range(NLOC)]
    gbT_d = dscr("gbT", [1024, TL], BF16)
    uT_d = dscr("uT", [1024, TL], BF16)
    ut_in = dscr("ut_in", [1024, 2 * NLOC], BF16)
    ut_out = dscr("ut_out", [NRANK * 1024, 2 * NLOC], BF16)
    lf_in = dscr("lf_in", [8, NXL], F32)
    lf_out = dscr("lf_out", [NRANK * 8, NXL], F32)
    lfm_d = dscr("lfm", [8, NMETA], F32)
    yT_d = dscr("yT", [1024, TL], BF16)

    B = {k: Buf() for k in ["hT", "qT", "kmeta", "vmeta", "kg_in", "vg_in", "kg_out", "vg_out", "gbT", "uT",
                            "ut_in", "ut_out", "lf_in", "lf_out", "lfm", "yT", "out", "const"]}
    for i_ in range(NLOC):
        for n_ in ["kg_in", "vg_in", "kg_out", "vg_out"]:
            B[(n_, i_)] = Buf()
    Bwg = {(l, g): Buf() for l in range(2) for g in range(12)}
    Bw = {(n, l): Buf() for n in ["wib", "wfb", "wob", "wgb", "wub", "wdb"] for l in range(2)}

    es = ExitStack()
    with es:
        def sem(name):
            return Sem(es.enter_context(nc.semaphore(name)))

        PE = Eng("pe", nc.tensor, sem("s_pe"))
        ACT = Eng("act", nc.scalar, sem("s_act"))
        DVE = Eng("dve", nc.vector, sem("s_dve"))
        POOL = Eng("pool", nc.gpsimd, sem("s_pool"))
        SP = Eng("sp", nc.sync, sem("s_sp"))
        SPQ = DmaQ(SP, [sem(f"dq{i}") for i in range(24)])
        PQ = DmaQ(POOL, [sem(f"pq{i}") for i in range(16)])
        conv_jobs = []
        ccsems = [sem(f"cc{i}") for i in range(8)]
        cc_i = [0]

        sb_n = [0]

        def sb(name, shape, dt, stack=None):
            sb_n[0] += 1
            return (stack or es).enter_context(nc.sbuf_tensor(f"sb{sb_n[0]}_{name}", shape, dt))

        pbank = [es.enter_context(nc.psum_tensor(f"pb{i}", [128, 512], F32)) for i in range(8)]
        Bp = [Buf() for _ in range(8)]

        identf = sb("identf", [128, 128], F32)
        identb = sb("identb", [128, 128], BF16)
        tri = sb("tri", [128, 128], BF16)
        vecs = sb("vecs", [128, NV], F32)
        bfv = sb("bfv", [8, 2], F32)
        nbf = sb("nbf", [8, 2], F32)
        oh = sb("oh", [128, 4], F32)
        pen = sb("pen", [128, NLOC, NKT], F32)
        onesD = sb("onesD", [128, 128], BF16)
        onesG = sb("onesG", [128, 128], BF16)
        ones1 = sb("ones1", [128, 128], BF16)
        ones8f = sb("ones8f", [8, 128], F32)
        zerosf = sb("zerosf", [128, 512], F32)
        Bc = B["const"]
        SPQ.dma(identf[:], identf_in.ap(), writes=[Bc])
        SPQ.dma(identb[:], identb_in.ap(), writes=[Bc])
        SPQ.dma(tri[:], tri_in.ap(), writes=[Bc])
        SPQ.dma(vecs[:], vecs_in.ap(), writes=[Bc])
        SPQ.dma(bfv[:], bf_in.ap(), writes=[Bc])
        SPQ.dma(oh[:], oh_in.ap(), writes=[Bc])
        SPQ.dma(pen[:], pen_in.ap().rearrange("p (a b) -> p a b", a=NLOC), writes=[Bc])
        op(DVE, lambda: nc.vector.memset(onesD[:], 1.0 / D), writes=[Bc])
        op(DVE, lambda: nc.vector.memset(onesG[:], 1.0 / 128), writes=[Bc])
        op(DVE, lambda: nc.vector.memset(ones1[:], 1.0), writes=[Bc])
        op(DVE, lambda: nc.vector.memset(ones8f[:], 1.0), writes=[Bc])
        op(DVE, lambda: nc.vector.memset(zerosf[:], 0.0), writes=[Bc])
        op(DVE, lambda: nc.vector.tensor_scalar(nbf[:], bfv[:], -1.0, None, ALU.mult), reads=[Bc], writes=[Bc])

        def vcol(c):
            return vecs[:, c:c + 1]

        def convert(l, defer=False):
            jobs = []

            class _Q:
                @staticmethod
                def dma(out, in_, writes):
                    jobs.append((out, in_, writes))
            PQ_ = _Q
            convert_body(l, PQ_)
            if defer:
                conv_jobs.extend(jobs)
            else:
                for (o_, i_, w_) in jobs[:49]:
                    PQ.dma(o_, i_, writes=w_)
                conv_jobs.extend(jobs[49:])

        def conv_slice(n):
            for _ in range(min(n, len(conv_jobs))):
                o_, i_, w_ = conv_jobs.pop(0)
                PQ.dma(o_, i_, writes=w_)

        def convert_body(l, PQ):
            PQ.dma(wfb[l].ap(), w_in.ap()[l, :, 6144:6152].rearrange("(k p) j -> p k j", p=128),
                   writes=[Bw[("wfb", l)]])
            for bi, c0 in enumerate(W_IN_COLS):
                PQ.dma(wib[l].ap()[bi], w_in.ap()[l, :, c0:c0 + 128].rearrange("(k p) j -> p k j", p=128),
                       writes=[Bwg[(l, bi // 4)]])
            for m in range(16):
                PQ.dma(wob[l].ap()[m], w_out.ap()[l, :, m * 128:(m + 1) * 128].rearrange("(k p) j -> p k j", p=128),
                       writes=[Bw[("wob", l)]])
            for f in range(FC):
                PQ.dma(wgb[l].ap()[f], w_gate.ap()[l, :, f * 128:(f + 1) * 128].rearrange("(k p) j -> p k j", p=128),
                       writes=[Bw[("wgb", l)]])
                PQ.dma(wub[l].ap()[f], w_up.ap()[l, :, f * 128:(f + 1) * 128].rearrange("(k p) j -> p k j", p=128),
                       writes=[Bw[("wub", l)]])
            for m in range(16):
                for half in range(2):
                    PQ.dma(wdb[l].ap()[m, :, half * 22:(half + 1) * 22, :],
                           w_down.ap()[l, half * 2816:(half + 1) * 2816, m * 128:(m + 1) * 128].rearrange(
                               "(k p) j -> p k j", p=128),
                           writes=[Bw[("wdb", l)]])

        convert(0)

        ev_i = [0]

        def evac(out, in_, reads, writes):
            ev_i[0] ^= 1
            if ev_i[0]:
                op(ACT, lambda: nc.scalar.copy(out, in_), reads, writes)
            else:
                op(DVE, lambda: nc.vector.tensor_copy(out, in_), reads, writes)

        def rsqrt_act(out, in_, eps, reads, writes):
            op(ACT, lambda: nc.scalar.activation(out, in_, AF.Ln, bias=eps, scale=1.0), reads, writes)
            op(ACT, lambda: nc.scalar.activation(out, out, AF.Exp, scale=-0.5), writes, writes)

        acc_i = [0]

        def next_acc(banks=(0, 1, 2, 3, 4, 5)):
            acc_i[0] = (acc_i[0] + 1) % len(banks)
            return banks[acc_i[0]]

        def unit_info(u):
            if u == 0:
                return 0, [(0, NMETA), (NMETA, CH)], NMETA + CH
            return NMETA + u * CH, [(0, CH)], CH

        def allgather(src, dst, bsrc, bdst):
            s = ccsems[cc_i[0] % len(ccsems)]
            cc_i[0] += 1
            POOL.wait_t(s, s.val)
            POOL.wait_bufs([bsrc], [bdst])
            nc.gpsimd.collective_compute("AllGather", ALU.bypass, replica_groups=[[0, 1, 2, 3], [4, 5, 6, 7]],
                                         ins=[src.ap().opt()], outs=[dst.ap().opt()]).then_inc(s.h, 1)
            s.val += 1
            _commit(s, s.val, [bsrc], [bdst])

        TUM = NMETA + CH

        def phase1(l):
            with ExitStack() as st:
                hT = sb("p1_hT", [128, KC, TUM], F32, st)
                Bh = [Buf() for _ in range(KC)]
                sq = sb("p1_sq", [128, KC, TUM], BF16, st)
                Bsq = [Buf() for _ in range(KC)]
                hns = [sb(f"p1_hn{i}", [128, KC, TUM], BF16, st) for i in range(2)]
                Bhns = [Buf(), Buf()]
                rstd = sb("p1_rstd", [128, TUM], F32, st)
                Brs = Buf()
                wt = [sb(f"p1_wt{i}", [128, 4, KC, 128], BF16, st) for i in range(3)]
                Bwt = [Buf(), Buf(), Buf()]
                wl_issued = [0]

                def ensure_w(upto):
                    while wl_issued[0] < min(upto, 12 * NLOC):
                        gi = wl_issued[0]
                        SPQ.dma(wt[gi % 3][:], wib[l].ap()[4 * (gi % 12):4 * (gi % 12) + 4].rearrange(
                            "b p k j -> p b k j"), reads=[Bwg[(l, gi % 12)]], writes=[Bwt[gi % 3]])
                        wl_issued[0] += 1
                wf = sb("p1_wf", [128, KC, 8], BF16, st)
                Bwf = Buf()
                ost = [sb(f"p1_ost{i}", [128, TUM], BF16, st) for i in range(3)]
                Bost = [Buf() for _ in range(3)]
                gcs = sb("p1_gcs", [128, TUM], BF16, st)
                Bgcs = Buf()
                lfe = sb("p1_lfe", [8, TUM], F32, st)
                lfs = sb("p1_lfs", [8, TUM], F32, st)
                Blf = Buf()
                if l == 0:
                    xt = sb("p1_xt", [128, 4, D], F32, st)
                    Bxt = [Buf() for _ in range(4)]
                    xm = sb("p1_xm", [NMETA, D], F32, st)
                    Bxm = Buf()
                SPQ.dma(wf[:], wfb[l].ap(), reads=[Bw[("wfb", l)]], writes=[Bwf])
                ost_i = [0]
                def pro_load(u):
                    t0, segs, TU = unit_info(u)
                    if l == 0:
                        for (so, n) in segs:
                            if n == NMETA:
                                SPQ.dma(xm[:], meta_in.ap(), writes=[Bxm])
                            else:
                                for s4 in range(4):
                                    r0 = u * CH + s4 * 128
                                    SPQ.dma(xt[:, s4, :], x_in.ap()[r0:r0 + 128, :], writes=[Bxt[s4]])
                    else:
                        SPQ.dma(hT[:, :, :TU], hT_d.ap()[:, t0:t0 + TU].rearrange("(k p) t -> p k t", p=128),
                                reads=[B["hT"]], writes=Bh)

                def prologue(u):
                    t0, segs, TU = unit_info(u)
                    hn, Bhn = hns[u % 2], Bhns[u % 2]
                    if l == 0:
                        for (so, n) in segs:
                            if n == NMETA:
                                for kc in range(KC):
                                    op(PE, lambda: nc.tensor.transpose(pbank[6][:, kc * 16:(kc + 1) * 16],
                                                                       xm[:, kc * 128:(kc + 1) * 128],
                                                                       identf[:NMETA, :NMETA]),
                                       reads=[Bxm, Bc], writes=[Bp[6]], sig=(kc == KC - 1))
                                for kc in range(KC):
                                    evac(hT[:, kc, so:so + n], pbank[6][:, kc * 16:(kc + 1) * 16], [Bp[6]], [Bh[kc]])
                            else:
                                for kc in range(KC):
                                    bk = next_acc()
                                    for s4 in range(4):
                                        op(PE, lambda: nc.tensor.transpose(pbank[bk][:, s4 * 128:(s4 + 1) * 128],
                                                                           xt[:, s4, kc * 128:(kc + 1) * 128],
                                                                           identf[:]),
                                           reads=[Bxt[s4], Bc], writes=[Bp[bk]], sig=(s4 == 3))
                                    evac(hT[:, kc, so:so + n], pbank[bk][:, :n], [Bp[bk]], [Bh[kc]])
                        SPQ.dma(hT_d.ap()[:, t0:t0 + TU].rearrange("(k p) t -> p k t", p=128), hT[:, :, :TU],
                                reads=Bh, writes=[B["hT"]])
                    for kc in range(KC):
                        op(ACT, lambda: nc.scalar.activation(sq[:, kc, :TU], hT[:, kc, :TU], AF.Square),
                           reads=[Bh[kc]], writes=[Bsq[kc]])
                    for (so, n) in segs:
                        for kc in range(KC):
                            op(PE, lambda: nc.tensor.matmul(pbank[6][:, :n], lhsT=onesD[:], rhs=sq[:, kc, so:so + n],
                                                            start=(kc == 0), stop=(kc == KC - 1)),
                               reads=[Bsq[kc], Bc], writes=[Bp[6]], sig=(kc == KC - 1))
                        rsqrt_act(rstd[:, so:so + n], pbank[6][:, :n], EPS, [Bp[6]], [Brs])
                    for kc in range(KC):
                        op(DVE, lambda: nc.vector.scalar_tensor_tensor(hn[:, kc, :TU], hT[:, kc, :TU],
                                                                       vcol(l * 16 + kc), rstd[:, :TU],
                                                                       ALU.mult, ALU.mult),
                           reads=[Bh[kc], Brs, Bc], writes=[Bhn])
                def forget_block(u, segs, TU, hn, Bhn):
                    for (so, n) in segs:
                        for kc in range(KC):
                            op(PE, lambda: nc.tensor.matmul(pbank[7][:8, :n], lhsT=wf[:, kc, :], rhs=hn[:, kc, so:so + n],
                                                            start=(kc == 0), stop=(kc == KC - 1)),
                               reads=[Bwf, Bhn], writes=[Bp[7]], sig=(kc == KC - 1))
                        op(ACT, lambda: nc.scalar.activation(lfe[:, so:so + n], pbank[7][:8, :n], AF.Exp,
                                                             bias=nbf[:, l:l + 1], scale=-1.0),
                           reads=[Bp[7], Bc], writes=[Blf])
                        op(ACT, lambda: nc.scalar.activation(lfe[:, so:so + n], lfe[:, so:so + n], AF.Ln,
                                                             bias=1.0, scale=1.0),
                           reads=[Blf], writes=[Blf])
                        op(DVE, lambda: nc.vector.tensor_scalar(lfs[:, so:so + n], lfe[:, so:so + n], -1.0, None,
                                                                ALU.mult),
                           reads=[Blf], writes=[Blf])
                    if u == 0:
                        SPQ.dma(lfm_d.ap(), lfs[:, :NMETA], reads=[Blf], writes=[B["lfm"]])
                    SPQ.dma(lf_in.ap()[:, u * CH:(u + 1) * CH], lfs[:, TU - CH:TU], reads=[Blf], writes=[B["lf_in"]])
                    if u == NLOC - 1:
                        allgather(lf_in, lf_out, B["lf_in"], B["lf_out"])

                def proj(u):
                    t0, segs, TU = unit_info(u)
                    hn, Bhn = hns[u % 2], Bhns[u % 2]
                    forget_block(u, segs, TU, hn, Bhn)
                    for bg in range(12):
                        gi = u * 12 + bg
                        ensure_w(gi + 3)
                        if bg == 5 and u + 2 < NLOC:
                            pro_load(u + 2)
                        for b4 in range(4):
                            blk = 4 * bg + b4
                            kind = blk // 8 if blk < 32 else (4 if blk % 2 == 0 else 5)
                            if kind == 4:
                                dst, Bdst = gcs, Bgcs
                            else:
                                oi = ost_i[0] = (ost_i[0] + 1) % 3
                                dst, Bdst = ost[oi], Bost[oi]
                            for (so, n) in segs:
                                bk = next_acc()
                                for kc in range(KC):
                                    op(PE, lambda: nc.tensor.matmul(pbank[bk][:, :n], lhsT=wt[gi % 3][:, b4, kc, :],
                                                                    rhs=hn[:, kc, so:so + n],
                                                                    start=(kc == 0), stop=(kc == KC - 1)),
                                       reads=[Bwt[gi % 3], Bhn], writes=[Bp[bk]], sig=(kc == KC - 1))
                                if kind == 5:
                                    op(DVE, lambda: nc.vector.tensor_tensor(dst[:, so:so + n], pbank[bk][:, :n],
                                                                            gcs[:, so:so + n], ALU.mult),
                                       reads=[Bp[bk], Bgcs], writes=[Bdst])
                                else:
                                    evac(dst[:, so:so + n], pbank[bk][:, :n], [Bp[bk]], [Bdst])
                            if kind == 4:
                                continue
                            xo = u * CH
                            if kind == 0:
                                SPQ.dma(qT_d.ap()[blk * 128:(blk + 1) * 128, t0:t0 + TU], dst[:, :TU],
                                        reads=[Bdst], writes=[B["qT"]])
                            elif kind in (1, 2):
                                hb = blk - 8 * kind
                                md, gd, bm, bgn = ((kmeta_d, kg_in, "kmeta", "kg_in") if kind == 1
                                                   else (vmeta_d, vg_in, "vmeta", "vg_in"))
                                if u == 0:
                                    SPQ.dma(md.ap()[hb * 128:(hb + 1) * 128, :], dst[:, :NMETA],
                                            reads=[Bdst], writes=[B[bm]])
                                SPQ.dma(gd[u].ap()[hb * 128:(hb + 1) * 128, :], dst[:, TU - CH:TU],
                                        reads=[Bdst], writes=[B[(bgn, u)]])
                                if hb == 7:
                                    go = kg_out if kind == 1 else vg_out
                                    allgather(gd[u], go[u], B[(bgn, u)], B[(bgn.replace("_in", "_out"), u)])
                            elif kind == 3:
                                g = blk - 24
                                SPQ.dma(gbT_d.ap()[g * 128:(g + 1) * 128, t0:t0 + TU], dst[:, :TU],
                                        reads=[Bdst], writes=[B["gbT"]])
                            else:
                                g = (blk - 32) // 2
                                SPQ.dma(uT_d.ap()[g * 128:(g + 1) * 128, t0:t0 + TU], dst[:, :TU],
                                        reads=[Bdst], writes=[B["uT"]])
                                SPQ.dma(ut_in.ap()[g * 128:(g + 1) * 128, 2 * u:2 * u + 2], dst[:, TU - 2:TU],
                                        reads=[Bdst], writes=[B["ut_in"]])
                pro_load(0)
                prologue(0)
                if NLOC > 1:
                    pro_load(1)
                for u in range(NLOC):
                    if u + 1 < NLOC:
                        prologue(u + 1)
                    proj(u)
            allgather(ut_in, ut_out, B["ut_in"], B["ut_out"])


        def barrier():
            engs = [PE, ACT, DVE, POOL]
            dsems = SPQ.sems + PQ.sems
            for E in engs + [SP]:
                for F_ in engs:
                    if F_ is not E:
                        E.wait_t(F_.sem, F_.cnt)
                for s in dsems:
                    E.wait_t(s, s.val)

        def gcols(jj):
            rho, lam = chunk_owner(jj)
            return rho, lam * CH

        def jmax(lam):
            return 8 * (lam // 2) + (3 if lam % 2 == 0 else 7)

        def cands(lam):
            return [chunk_global(r, lam) for r in range(4)]

        def phase2(l):
            with ExitStack() as st:
                CTn = sb("p2_CTn", [128, NKT, 8], F32, st)
                CTo = sb("p2_CTo", [128, 4 * NLOC, 8], F32, st)
                Rbc = sb("p2_Rbc", [128, 8 * NSUB], F32, st)
                Btab = Buf()
                with ExitStack() as st2:
                    lfF = sb("p2_lfF", [8, NTOK], F32, st2)
                    cF = sb("p2_cF", [8, NTOK], F32, st2)
                    cown = sb("p2_cown", [8, NXL], F32, st2)
                    Rm = sb("p2_Rm", [8, NSUB], F32, st2)
                    Dh = sb("p2_Dh", [8, NSUB], F32, st2)
                    Bl, Bcf, Bco, Brm, Bdh = Buf(), Buf(), Buf(), Buf(), Buf()
                    SPQ.dma(lfF[:, :NMETA], lfm_d.ap(), reads=[B["lfm"]], writes=[Bl])
                    for jj in range(NG):
                        rho, co = gcols(jj)
                        SPQ.dma(lfF[:, NMETA + jj * CH:NMETA + (jj + 1) * CH],
                                lf_out.ap()[rho * 8:(rho + 1) * 8, co:co + CH], reads=[B["lf_out"]], writes=[Bl])
                    pos = 0
                    while pos < NTOK:
                        n = min(2048, NTOK - pos)
                        init = 0.0 if pos == 0 else cF[:, pos - 1:pos]
                        op(DVE, lambda: nc.vector.tensor_tensor_scan(cF[:, pos:pos + n], lfF[:, pos:pos + n],
                                                                     lfF[:, pos:pos + n], init, ALU.add, ALU.min),
                           reads=[Bl, Bcf], writes=[Bcf])
                        pos += n
                    for lam in range(NLOC):
                        cs = cands(lam)
                        dstc = cown[:, lam * CH:(lam + 1) * CH]
                        for r4 in range(4):
                            src = cF[:, NMETA + cs[r4] * CH:NMETA + (cs[r4] + 1) * CH]
                            if r4 == 0:
                                op(DVE, lambda: nc.vector.tensor_scalar(dstc, src, oh[:8, 0:1], None, ALU.mult),
                                   reads=[Bcf, Bc], writes=[Bco])
                            else:
                                op(DVE, lambda: nc.vector.scalar_tensor_tensor(dstc, src, oh[:8, r4:r4 + 1], dstc,
                                                                               ALU.mult, ALU.add),
                                   reads=[Bcf, Bc, Bco], writes=[Bco])
                    op(DVE, lambda: nc.vector.tensor_tensor(Rm[:, 0:1], cF[:, 0:1], cF[:, NMETA - 1:NMETA], ALU.add),
                       reads=[Bcf], writes=[Brm])
                    for s in range(4 * NLOC):
                        a = s * 128
                        op(DVE, lambda: nc.vector.tensor_tensor(Rm[:, 1 + s:2 + s], cown[:, a:a + 1],
                                                                cown[:, a + 127:a + 128], ALU.add),
                           reads=[Bco], writes=[Brm])
                    for h in range(NH):
                        op(DVE, lambda: nc.vector.tensor_scalar(Dh[:], Rm[:], identf[:8, h:h + 1], None, ALU.mult),
                           reads=[Brm, Bc], writes=[Bdh])
                        op(PE, lambda: nc.tensor.matmul(pbank[6][:, h * NSUB:(h + 1) * NSUB], lhsT=ones8f[:], rhs=Dh[:],
                                                        start=True, stop=True),
                           reads=[Bdh, Bc], writes=[Bp[6]])
                    op(DVE, lambda: nc.vector.tensor_scalar(Rbc[:], pbank[6][:, :8 * NSUB], 0.5 / SCALE, None, ALU.mult),
                       reads=[Bp[6]], writes=[Btab])
                    for b0 in range(0, NKT, 64):
                        cnt = min(64, NKT - b0)
                        bk = next_acc()
                        for i in range(cnt):
                            kt = b0 + i
                            k0, nk = (0, NMETA) if kt == 0 else (NMETA + (kt - 1) * 128, 128)
                            op(PE, lambda: nc.tensor.transpose(pbank[bk][:nk, i * 8:(i + 1) * 8], cF[:, k0:k0 + nk],
                                                               identf[:8, :8]),
                               reads=[Bcf, Bc], writes=[Bp[bk]], sig=(i == cnt - 1))
                        op(DVE, lambda: nc.vector.tensor_scalar(
                            CTn[:, b0:b0 + cnt, :], pbank[bk][:, :cnt * 8].rearrange("p (a b) -> p a b", b=8),
                            -1.0 / SCALE, None, ALU.mult), reads=[Bp[bk]], writes=[Btab])
                    bk = next_acc()
                    for i in range(4 * NLOC):
                        op(PE, lambda: nc.tensor.transpose(pbank[bk][:, i * 8:(i + 1) * 8], cown[:, i * 128:(i + 1) * 128],
                                                           identf[:8, :8]),
                           reads=[Bco, Bc], writes=[Bp[bk]], sig=(i == 4 * NLOC - 1))
                    op(DVE, lambda: nc.vector.tensor_scalar(
                        CTo[:], pbank[bk][:, :4 * NLOC * 8].rearrange("p (a b) -> p a b", b=8),
                        -1.0 / SCALE, None, ALU.mult), reads=[Bp[bk]], writes=[Btab])
                    if DEBUG:
                        dC = nc.dram_tensor(f"dbg_CTn{l}", [128, NKT * 8], F32, kind="ExternalOutput")
                        dR = nc.dram_tensor(f"dbg_Rbc{l}", [128, 8 * NSUB], F32, kind="ExternalOutput")
                        dO = nc.dram_tensor(f"dbg_CTo{l}", [128, 4 * NLOC * 8], F32, kind="ExternalOutput")
                        dcF = nc.dram_tensor(f"dbg_cF{l}", [8, NTOK], F32, kind="ExternalOutput")
                        SPQ.dma(dC.ap(), CTn[:].rearrange("p a b -> p (a b)"), reads=[Btab], writes=[B["out"]])
                        SPQ.dma(dR.ap(), Rbc[:], reads=[Btab], writes=[B["out"]])
                        SPQ.dma(dO.ap(), CTo[:].rearrange("p a b -> p (a b)"), reads=[Btab], writes=[B["out"]])
                        SPQ.dma(dcF.ap(), cF[:], reads=[Bcf], writes=[B["out"]])
                    barrier()

                kT = [sb(f"p2_kT{i}", [128, NTOK], BF16, st) for i in range(2)]
                BkT = [Buf(), Buf()]
                kown = [sb(f"p2_ko{i}", [128, NXL], BF16, st) for i in range(2)]
                Bko = [Buf(), Buf()]
                vT = sb("p2_vT", [128, NTOK], BF16, st)
                BvT = Buf()
                vownT = sb("p2_voT", [128, NXL], BF16, st)
                BvoT = Buf()
                V = sb("p2_V", [128, NKT, 128], BF16, st)
                BV = Buf()
                Vo = sb("p2_Vo", [128, 4 * NLOC, 128], BF16, st)
                BVo = Buf()
                qs = [sb(f"p2_q{i}", [128, CH], BF16, st) for i in range(2)]
                Bq = [Buf(), Buf()]
                Rt = [sb(f"p2_Rt{i}", [128, CH], F32, st) for i in range(2)]
                BRt = [Buf(), Buf()]
                CTl = [sb(f"p2_CTl{i}", [128, NKT], F32, st) for i in range(2)]
                BCl = [Buf(), Buf()]
                Tb = [sb(f"p2_T{i}", [128, CH], F32, st) for i in range(4)]
                BT = [Buf() for _ in range(4)]
                Pb = [sb(f"p2_P{i}", [128, CH], BF16, st) for i in range(5)]
                BP = [Buf() for _ in range(5)]
                LA = 3
                pending = []
                Osb = sb("p2_Osb", [128, CH], F32, st)
                d2 = sb("p2_d2", [128, CH], F32, st)
                sqo = sb("p2_sqo", [128, CH], BF16, st)
                uu = sb("p2_uu", [128, CH], F32, st)
                yst = [sb(f"p2_y{i}", [128, CH], BF16, st) for i in range(2)]
                BOs, Bd2, Bsqo, Buu = Buf(), Buf(), Buf(), Buf()
                Byst = [Buf(), Buf()]
                TRB = [7, 4]
                pb67b = [pbank[7][:].bitcast(BF16), pbank[4][:].bitcast(BF16)]

                def head_jobs(h):
                    kb, Bk = kT[h % 2], BkT[h % 2]
                    hs = slice(h * 128, (h + 1) * 128)
                    jobs = []

                    def J(out, in_, r, w):
                        jobs.append(lambda: SPQ.dma(out, in_, reads=r, writes=w))
                    J(vT[:, :NMETA], vmeta_d.ap()[hs, :], [B["vmeta"]], [BvT])
                    for jj in range(NG):
                        rho, lp = chunk_owner(jj)
                        rs_ = slice(rho * 1024 + h * 128, rho * 1024 + (h + 1) * 128)
                        J(vT[:, NMETA + jj * CH:NMETA + (jj + 1) * CH], vg_out[lp].ap()[rs_, :],
                          [B[("vg_out", lp)]], [BvT])
                    for lam in range(NLOC):
                        J(vownT[:, lam * CH:(lam + 1) * CH], vg_in[lam].ap()[hs, :], [B[("vg_in", lam)]], [BvoT])
                    J(kb[:, :NMETA], kmeta_d.ap()[hs, :], [B["kmeta"]], [Bk])
                    for jj in range(NG):
                        rho, lp = chunk_owner(jj)
                        rs_ = slice(rho * 1024 + h * 128, rho * 1024 + (h + 1) * 128)
                        J(kb[:, NMETA + jj * CH:NMETA + (jj + 1) * CH], kg_out[lp].ap()[rs_, :],
                          [B[("kg_out", lp)]], [Bk])
                    for lam in range(NLOC):
                        J(kown[h % 2][:, lam * CH:(lam + 1) * CH], kg_in[lam].ap()[hs, :], [B[("kg_in", lam)]],
                          [Bko[h % 2]])
                    return jobs

                def load_head(h):
                    for j_ in head_jobs(h):
                        j_()

                seglist_all = []
                for u_ in range(NLOC):
                    if u_ == 0:
                        seglist_all.append((u_, "meta", 0, NMETA))
                    seglist_all.append((u_, "chunk", NMETA + u_ * CH, CH))
                nsegs = len(seglist_all)

                def load_q(si_, h_, t0_, nq_):
                    SPQ.dma(qs[si_ % 2][:, :nq_], qT_d.ap()[h_ * 128:(h_ + 1) * 128, t0_:t0_ + nq_], reads=[B["qT"]],
                            writes=[Bq[si_ % 2]])

                tcount = [0]
                segcount = [0]
                load_head(0)
                load_q(1, 0, seglist_all[0][2], seglist_all[0][3])
                for h in range(NH):
                    while pending:
                        pending.pop(0)()
                    TRB[1] = 4 + (segcount[0] + 1) % 2
                    pb67b[1] = pbank[TRB[1]][:].bitcast(BF16)
                    ti = 0
                    for b0 in range(0, NKT, 8):
                        cnt = min(8, NKT - b0)
                        pi = ti % 2
                        ti += 1
                        for i in range(cnt):
                            kt = b0 + i
                            k0, nk = (0, NMETA) if kt == 0 else (NMETA + (kt - 1) * 128, 128)
                            op(PE, lambda: nc.tensor.transpose(pb67b[pi][:nk, i * 128:(i + 1) * 128],
                                                               vT[:, k0:k0 + nk], identb[:]),
                               reads=[BvT, Bc], writes=[Bp[TRB[pi]]], sig=(i == cnt - 1))
                        evac(V[:, b0:b0 + cnt, :], pb67b[pi][:, :cnt * 128].rearrange("p (a d) -> p a d", d=128),
                             [Bp[TRB[pi]]], [BV])
                    for b0 in range(0, 4 * NLOC, 8):
                        cnt = min(8, 4 * NLOC - b0)
                        pi = ti % 2
                        ti += 1
                        for i in range(cnt):
                            op(PE, lambda: nc.tensor.transpose(pb67b[pi][:, i * 128:(i + 1) * 128],
                                                               vownT[:, (b0 + i) * 128:(b0 + i + 1) * 128], identb[:]),
                               reads=[BvoT, Bc], writes=[Bp[TRB[pi]]], sig=(i == cnt - 1))
                        evac(Vo[:, b0:b0 + cnt, :], pb67b[pi][:, :cnt * 128].rearrange("p (a d) -> p a d", d=128),
                             [Bp[TRB[pi]]], [BVo])
                    nxt_jobs = head_jobs(h + 1) if h + 1 < NH else []
                    per_seg = -(-len(nxt_jobs) // nsegs)
                    kb, Bk = kT[h % 2], BkT[h % 2]
                    ko, Bkow = kown[h % 2], Bko[h % 2]
                    for sidx, (u, skind, t0, nq) in enumerate(seglist_all):
                        if True:
                            si = segcount[0] = segcount[0] + 1
                            conv_slice(2)
                            for _ in range(min(per_seg, len(nxt_jobs))):
                                nxt_jobs.pop(0)()
                            if sidx + 1 < nsegs:
                                load_q(si + 1, h, seglist_all[sidx + 1][2], seglist_all[sidx + 1][3])
                            elif h + 1 < NH:
                                load_q(si + 1, h + 1, seglist_all[0][2], seglist_all[0][3])
                            q, Bqq = qs[si % 2], Bq[si % 2]
                            rt, Brt = Rt[si % 2], BRt[si % 2]
                            ctl, Bcl = CTl[si % 2], BCl[si % 2]
                            ob, db = 4 + si % 2, 6
                            tiles = []
                            if skind == "meta":
                                op(DVE, lambda: nc.vector.tensor_scalar(rt[:, :nq], zerosf[:, :nq],
                                                                        Rbc[:, h * NSUB:h * NSUB + 1], None, ALU.add),
                                   reads=[Btab, Bc], writes=[Brt])
                                tiles.append((kb[:, 0:NMETA], V[:NMETA, 0, :], CTn[:NMETA, 0, h:h + 1], NMETA, 0,
                                              [Bk], [BV], [Btab]))
                            else:
                                lam = u
                                for sj in range(4):
                                    s = 1 + 4 * lam + sj
                                    op(DVE, lambda: nc.vector.tensor_scalar(
                                        rt[:, sj * 128:(sj + 1) * 128], zerosf[:, :128],
                                        Rbc[:, h * NSUB + s:h * NSUB + s + 1], None, ALU.add),
                                       reads=[Btab, Bc], writes=[Brt])
                                npast = 1 + 4 * jmax(lam)
                                op(DVE, lambda: nc.vector.tensor_tensor(ctl[:, :npast], CTn[:, :npast, h],
                                                                        pen[:, lam, :npast], ALU.add),
                                   reads=[Btab, Bc], writes=[Bcl])
                                tiles.append((kb[:, 0:NMETA], V[:NMETA, 0, :], ctl[:NMETA, 0:1], NMETA, None,
                                              [Bk], [BV], [Bcl]))
                                for kt in range(1, npast):
                                    k0 = NMETA + (kt - 1) * 128
                                    tiles.append((kb[:, k0:k0 + 128], V[:, kt, :], ctl[:, kt:kt + 1], 128, None,
                                                  [Bk], [BV], [Bcl]))
                                for i in range(4):
                                    k0 = lam * CH + i * 128
                                    tiles.append((ko[:, k0:k0 + 128], Vo[:, 4 * lam + i, :],
                                                  CTo[:, 4 * lam + i, h:h + 1], 128, i, [Bkow], [BVo], [Btab]))
                            nt = len(tiles)

                            def emit_S(i):
                                kap, vap, bcol, nk, dg, rk, rv, rb = tiles[i]
                                sbk = (tcount[0] + i) % 4
                                op(PE, lambda: nc.tensor.matmul(pbank[sbk][:nk, :nq], lhsT=kap, rhs=q[:, :nq],
                                                                start=True, stop=True),
                                   reads=rk + [Bqq], writes=[Bp[sbk]])

                            for i in range(min(LA, nt)):
                                emit_S(i)
                            for i in range(nt):
                                if i + LA < nt:
                                    emit_S(i + LA)
                                kap, vap, bcol, nk, dg, rk, rv, rb = tiles[i]
                                g = tcount[0] + i
                                sbk = g % 4
                                T_, BT_ = Tb[g % 4], BT[g % 4]
                                P_, BP_ = Pb[g % 5], BP[g % 5]
                                if i == min(4, nt - 1) and pending:
                                    pending.pop(0)()
                                c0 = 0 if (dg is None or skind == "meta") else dg * 128
                                op(DVE, lambda: nc.vector.scalar_tensor_tensor(T_[:nk, c0:nq], pbank[sbk][:nk, c0:nq],
                                                                               bcol, rt[:nk, c0:nq], ALU.add, ALU.add),
                                   reads=[Bp[sbk], Brt] + rb, writes=[BT_])
                                op(ACT, lambda: nc.scalar.activation(P_[:nk, c0:nq], T_[:nk, c0:nq], AF.Exp, scale=SCALE),
                                   reads=[BT_], writes=[BP_])
                                if dg is not None:
                                    if c0 > 0:
                                        op(POOL, lambda: nc.gpsimd.memset(P_[:nk, :c0], 0.0), reads=[], writes=[BP_])
                                    w = min(128, nq)
                                    op(POOL, lambda: nc.gpsimd.tensor_tensor(P_[:nk, c0:c0 + w], P_[:nk, c0:c0 + w],
                                                                             tri[:nk, :w], ALU.mult),
                                       reads=[Bc], writes=[BP_])
                                op(PE, lambda: nc.tensor.matmul(pbank[ob][:, :nq], lhsT=vap, rhs=P_[:nk, :nq],
                                                                start=(i == 0), stop=(i == nt - 1)),
                                   reads=rv + [BP_], writes=[Bp[ob]], sig=False)
                                op(PE, lambda: nc.tensor.matmul(pbank[db][:, :nq], lhsT=ones1[:nk, :], rhs=P_[:nk, :nq],
                                                                start=(i == 0), stop=(i == nt - 1)),
                                   reads=[Bc, BP_], writes=[Bp[db]], sig=True)
                            tcount[0] += nt
                            ys, Bys = yst[si % 2], Byst[si % 2]
                            op(ACT, lambda: nc.scalar.activation(d2[:, :nq], pbank[db][:, :nq], AF.Ln), reads=[Bp[db]],
                               writes=[Bd2])
                            op(ACT, lambda: nc.scalar.activation(d2[:, :nq], d2[:, :nq], AF.Exp, scale=-1.0), reads=[Bd2],
                               writes=[Bd2])
                            op(DVE, lambda: nc.vector.tensor_tensor(Osb[:, :nq], pbank[ob][:, :nq], d2[:, :nq], ALU.mult),
                               reads=[Bp[ob], Bd2], writes=[BOs])

                            def ep_tail(ys=ys, Bys=Bys, nq=nq, t0=t0, h=h):
                                op(ACT, lambda: nc.scalar.activation(sqo[:, :nq], Osb[:, :nq], AF.Square),
                                   reads=[BOs], writes=[Bsqo])
                                op(PE, lambda: nc.tensor.matmul(pbank[7][:, :nq], lhsT=onesG[:], rhs=sqo[:, :nq],
                                                                start=True, stop=True),
                                   reads=[Bsqo, Bc], writes=[Bp[7]])
                                rsqrt_act(uu[:, :nq], pbank[7][:, :nq], EPS, [Bp[7]], [Buu])
                                op(DVE, lambda: nc.vector.scalar_tensor_tensor(ys[:, :nq], Osb[:, :nq],
                                                                               vcol(80 + l * 16 + h), uu[:, :nq],
                                                                               ALU.mult, ALU.mult),
                                   reads=[BOs, Buu, Bc], writes=[Bys])
                                SPQ.dma(yT_d.ap()[h * 128:(h + 1) * 128, t0:t0 + nq], ys[:, :nq], reads=[Bys],
                                        writes=[B["yT"]])

                            pending.append(ep_tail)
                            if nt <= 4:
                                while pending:
                                    pending.pop(0)()
                while pending:
                    pending.pop(0)()

        def phase34(l, last):
            with ExitStack() as st:
                hT1 = sb("p3_hT1", [128, KC, TUM], F32, st)
                Bh = [Buf() for _ in range(KC)]
                yTs = sb("p3_yTs", [128, KC, TUM], BF16, st)
                By = Buf()
                aT = sb("p3_aT", [128, FC, TUM], BF16, st)
                Ba = [Buf() for _ in range(FC)]
                wA = [sb(f"p3_wA{i}", [128, KC, 128], BF16, st) for i in range(2)]
                wB_ = [sb(f"p3_wB{i}", [128, KC, 128], BF16, st) for i in range(2)]
                BwA = [Buf(), Buf()]
                BwB = [Buf(), Buf()]
                wD = [sb(f"p3_wD{i}", [128, FC, 128], BF16, st) for i in range(2)]
                BwD = [Buf(), Buf()]
                wO = [sb(f"p3_wO{i}", [128, 2, KC, 128], BF16, st) for i in range(2)]
                BwO = [Buf(), Buf()]
                uh = sb("p3_uh", [128, 8, 2 + CH], BF16, st)
                Buh = Buf()
                gb = sb("p3_gb", [128, 8, TUM], BF16, st)
                Bgb = Buf()
                t1 = sb("p3_t1", [128, CH], F32, st)
                cv = sb("p3_cv", [128, CH], F32, st)
                sqc = sb("p3_sqc", [128, CH], BF16, st)
                rsx = sb("p3_rsx", [128, TUM], F32, st)
                sg = [sb(f"p3_sg{i}", [128, CH], F32, st) for i in range(2)]
                Bt1, Bcv, Bsqc, Brsx = Buf(), Buf(), Buf(), Buf()
                Bsg = [Buf(), Buf()]
                tl = sb("p3_tl", [128, NRANK, 8, 2 * NLOC], BF16, st)
                mt = sb("p3_mt", [128, 8, 2], BF16, st)
                hal = sb("p3_hal", [128, 8, 2], F32, st)
                Btl, Bhal = Buf(), Buf()
                if last:
                    otile = [sb(f"p3_ot{i}", [128, D // 2], F32, st) for i in range(2)]
                    Bot = [Buf(), Buf()]
                SPQ.dma(tl[:], ut_out.ap().rearrange("(r g p) c -> p r g c", r=NRANK, g=8), reads=[B["ut_out"]],
                        writes=[Btl])
                SPQ.dma(mt[:], uT_d.ap()[:, NMETA - 2:NMETA].rearrange("(g p) c -> p g c", p=128), reads=[B["uT"]],
                        writes=[Btl])
                ycv = sb("p3_ycv", [128, 8, TUM], BF16, st)
                Bycv = Buf()
                cwb = 112 + l * 24
                oti = [0]
                sgi = [0]
                def conv_gb(u):
                    t0, segs, TU = unit_info(u)
                    SPQ.dma(gb[:, :, :TU], gbT_d.ap()[:, t0:t0 + TU].rearrange("(k p) t -> p k t", p=128),
                            reads=[B["gbT"]], writes=[Bgb])

                def conv_pre(u, so, n):
                    t0, segs, TU = unit_info(u)
                    if True:
                        SPQ.dma(uh[:, :, 2:2 + n], uT_d.ap()[:, t0 + so:t0 + so + n].rearrange("(k p) t -> p k t", p=128),
                                reads=[B["uT"]], writes=[Buh])
                        if n == NMETA:
                            op(DVE, lambda: nc.vector.memset(uh[:, :, 0:2], 0.0), reads=[], writes=[Buh])
                        else:
                            lam = u
                            for r4 in range(4):
                                j = chunk_global(r4, lam)
                                if j == 0:
                                    cand = mt[:]
                                else:
                                    rho, lp = chunk_owner(j - 1)
                                    cand = tl[:, rho, :, 2 * lp:2 * lp + 2]
                                if r4 == 0:
                                    op(DVE, lambda: nc.vector.tensor_scalar(hal[:], cand, oh[:, 0:1], None, ALU.mult),
                                       reads=[Btl, Bc], writes=[Bhal])
                                else:
                                    op(DVE, lambda: nc.vector.scalar_tensor_tensor(hal[:], cand, oh[:, r4:r4 + 1], hal[:],
                                                                                   ALU.mult, ALU.add),
                                       reads=[Btl, Bc, Bhal], writes=[Bhal])
                            op(DVE, lambda: nc.vector.tensor_copy(uh[:, :, 0:2], hal[:]), reads=[Bhal], writes=[Buh])

                def conv_a(u, so, n, g):
                    if True:
                        if True:
                            op(DVE, lambda: nc.vector.tensor_scalar(t1[:, :n], uh[:, g, 0:n], vcol(cwb + g), None, ALU.mult),
                               reads=[Buh, Bc], writes=[Bt1])
                            op(DVE, lambda: nc.vector.scalar_tensor_tensor(t1[:, :n], uh[:, g, 1:n + 1], vcol(cwb + 8 + g),
                                                                           t1[:, :n], ALU.mult, ALU.add),
                               reads=[Buh, Bc, Bt1], writes=[Bt1])
                            op(DVE, lambda: nc.vector.scalar_tensor_tensor(t1[:, :n], uh[:, g, 2:n + 2], vcol(cwb + 16 + g),
                                                                           t1[:, :n], ALU.mult, ALU.add),
                               reads=[Buh, Bc, Bt1], writes=[Bt1])
                            op(DVE, lambda: nc.vector.tensor_tensor(cv[:, :n], t1[:, :n], gb[:, g, so:so + n], ALU.mult),
                               reads=[Bt1, Bgb], writes=[Bcv])
                            op(ACT, lambda: nc.scalar.activation(sqc[:, :n], cv[:, :n], AF.Square),
                               reads=[Bcv], writes=[Bsqc])

                def conv_b(u, so, n, g):
                    if True:
                        if True:
                            op(PE, lambda: nc.tensor.matmul(pbank[6][:, :n], lhsT=onesG[:], rhs=sqc[:, :n],
                                                            start=True, stop=True),
                               reads=[Bsqc, Bc], writes=[Bp[6]])
                            rsqrt_act(rsx[:, :n], pbank[6][:, :n], EPS, [Bp[6]], [Brsx])
                            op(DVE, lambda: nc.vector.scalar_tensor_tensor(ycv[:, g, so:so + n], cv[:, :n],
                                                                           vcol(80 + l * 16 + 8 + g), rsx[:, :n],
                                                                           ALU.mult, ALU.mult),
                               reads=[Bcv, Brsx, Bc], writes=[Bycv])

                def conv(u):
                    conv_gb(u)
                    for (so_, n_) in unit_info(u)[1]:
                        conv_pre(u, so_, n_)
                        for g_ in range(8):
                            conv_a(u, so_, n_, g_)
                            conv_b(u, so_, n_, g_)

                conv(0)
                for u in range(NLOC):
                    t0, segs, TU = unit_info(u)
                    SPQ.dma(hT1[:, :, :TU], hT_d.ap()[:, t0:t0 + TU].rearrange("(k p) t -> p k t", p=128),
                            reads=[B["hT"]], writes=Bh)
                    if u == 0:
                        SPQ.dma(yTs[:, 0:8, :TU], yT_d.ap()[:, t0:t0 + TU].rearrange("(k p) t -> p k t", p=128),
                                reads=[B["yT"]], writes=[By])

                    def load_wo(i):
                        SPQ.dma(wO[i % 2][:], wob[l].ap()[2 * i:2 * i + 2].rearrange("b p k j -> p b k j"),
                                reads=[Bw[("wob", l)]], writes=[BwO[i % 2]])

                    load_wo(0)
                    def load_gu(f):
                        SPQ.dma(wA[f % 2][:], wgb[l].ap()[f], reads=[Bw[("wgb", l)]], writes=[BwA[f % 2]])
                        SPQ.dma(wB_[f % 2][:], wub[l].ap()[f], reads=[Bw[("wub", l)]], writes=[BwB[f % 2]])

                    load_gu(0)
                    def load_d(m):
                        SPQ.dma(wD[m % 2][:], wdb[l].ap()[m], reads=[Bw[("wdb", l)]], writes=[BwD[m % 2]])

                    load_d(0)
                    for i in range(8):
                        if i + 1 < 8:
                            load_wo(i + 1)
                        for b2 in range(2):
                            m = 2 * i + b2
                            for (so, n) in segs:
                                bk = next_acc()
                                for kc in range(KC):
                                    rhs_ = yTs[:, kc, so:so + n] if kc < 8 else ycv[:, kc - 8, so:so + n]
                                    op(PE, lambda: nc.tensor.matmul(pbank[bk][:, :n], lhsT=wO[i % 2][:, b2, kc, :],
                                                                    rhs=rhs_,
                                                                    start=(kc == 0), stop=(kc == KC - 1)),
                                       reads=[BwO[i % 2], By if kc < 8 else Bycv], writes=[Bp[bk]],
                                       sig=(kc == KC - 1))
                                op(DVE, lambda: nc.vector.tensor_tensor(hT1[:, m, so:so + n], hT1[:, m, so:so + n],
                                                                        pbank[bk][:, :n], ALU.add),
                                   reads=[Bp[bk]], writes=[Bh[m]])

                    def rms_stats():
                        for kc in range(KC):
                            op(ACT, lambda: nc.scalar.activation(aT[:, kc, :TU], hT1[:, kc, :TU], AF.Square),
                               reads=[Bh[kc]], writes=[Ba[kc]])
                        for (so, n) in segs:
                            for kc in range(KC):
                                op(PE, lambda: nc.tensor.matmul(pbank[6][:, :n], lhsT=onesD[:], rhs=aT[:, kc, so:so + n],
                                                                start=(kc == 0), stop=(kc == KC - 1)),
                                   reads=[Ba[kc], Bc], writes=[Bp[6]], sig=(kc == KC - 1))
                            rsqrt_act(rsx[:, so:so + n], pbank[6][:, :n], EPS, [Bp[6]], [Brsx])

                    rms_stats()
                    for kc in range(KC):
                        op(DVE, lambda: nc.vector.scalar_tensor_tensor(yTs[:, kc, :TU], hT1[:, kc, :TU],
                                                                       vcol(32 + l * 16 + kc), rsx[:, :TU],
                                                                       ALU.mult, ALU.mult),
                           reads=[Bh[kc], Brsx, Bc], writes=[By])
                    for f in range(FC):
                        conv_slice(1)
                        if f + 1 < FC:
                            load_gu(f + 1)
                        for (so, n) in segs:
                            bg_, bu_ = next_acc(), next_acc()
                            for kc in range(KC):
                                op(PE, lambda: nc.tensor.matmul(pbank[bg_][:, :n], lhsT=wA[f % 2][:, kc, :],
                                                                rhs=yTs[:, kc, so:so + n], start=(kc == 0),
                                                                stop=(kc == KC - 1)),
                                   reads=[BwA[f % 2], By], writes=[Bp[bg_]], sig=(kc == KC - 1))
                            for kc in range(KC):
                                op(PE, lambda: nc.tensor.matmul(pbank[bu_][:, :n], lhsT=wB_[f % 2][:, kc, :],
                                                                rhs=yTs[:, kc, so:so + n], start=(kc == 0),
                                                                stop=(kc == KC - 1)),
                                   reads=[BwB[f % 2], By], writes=[Bp[bu_]], sig=(kc == KC - 1))
                            k_ = sgi[0] = (sgi[0] + 1) % 2
                            op(ACT, lambda: nc.scalar.activation(sg[k_][:, :n], pbank[bg_][:, :n], AF.Silu),
                               reads=[Bp[bg_]], writes=[Bsg[k_]])
                            op(DVE, lambda: nc.vector.tensor_tensor(aT[:, f, so:so + n], sg[k_][:, :n], pbank[bu_][:, :n],
                                                                    ALU.mult),
                               reads=[Bsg[k_], Bp[bu_]], writes=[Ba[f]])
                    ovl = u + 1 < NLOC
                    if ovl:
                        t0n, segsn, TUn = unit_info(u + 1)
                        SPQ.dma(yTs[:, 0:8, :TUn], yT_d.ap()[:, t0n:t0n + TUn].rearrange("(k p) t -> p k t", p=128),
                                reads=[B["yT"]], writes=[By])
                        conv_gb(u + 1)
                        conv_pre(u + 1, 0, CH)
                    for m in range(16):
                        if m + 1 < 16:
                            load_d(m + 1)
                        if ovl and m % 2 == 0:
                            conv_a(u + 1, 0, CH, m // 2)
                        if ovl and m % 2 == 1:
                            conv_b(u + 1, 0, CH, m // 2)
                        for (so, n) in segs:
                            bk = next_acc()
                            for f in range(FC):
                                op(PE, lambda: nc.tensor.matmul(pbank[bk][:, :n], lhsT=wD[m % 2][:, f, :],
                                                                rhs=aT[:, f, so:so + n], start=(f == 0), stop=(f == FC - 1)),
                                   reads=[BwD[m % 2], Ba[f]], writes=[Bp[bk]], sig=(f == FC - 1))
                            op(DVE, lambda: nc.vector.tensor_tensor(hT1[:, m, so:so + n], hT1[:, m, so:so + n],
                                                                    pbank[bk][:, :n], ALU.add),
                               reads=[Bp[bk]], writes=[Bh[m]])
                    if not last:
                        SPQ.dma(hT_d.ap()[:, t0:t0 + TU].rearrange("(k p) t -> p k t", p=128), hT1[:, :, :TU],
                                reads=Bh, writes=[B["hT"]])
                    else:
                        rms_stats()
                        for kc in range(KC):
                            op(DVE, lambda: nc.vector.scalar_tensor_tensor(hT1[:, kc, :TU], hT1[:, kc, :TU],
                                                                           vcol(64 + kc), rsx[:, :TU], ALU.mult, ALU.mult),
                               reads=[Brsx, Bc], writes=[Bh[kc]])
                        so = TU - CH
                        for s4 in range(4):
                            for half in range(2):
                                oi = oti[0] = (oti[0] + 1) % 2
                                for k2 in range(2):
                                    k4 = 2 * half + k2
                                    bk = next_acc()
                                    for kk in range(4):
                                        kc = 4 * k4 + kk
                                        op(PE, lambda: nc.tensor.transpose(pbank[bk][:, kk * 128:(kk + 1) * 128],
                                                                           hT1[:, kc, so + s4 * 128:so + (s4 + 1) * 128],
                                                                           identf[:]),
                                           reads=[Bh[kc], Bc], writes=[Bp[bk]], sig=(kk == 3))
                                    evac(otile[oi][:, k2 * 512:(k2 + 1) * 512], pbank[bk][:, :], [Bp[bk]], [Bot[oi]])
                                r0 = u * CH + s4 * 128
                                SPQ.dma(out_d.ap()[r0:r0 + 128, half * 1024:(half + 1) * 1024], otile[oi][:],
                                        reads=[Bot[oi]], writes=[B["out"]])

        for l in range(NLAYERS):
            phase1(l)
            barrier()
            phase2(l)
            conv_slice(10 ** 6)
            if l == 0 and NLAYERS > 1:
                convert(1, defer=True)
            barrier()
            phase34(l, l == NLAYERS - 1)
            conv_slice(10 ** 6)
            barrier()
    return nc


_PROG_CACHE = {}


def _run(inputs, NLOC, NLAYERS):
    f32 = np.float32
    x = np.asarray(inputs["x"], f32)
    meta = np.ascontiguousarray(np.asarray(inputs["meta"], f32))
    norm_mix = np.asarray(inputs["norm_mix"], f32)
    b_f = np.asarray(inputs["b_f"], f32)
    conv_w = np.asarray(inputs["conv_w"], f32)
    out_gain = np.asarray(inputs["out_gain"], f32)
    norm_ffn = np.asarray(inputs["norm_ffn"], f32)
    final_norm = np.asarray(inputs["final_norm"], f32)
    wts = {k: np.ascontiguousarray(np.asarray(inputs[k], f32)) for k in ["w_in", "w_out", "w_gate", "w_up", "w_down"]}
    NG = 4 * NLOC
    NKT = 1 + 4 * NG
    vecs = np.zeros((128, NV), f32)
    for l in range(2):
        vecs[:, l * 16:(l + 1) * 16] = norm_mix[l].reshape(16, 128).T
        vecs[:, 32 + l * 16:32 + (l + 1) * 16] = norm_ffn[l].reshape(16, 128).T
        vecs[:, 80 + l * 16:80 + (l + 1) * 16] = out_gain[l].reshape(16, 128).T
        for k in range(3):
            vecs[:, 112 + l * 24 + k * 8:112 + l * 24 + (k + 1) * 8] = conv_w[l, k].reshape(8, 128).T
    vecs[:, 64:80] = final_norm.reshape(16, 128).T
    bf = np.ascontiguousarray(b_f.T)
    identf = np.eye(128, dtype=f32)
    identb = np.eye(128).astype(ml_dtypes.bfloat16)
    tri = np.triu(np.ones((128, 128))).astype(ml_dtypes.bfloat16)
    in_maps = []
    for c in range(8):
        b, r = divmod(c, 4)
        xl = np.concatenate([x[b, chunk_global(r, lam) * CH:(chunk_global(r, lam) + 1) * CH] for lam in range(NLOC)], 0)
        ohm = np.zeros((128, 4), f32)
        ohm[:, r] = 1.0
        penm = np.zeros((NLOC, NKT), f32)
        for lam in range(NLOC):
            j = chunk_global(r, lam)
            for kt in range(1, NKT):
                if (kt - 1) // 4 >= j:
                    penm[lam, kt] = -BIG
        penb = np.ascontiguousarray(np.broadcast_to(penm.reshape(1, -1), (128, NLOC * NKT)))
        m = {"x": np.ascontiguousarray(xl), "meta": meta, "vecs": vecs, "bf": bf, "identf": identf, "identb": identb,
             "tri": tri, "oh": ohm, "pen": penb}
        m.update(wts)
        in_maps.append(m)
    key = (NLOC, NLAYERS)
    if key not in _PROG_CACHE:
        _PROG_CACHE[key] = build(NLOC, NLAYERS)
    nc = _PROG_CACHE[key]
    res = run_bass_kernel_spmd(nc, in_maps, core_ids=list(range(8)))
    if DEBUG:
        _run.last = res.results
    out = np.zeros((2, NG * CH, D), f32)
    for c in range(8):
        b, r = divmod(c, 4)
        o = np.asarray(res.results[c]["out"])
        for lam in range(NLOC):
            j = chunk_global(r, lam)
            out[b, j * CH:(j + 1) * CH] = o[lam * CH:(lam + 1) * CH]
    return out


def kernel(**inputs):
    return _run(inputs, 8, 2)
```
